# Optimizing a Trainium2 kernel written in Bass

```python
import jax, jax.numpy as jnp
from jax import lax
import numpy as np

D_MODEL = 1024
BATCH = 32
SEQ = 256
DEPTH = 2
DEC_BATCH = 4
DEC_SEQ = 4096
PAST_LEN = 256

GRID_W = 64
N_HEADS = 8
N_KV = 2
HEAD_DIM = 64
GROUP = N_HEADS // N_KV
ATT_W = N_HEADS * HEAD_DIM
KV_W = N_KV * HEAD_DIM
WINDOW = 128
BLOCK = 128
ROPE_BASE = 10000.0
CONV_W = D_MODEL // 4
CONV_K = 31
POOL_W = D_MODEL // 4
POOL_SIZES = (2, 4, 8, 16)
N_POOL_G = 4
POOL_G = POOL_W // N_POOL_G
SC_W = D_MODEL // 4
SC_K = 3
N_BRANCH = 4
D_FF = 4 * D_MODEL
EPS = 1e-6
NEG = -1e30
SPLIT_SIZES = (CONV_W, CONV_W, POOL_W, SC_W, SC_W, SC_W, ATT_W, KV_W, KV_W, N_BRANCH * D_MODEL)
IN_W = 2 * CONV_W + POOL_W + 3 * SC_W + ATT_W + 2 * KV_W + N_BRANCH * D_MODEL

kernel_name = "hybrid_diffusion_parallel_mixer_step"


def _rmsnorm(x, g):
    xf = x.astype(jnp.float32)
    y = xf * lax.rsqrt(jnp.mean(xf * xf, axis=-1, keepdims=True) + EPS)
    return y.astype(x.dtype) * g


def _layernorm(x, g, b):
    xf = x.astype(jnp.float32)
    mu = jnp.mean(xf, axis=-1, keepdims=True)
    var = jnp.mean(jnp.square(xf - mu), axis=-1, keepdims=True)
    y = (xf - mu) * lax.rsqrt(var + EPS)
    return y.astype(x.dtype) * g + b


def _dwconv(x, w):
    pad = w.shape[0] // 2
    return lax.conv_general_dilated(x, w[:, None, :], window_strides=(1,), padding=[(pad, pad)],
                                    dimension_numbers=('NWC', 'WIO', 'NWC'),
                                    feature_group_count=x.shape[-1])


def _multiscale_pool(u):
    bsz, n, _ = u.shape
    ug = u.astype(jnp.float32).reshape(bsz, n, N_POOL_G, POOL_G)
    cs = jnp.concatenate([jnp.zeros((bsz, 1, N_POOL_G, POOL_G), jnp.float32), lax.cumsum(ug, axis=1)], axis=1)
    t = jnp.arange(n)
    outs = []
    for gi, w in enumerate(POOL_SIZES):
        lo = jnp.clip(t - w // 2, 0, n)
        hi = jnp.clip(t + w // 2, 0, n)
        cnt = (hi - lo).astype(jnp.float32)[None, :, None]
        outs.append((cs[:, hi, gi] - cs[:, lo, gi]) / cnt - ug[:, :, gi])
    return jnp.stack(outs, axis=2).astype(u.dtype)


def _rope_1d(x, pos):
    half = x.shape[-1] // 2
    freq = ROPE_BASE ** (-jnp.arange(half, dtype=jnp.float32) / half)
    ang = pos.astype(jnp.float32)[:, None] * freq[None, :]
    cos = jnp.cos(ang)[None, :, None, :]
    sin = jnp.sin(ang)[None, :, None, :]
    xf = x.astype(jnp.float32)
    x1, x2 = xf[..., :half], xf[..., half:]
    return jnp.concatenate([x1 * cos - x2 * sin, x2 * cos + x1 * sin], axis=-1).astype(x.dtype)


def _rope_2d(x):
    n = x.shape[1]
    rows = n // GRID_W
    row = jnp.repeat(jnp.arange(rows), GRID_W)
    col = jnp.tile(jnp.arange(GRID_W), rows)
    r = x.shape[-1] // 2
    return jnp.concatenate([_rope_1d(x[..., :r], row), _rope_1d(x[..., r:], col)], axis=-1)


def _context_attention(q, k, v, sink):
    bsz, n = q.shape[:2]
    nbq = n // BLOCK
    scale = HEAD_DIM ** -0.5
    qb = q.reshape(bsz, nbq, BLOCK, N_KV, GROUP, HEAD_DIM).transpose(1, 0, 2, 3, 4, 5)
    sink_f = sink.astype(jnp.float32).reshape(1, N_KV, GROUP, 1, 1)

    def one_block(qblk):
        s = jnp.einsum('bqhgd,bchd->bhgqc', qblk, k).astype(jnp.float32) * scale
        sk = jnp.broadcast_to(sink_f, s.shape[:-1] + (1,))
        p = jax.nn.softmax(jnp.concatenate([s, sk], axis=-1), axis=-1)[..., :-1]
        return jnp.einsum('bhgqc,bchd->bqhgd', p.astype(v.dtype), v)

    o = lax.map(one_block, qb)
    return o.transpose(1, 0, 2, 3, 4, 5).reshape(bsz, n, ATT_W)


def _latent_attention(q, k, v, kc, vc, sink):
    bsz, n = q.shape[:2]
    nb = n // BLOCK
    scale = HEAD_DIM ** -0.5
    qb = q.reshape(bsz, nb, BLOCK, N_KV, GROUP, HEAD_DIM)
    pad = ((0, 0), (BLOCK, BLOCK), (0, 0), (0, 0))
    kp = jnp.pad(k, pad)
    vp = jnp.pad(v, pad)
    idx = jnp.arange(nb)[:, None] * BLOCK + jnp.arange(3 * BLOCK)[None, :]
    kl = kp[:, idx]
    vl = vp[:, idx]
    kpos = idx - BLOCK
    qpos = jnp.arange(nb)[:, None] * BLOCK + jnp.arange(BLOCK)[None, :]
    valid = ((jnp.abs(qpos[:, :, None] - kpos[:, None, :]) <= WINDOW)
             & (kpos[:, None, :] >= 0) & (kpos[:, None, :] < n))
    s_loc = jnp.einsum('bnqhgd,bnshd->bnhgqs', qb, kl).astype(jnp.float32) * scale
    s_loc = jnp.where(valid[None, :, None, None], s_loc, NEG)
    s_ctx = jnp.einsum('bnqhgd,bchd->bnhgqc', qb, kc).astype(jnp.float32) * scale
    sk = jnp.broadcast_to(sink.astype(jnp.float32).reshape(1, 1, N_KV, GROUP, 1, 1), s_loc.shape[:-1] + (1,))
    p = jax.nn.softmax(jnp.concatenate([s_loc, s_ctx, sk], axis=-1), axis=-1)
    p_loc = p[..., :3 * BLOCK].astype(v.dtype)
    p_ctx = p[..., 3 * BLOCK:-1].astype(vc.dtype)
    o = (jnp.einsum('bnhgqs,bnshd->bnqhgd', p_loc, vl)
         + jnp.einsum('bnhgqc,bchd->bnqhgd', p_ctx, vc))
    return o.reshape(bsz, n, ATT_W)


def _mixer(h, w_in, w_conv_a, b_conv_a, ln_g_a, ln_b_a, w_a_out, w_pool, pool_scale, w_b_out,
           w_sc, w_c_out, sink, w_d_out, w_o, ctx_k, ctx_v):
    bsz, n, _ = h.shape
    offs = np.cumsum(SPLIT_SIZES)[:-1].tolist()
    a_val, a_gate, u_pool, s_b, s_c, s_h, q, k, v, g_br = jnp.split(h @ w_in, offs, axis=-1)
    ua = _dwconv(a_val * jax.nn.sigmoid(a_gate), w_conv_a) + b_conv_a
    br_a = jax.nn.silu(_layernorm(ua, ln_g_a, ln_b_a)) @ w_a_out
    pooled = _multiscale_pool(u_pool)
    ub = jnp.einsum('bngc,gcd->bngd', pooled, w_pool).reshape(bsz, n, POOL_W) * pool_scale
    br_b = ub @ w_b_out
    br_c = (s_b * _dwconv(s_c * s_h, w_sc)) @ w_c_out
    q = q.reshape(bsz, n, N_HEADS, HEAD_DIM)
    k = k.reshape(bsz, n, N_KV, HEAD_DIM)
    v = v.reshape(bsz, n, N_KV, HEAD_DIM)
    if ctx_k is None:
        o = _context_attention(q, k, v, sink)
    else:
        o = _latent_attention(_rope_2d(q), _rope_2d(k), v, ctx_k, ctx_v, sink)
    br_d = o @ w_d_out
    g = jax.nn.sigmoid(g_br).reshape(bsz, n, N_BRANCH, D_MODEL)
    merged = g[:, :, 0] * br_a + g[:, :, 1] * br_b + g[:, :, 2] * br_c + g[:, :, 3] * br_d
    return merged @ w_o, k, v


def _layer(x, cond, w_ada, b_ada, g_pre_mix, g_post_mix, g_pre_ffn, g_post_ffn, w_in, w_conv_a, b_conv_a,
           ln_g_a, ln_b_a, w_a_out, w_pool, pool_scale, w_b_out, w_sc, w_c_out, sink, w_d_out, w_o,
           w_ff1, w_ff2, ctx_k, ctx_v):
    mod = jax.nn.silu(cond) @ w_ada + b_ada
    sh1, sc1, gt1, sh2, sc2, gt2 = jnp.split(mod[:, None, :], 6, axis=-1)
    h = _rmsnorm(x, g_pre_mix) * (1 + sc1) + sh1
    m, k, v = _mixer(h, w_in, w_conv_a, b_conv_a, ln_g_a, ln_b_a, w_a_out, w_pool, pool_scale, w_b_out,
                     w_sc, w_c_out, sink, w_d_out, w_o, ctx_k, ctx_v)
    x = x + gt1 * _rmsnorm(m, g_post_mix)
    h = _rmsnorm(x, g_pre_ffn) * (1 + sc2) + sh2
    f = jnp.square(jax.nn.relu(h @ w_ff1)) @ w_ff2
    x = x + gt2 * _rmsnorm(f, g_post_ffn)
    return x, k, v


def setup_inputs(seed: int = 0) -> dict:
    key = jax.random.key(seed)
    ks = jax.random.split(key, 32)
    f32 = jnp.float32

    def nrm(i, shape, scale=1.0):
        return jax.random.normal(ks[i], shape, f32) * scale

    L = DEPTH
    return {
        'x_prompt': nrm(0, (BATCH, SEQ, D_MODEL)),
        'x_sample': nrm(1, (DEC_BATCH, DEC_SEQ, D_MODEL)),
        'cache_k': nrm(2, (DEC_BATCH, DEPTH, PAST_LEN, N_KV, HEAD_DIM)),
        'cache_v': nrm(3, (DEC_BATCH, DEPTH, PAST_LEN, N_KV, HEAD_DIM)),
        'c': nrm(4, (DEC_BATCH, D_MODEL)),
        'c_ctx': nrm(5, (D_MODEL,)),
        'w_ada': nrm(6, (L, D_MODEL, 6 * D_MODEL), 0.5 * D_MODEL ** -0.5),
        'b_ada': nrm(7, (L, 6 * D_MODEL), 0.02),
        'g_pre_mix': 1.0 + nrm(8, (L, D_MODEL), 0.05),
        'g_post_mix': 1.0 + nrm(9, (L, D_MODEL), 0.05),
        'g_pre_ffn': 1.0 + nrm(10, (L, D_MODEL), 0.05),
        'g_post_ffn': 1.0 + nrm(11, (L, D_MODEL), 0.05),
        'w_in': nrm(12, (L, D_MODEL, IN_W), D_MODEL ** -0.5),
        'w_conv_a': nrm(13, (L, CONV_K, CONV_W), CONV_K ** -0.5),
        'b_conv_a': nrm(14, (L, CONV_W), 0.02),
        'ln_g_a': 1.0 + nrm(15, (L, CONV_W), 0.05),
        'ln_b_a': nrm(16, (L, CONV_W), 0.02),
        'w_a_out': nrm(17, (L, CONV_W, D_MODEL), CONV_W ** -0.5),
        'w_pool': nrm(18, (L, N_POOL_G, POOL_G, POOL_G), POOL_G ** -0.5),
        'pool_scale': 1.0 + nrm(19, (L, POOL_W), 0.1),
        'w_b_out': nrm(20, (L, POOL_W, D_MODEL), POOL_W ** -0.5),
        'w_sc': nrm(21, (L, SC_K, SC_W), SC_K ** -0.5),
        'w_c_out': nrm(22, (L, SC_W, D_MODEL), SC_W ** -0.5),
        'sink': nrm(23, (L, N_HEADS), 0.5),
        'w_d_out': nrm(24, (L, ATT_W, D_MODEL), ATT_W ** -0.5),
        'w_o': nrm(25, (L, D_MODEL, D_MODEL), D_MODEL ** -0.5),
        'w_ff1': nrm(26, (L, D_MODEL, D_FF), D_MODEL ** -0.5),
        'w_ff2': nrm(27, (L, D_FF, D_MODEL), D_FF ** -0.5),
    }


def reference(x_prompt, x_sample, cache_k, cache_v, c, c_ctx, w_ada, b_ada, g_pre_mix, g_post_mix,
              g_pre_ffn, g_post_ffn, w_in, w_conv_a, b_conv_a, ln_g_a, ln_b_a, w_a_out, w_pool,
              pool_scale, w_b_out, w_sc, w_c_out, sink, w_d_out, w_o, w_ff1, w_ff2):
    def lw(l):
        return (w_ada[l], b_ada[l], g_pre_mix[l], g_post_mix[l], g_pre_ffn[l], g_post_ffn[l], w_in[l],
                w_conv_a[l], b_conv_a[l], ln_g_a[l], ln_b_a[l], w_a_out[l], w_pool[l], pool_scale[l],
                w_b_out[l], w_sc[l], w_c_out[l], sink[l], w_d_out[l], w_o[l], w_ff1[l], w_ff2[l])

    xp = x_prompt
    cond_ctx = c_ctx[None, :]
    ks, vs = [], []
    for l in range(DEPTH):
        xp, k, v = _layer(xp, cond_ctx, *lw(l), None, None)
        ks.append(k)
        vs.append(v)
    new_k = jnp.stack(ks, axis=1)
    new_v = jnp.stack(vs, axis=1)

    xs = x_sample
    for l in range(DEPTH):
        xs, _, _ = _layer(xs, c, *lw(l), cache_k[:, l], cache_v[:, l])

    return (xp, xs, new_k, new_v)
```

```python
from contextlib import ExitStack
import numpy as np
import concourse.bass as bass
import concourse.mybir as mybir
from concourse.bass_utils import run_bass_kernel_spmd

F32 = mybir.dt.float32
BF16 = mybir.dt.bfloat16
AF = mybir.ActivationFunctionType
ALU = mybir.AluOpType

ENGS = ("pe", "act", "dve", "pool", "sp")
DEBUG_STAGE = 0
D = 1024
L = 2
NB_S = 20
WIN = NB_S * 128
PADC = 16
EPS = 1e-6
SLAB = 4096


class Sem:
    __slots__ = ("h", "count", "name")

    def __init__(self, h, name):
        self.h = h
        self.count = 0
        self.name = name


class Res:
    __slots__ = ("name", "w", "r", "excl")

    def __init__(self, name=""):
        self.name = name
        self.w = None
        self.r = {}
        self.excl = False


class Prog:
    def __init__(self, nc, stack):
        self.nc = nc
        self.stack = stack
        self.streams = {e: [] for e in ENGS}
        self.esem = {e: self.new_sem("eng_" + e) for e in ENGS}
        self.waited = {e: {} for e in ENGS}
        self.n_inst = 0
        self.n_wait = 0
        self._res = {}

    def new_sem(self, name):
        return Sem(self.stack.enter_context(self.nc.semaphore(name)), name)

    def R(self, *key):
        r = self._res.get(key)
        if r is None:
            r = Res(str(key))
            self._res[key] = r
        return r

    def _collect(self, eng, reads, writes):
        deps = {}

        def add(t, kind):
            if t is None:
                return
            s, v, e = t
            if e == eng and eng == "pe":
                return
            if deps.get(s, 0) < v:
                deps[s] = v

        for r in reads:
            add(r.w, "raw")
            if r.excl:
                for s, (v, e) in r.r.items():
                    if e != eng:
                        add((s, v, e), "rar")
        for w in writes:
            add(w.w, "waw")
            for s, (v, e) in w.r.items():
                add((s, v, e), "war")
        waits = []
        wd = self.waited[eng]
        for s, v in deps.items():
            if wd.get(s, 0) < v:
                wd[s] = v
                waits.append((s.h, v))
        return waits

    def _mark(self, ticket, reads, writes):
        s, v, e = ticket
        for r in reads:
            cur = r.r.get(s)
            if cur is None or cur[0] < v:
                r.r[s] = (v, e)
        for w in writes:
            w.w = ticket
            w.r = {}

    def op(self, eng, fn, reads=(), writes=(), signal=True):
        waits = self._collect(eng, reads, writes)
        sem = self.esem[eng]
        ticket = (sem, sem.count + 1, eng)
        if signal:
            sem.count += 1
        self.streams[eng].append((waits, fn, (sem.h, 1) if signal else None))
        self._mark(ticket, reads, writes)
        self.n_inst += 1
        self.n_wait += len(waits)
        return ticket

    def dma(self, eng, dsem, out_ap, in_ap, reads=(), writes=(), **kw):
        waits = self._collect(eng, reads, writes)
        dsem.count += 16
        ticket = (dsem, dsem.count, "dma")

        def fn(e, out_ap=out_ap, in_ap=in_ap):
            return e.dma_start(out=out_ap, in_=in_ap, **kw)

        self.streams[eng].append((waits, fn, (dsem.h, 16)))
        self._mark(ticket, reads, writes)
        self.n_inst += 1
        self.n_wait += len(waits)
        return ticket

    def wait_all(self, eng, sems):
        waits = [(s.h, s.count) for s in sems if s.count > 0]
        self.streams[eng].append((waits, None, None))

    def emit(self):
        streams = self.streams

        def run(e_obj, lst):
            for waits, fn, inc in lst:
                for (h, v) in waits:
                    e_obj.wait_ge(h, v)
                if fn is None:
                    continue
                ins = fn(e_obj)
                if inc is not None:
                    ins.then_inc(inc[0], inc[1])

        with self.nc.Block() as block:
            @block.tensor
            def _(t):
                run(t, streams["pe"])

            @block.scalar
            def _(s):
                run(s, streams["act"])

            @block.vector
            def _(v):
                run(v, streams["dve"])

            @block.gpsimd
            def _(g):
                run(g, streams["pool"])

            @block.sync
            def _(sp):
                run(sp, streams["sp"])


def pack_spec():
    spec = []
    for i in range(12):
        spec.append((f"ada{i}", 8, 512))
    spec += [("A0", 8, 512), ("A1", 8, 512), ("A2", 8, 256), ("AT", 8, 512)]
    spec += [("B0", 8, 512), ("B1", 8, 512), ("B2", 8, 256)]
    spec.append(("wpool", 1, 256))
    for c in range(8):
        spec.append((f"g{c}", 8, 512))
        spec.append((f"br{c}", 1, 2048))
    spec += [("wo0", 8, 512), ("wo1", 8, 512)]
    for i in range(8):
        spec.append((f"f1_{i}", 8, 512))
    for c in range(8):
        spec.append((f"f2_{c}", 32, 128))
    offs = {}
    o = 0
    for name, kc, n in spec:
        offs[name] = (o, kc, n)
        o += kc * n
    return spec, offs, o


PACK_SPEC, PACK_OFFS, PACK_EL = pack_spec()


def rope_swap_idx():
    idx = np.zeros(64, np.int64)
    for d in range(64):
        blk, r = d // 32, d % 32
        partner = r + 16 if r < 16 else r - 16
        idx[d] = blk * 32 + partner
    return idx


def host_pack_layer(inp, l):
    w_in = inp["w_in"][l]
    out = np.zeros((128, PACK_EL), np.float32)

    def put(name, arr):
        o, kc, n = PACK_OFFS[name]
        assert arr.shape == (128, kc, n), (name, arr.shape)
        out[:, o:o + kc * n] = arr.reshape(128, kc * n)

    def kmaj(w):
        K, n = w.shape
        return np.ascontiguousarray(w.reshape(K // 128, 128, n).transpose(1, 0, 2))

    wada = inp["w_ada"][l]
    for i in range(12):
        put(f"ada{i}", kmaj(wada[:, i * 512:(i + 1) * 512]))
    sw = rope_swap_idx()
    kcols = np.arange(2048, 2176)
    kscols = np.concatenate([2048 + g * 64 + sw for g in range(2)])
    colsA = np.concatenate([np.arange(256, 512), np.arange(0, 256), np.arange(1024, 1280),
                            np.arange(1280, 1536), kcols, kscols])
    WA = w_in[:, colsA]
    put("A0", kmaj(WA[:, 0:512])); put("A1", kmaj(WA[:, 512:1024])); put("A2", kmaj(WA[:, 1024:1280]))
    colsT = np.concatenate([np.arange(512, 768), np.arange(2176, 2304), np.arange(2048, 2176)])
    put("AT", kmaj(w_in[:, colsT]))
    qcols = np.concatenate([np.concatenate([1536 + i * 64 + np.arange(64), 1536 + (4 + i) * 64 + np.arange(64)])
                            for i in range(4)])
    qscols = np.concatenate([np.concatenate([1536 + i * 64 + sw, 1536 + (4 + i) * 64 + sw]) for i in range(4)])
    colsB = np.concatenate([np.arange(768, 1024), qcols, qscols])
    WB = w_in[:, colsB]
    put("B0", kmaj(WB[:, 0:512])); put("B1", kmaj(WB[:, 512:1024])); put("B2", kmaj(WB[:, 1024:1280]))
    wp = np.zeros((128, 1, 256), np.float32)
    for g in range(4):
        wp[0:64, 0, g * 64:(g + 1) * 64] = inp["w_pool"][l, g]
    put("wpool", wp)
    for c in range(8):
        gcols = np.concatenate([2304 + b * 1024 + c * 128 + np.arange(128) for b in range(4)])
        put(f"g{c}", kmaj(w_in[:, gcols]))
        br = np.zeros((128, 1, 2048), np.float32)
        cs = slice(c * 128, (c + 1) * 128)
        for k in range(2):
            br[:, 0, k * 128:(k + 1) * 128] = inp["w_a_out"][l, k * 128:(k + 1) * 128, cs]
            br[:, 0, 768 + k * 128:768 + (k + 1) * 128] = inp["w_c_out"][l, k * 128:(k + 1) * 128, cs]
        for g in range(4):
            br[0:64, 0, 256 + g * 128:256 + (g + 1) * 128] = inp["w_b_out"][l, g * 64:(g + 1) * 64, cs]
        for h in range(8):
            br[0:64, 0, 1024 + h * 128:1024 + (h + 1) * 128] = inp["w_d_out"][l, h * 64:(h + 1) * 64, cs]
        put(f"br{c}", br)
    wo = inp["w_o"][l]
    put("wo0", kmaj(wo[:, 0:512])); put("wo1", kmaj(wo[:, 512:1024]))
    w1 = inp["w_ff1"][l]
    for i in range(8):
        put(f"f1_{i}", kmaj(w1[:, i * 512:(i + 1) * 512]))
    w2 = inp["w_ff2"][l]
    for c in range(8):
        put(f"f2_{c}", kmaj(w2[:, c * 128:(c + 1) * 128]))
    return out


def pool_mats():
    sizes = (2, 4, 8, 16)
    M = np.zeros((5, 4, 128, 128), np.float64)
    for g, w in enumerate(sizes):
        hw = w // 2
        for t in range(128):
            for kind in range(5):
                for tp_rel in range(t - hw, t + hw):
                    if kind in (0, 1, 2):
                        cnt = w
                    elif kind == 3:
                        cnt = (t + hw) - max(t - hw, 0)
                    else:
                        cnt = min(t + hw, 128) - (t - hw)
                    if kind == 0 and tp_rel < 0:
                        M[kind, g, tp_rel + 128, t] += 1.0 / cnt
                    elif kind == 1 and tp_rel >= 128:
                        M[kind, g, tp_rel - 128, t] += 1.0 / cnt
                    elif kind >= 2 and 0 <= tp_rel < 128:
                        M[kind, g, tp_rel, t] += 1.0 / cnt
                if kind >= 2:
                    M[kind, g, t, t] -= 1.0
    return M.astype(np.float32)


def build_program(do_sample=True, limit=None):
    nc = bass.Bass("TRN2", target_bir_lowering=False)
    dt_in = lambda name, shape: nc.dram_tensor(name, list(shape), F32, kind="ExternalInput").ap()
    xp_d = dt_in("xp", [1024, D])
    xs_d = dt_in("xs", [WIN, D])
    wpk_d = [dt_in(f"wpk{l}", [128, PACK_EL]) for l in range(L)]
    cnst_d = dt_in("cnst", [128, 3 * 128 + 3 * 64 + 128 * 2])
    pmat_d = dt_in("pmat", [128, 7 * 4 * 128])
    small_d = dt_in("small", [128, 16 + L * 128])
    rope_d = dt_in("rope", [128, 2, WIN])
    kc_d = dt_in("kcT", [128, L, 256])
    vc_d = dt_in("vc", [128, L, 2, 128])
    yp_d = nc.dram_tensor("yp", [1024, D], F32, kind="ExternalOutput").ap()
    ys_d = nc.dram_tensor("ys", [2048, D], F32, kind="ExternalOutput").ap()
    nk_d = nc.dram_tensor("nk", [L, 1024, 128], F32, kind="ExternalOutput").ap()
    nv_d = nc.dram_tensor("nv", [L, 1024, 128], F32, kind="ExternalOutput").ap()
    wbf_d = [nc.dram_tensor(f"wbf{l}", [128, PACK_EL], BF16).ap() for l in range(L)]
    xsc_d = {"p": nc.dram_tensor("xscp", [128, 8, 1024], F32).ap(),
             "s": nc.dram_tensor("xscs", [128, 8, WIN], F32).ap()}

    with ExitStack() as st:
        P = Prog(nc, st)
        R = P.R

        def sb(name, shape, dt):
            return st.enter_context(nc.sbuf_tensor(name, list(shape), dt))

        ring_n = 5
        ring = sb("ring", [128, ring_n, SLAB], BF16)
        xT = sb("xT", [128, 8, 512], F32)
        hT = sb("hT", [128, 8, 512], BF16)
        S16 = sb("S16", [128, 34, 512], BF16)
        S32 = sb("S32", [128, 14, 512], F32)

        def xtok(bi, lo=0, hi=D):
            return S32[:, 2 * bi:2 * bi + 2, :].rearrange("p a b -> p (a b)")[:, lo:hi]

        def xtok_res(bi):
            return [R("S32", 2 * bi), R("S32", 2 * bi + 1)]
        ua_c = sb("ua_c", [128, 2, WIN + 2 * PADC + 64], BF16)
        sc_c = sb("sc_c", [128, 2, WIN + 2 * PADC + 64], BF16)
        kT = sb("kT", [128, WIN], BF16)
        vtok = sb("vtok", [128, NB_S, 128], BF16)
        uptok = sb("uptok", [128, NB_S, 256], BF16)
        qT = sb("qT", [128, 4, 512], BF16)
        kvo = sb("kvo", [128, 2, 256], F32)
        ropeT = sb("ropeT", [128, 2, 512], F32)
        kcT = sb("kcTs", [128, L, 256], BF16)
        vcs = sb("vcs", [128, L, 2, 128], BF16)
        cn32 = sb("cn32", [128, 128], F32)
        cbf = sb("cbf", [128, 3 * 128 + 3 * 64 + 256], BF16)
        pmat = sb("pmat_sb", [128, 28, 128], BF16)
        small = sb("small_sb", [128, 16 + L * 128], F32)
        condb = sb("condb", [128, 8, 2], BF16)
        modv = sb("modv", [128, L, 48, 2], F32)
        dvec = sb("dvec", [128, L, 6, 8, 2], F32)
        diagA = sb("diagA", [128, 8, 128], BF16)
        diagC = sb("diagC", [128, 2, 3, 128], BF16)
        sinkf = sb("sinkf", [64, 8, 128], F32)
        zeros = sb("zeros", [128, 128], F32)
        psb = [st.enter_context(nc.psum_tensor(f"ps{i}", [128, 512], F32)) for i in range(8)]

        ident32 = cn32
        identb = cbf[:, 0:128]
        maskP = cbf[:, 128:256]
        maskN = cbf[:, 256:384]
        ones64 = [cbf[:, 384 + i * 64:384 + (i + 1) * 64] for i in range(3)]
        r1024 = cbf[:, 576:704]
        r256 = cbf[:, 704:832]

        NCH = 8
        d_cast = [P.new_sem(f"d_cast{l}_{i}") for l in range(L) for i in range(NCH)]
        d_ring = [P.new_sem(f"d_ring{i}") for i in range(ring_n)]
        d_x = P.new_sem("d_x")
        d_xb = [P.new_sem(f"d_xb{i}") for i in range(4)]
        d_cs = [P.new_sem(f"d_c{i}") for i in range(8)]
        d_rope = P.new_sem("d_rope")
        d_sc = P.new_sem("d_sc")
        d_outb = [P.new_sem(f"d_out{i}") for i in range(4)]
        d_kvs = [[P.new_sem(f"d_kv{i}{j}") for j in range(2)] for i in range(2)]

        ps_ctr = {}
        dg_ctr = [0]

        def psum(lo=0, hi=8):
            k = (lo, hi)
            n = ps_ctr.get(k, 0)
            ps_ctr[k] = n + 1
            i = lo + (n % (hi - lo))
            r = R("ps", i)
            r.excl = True
            return psb[i], r

        ncn = 3 * 128 + 3 * 64 + 256
        P.dma("pool", d_cs[0], cbf[:], cnst_d[:, 0:ncn], writes=[R("cbf")])
        P.dma("act", d_cs[1], cn32[:], cnst_d[:, 0:128], writes=[R("cn32")])
        P.dma("pool", d_cs[2], pmat[:], pmat_d.rearrange("p (a b) -> p a b", b=128), writes=[R("pmat")])
        P.dma("act", d_cs[3], small[:], small_d, writes=[R("small")])
        P.dma("pool", d_cs[4], kcT[:], kc_d, writes=[R("kc")])
        P.dma("pool", d_cs[5], vcs[:], vc_d, writes=[R("vc")])
        P.op("dve", lambda e: e.memset(zeros[:], 0.0), writes=[R("zeros")])
        P.op("pool", lambda e: e.memset(ua_c[:], 0.0), writes=[R("ua_c")])
        P.op("pool", lambda e: e.memset(sc_c[:], 0.0), writes=[R("sc_c")])
        ch = PACK_EL // NCH

        def issue_cast(l, i):
            hi_ = PACK_EL if i == NCH - 1 else (i + 1) * ch
            P.dma("pool", d_cast[l * NCH + i], wbf_d[l][:, i * ch:hi_], wpk_d[l][:, i * ch:hi_],
                  writes=[R("wbf", l, i)], max_dma_last_dim=16384)

        for i in range(NCH):
            issue_cast(0, i)
        l1_cast = list(range(NCH))

        def wbf_res(l, o, n):
            lo_i = min(NCH - 1, o // ch)
            hi_i = min(NCH - 1, (o + n - 1) // ch)
            return [R("wbf", l, i) for i in range(lo_i, hi_i + 1)]

        def sm(l, a, b):
            return small[:, 16 + l * 128 + a:16 + l * 128 + b]

        validL = small[:, 16 + 98:16 + 99]
        validR = small[:, 16 + 99:16 + 100]
        P.op("act", lambda e: e.activation(out=condb[:], in_=small[:, 0:16].rearrange("p (c t) -> p c t", t=2),
                                           func=AF.Silu), reads=[R("small")], writes=[R("condb")])

        seq = slab_sequence(do_sample, limit)
        wstate = {"next_load": 0, "pos": 0}

        def ensure_loaded(upto):
            while wstate["next_load"] <= upto and wstate["next_load"] < len(seq):
                s = wstate["next_load"]
                l, name = seq[s]
                o, kc, n = PACK_OFFS[name]
                slot = s % ring_n
                P.dma("sp", d_ring[slot], ring[:, slot, 0:kc * n], wbf_d[l][:, o:o + kc * n],
                      reads=wbf_res(l, o, kc * n), writes=[R("ring", slot)])
                wstate["next_load"] += 1

        def W(l, name):
            s = wstate["pos"]
            assert seq[s] == (l, name), (s, seq[s], l, name)
            wstate["pos"] += 1
            ensure_loaded(s + ring_n - 3)
            o, kc, n = PACK_OFFS[name]
            slot = s % ring_n
            return ring[:, slot, 0:kc * n].rearrange("p (k n) -> p k n", n=n), R("ring", slot)

        def mm_group(out_ap, out_res, items, extra_reads=(), signal_each=False):
            n = len(items)
            for i, (lhsT, rhs, rs) in enumerate(items):
                P.op("pe", lambda e, lhsT=lhsT, rhs=rhs, i=i: e.matmul(out_ap, lhsT=lhsT, rhs=rhs,
                                                                         start=(i == 0), stop=(i == n - 1)),
                     reads=list(rs) + list(extra_reads), writes=[out_res], signal=(signal_each or i == n - 1))

        def act(out, in_, func, reads, writes, **kw):
            P.op("act", lambda e: e.activation(out=out, in_=in_, func=func, **kw), reads=reads, writes=writes)

        def tt(eng, out, in0, in1, op, reads, writes):
            P.op(eng, lambda e: e.tensor_tensor(out=out, in0=in0, in1=in1, op=op), reads=reads, writes=writes)

        def stt(eng, out, in0, scalar, in1, op0, op1, reads, writes):
            P.op(eng, lambda e: e.scalar_tensor_tensor(out=out, in0=in0, scalar=scalar, in1=in1, op0=op0, op1=op1),
                 reads=reads, writes=writes)

        def layer_setup(l):
            ps, rps = psum()
            for i in range(12):
                w, rw = W(l, f"ada{i}")
                for mi in range(4):
                    m = i * 4 + mi
                    mm_group(ps[:, 2 * m:2 * m + 2], rps,
                             [(w[:, k, mi * 128:(mi + 1) * 128], condb[:, k, :], [rw, R("condb")]) for k in range(8)])
            bada = sm(l, 0, 48)
            P.op("dve", lambda e: e.tensor_tensor(out=modv[:, l], in0=ps[:, 0:96].rearrange("p (m t) -> p m t", t=2),
                                                  in1=bada.unsqueeze(2).to_broadcast([128, 48, 2]), op=ALU.add),
                 reads=[rps, R("small")], writes=[R("modv", l)])
            def gvec(i):
                return sm(l, 48 + 8 * i, 56 + 8 * i).unsqueeze(2).to_broadcast([128, 8, 2])
            mv = modv[:, l]
            dv = dvec[:, l]
            rd, wr = [R("modv", l), R("small")], [R("dvec", l)]
            stt("dve", dv[:, 0], mv[:, 8:16], 1.0, gvec(0), ALU.add, ALU.mult, rd, wr)
            P.op("dve", lambda e: e.tensor_copy(out=dv[:, 1], in_=mv[:, 0:8]), reads=rd, writes=wr)
            tt("dve", dv[:, 2], mv[:, 16:24], gvec(1), ALU.mult, rd, wr)
            stt("dve", dv[:, 3], mv[:, 32:40], 1.0, gvec(2), ALU.add, ALU.mult, rd, wr)
            P.op("dve", lambda e: e.tensor_copy(out=dv[:, 4], in_=mv[:, 24:32]), reads=rd, writes=wr)
            tt("dve", dv[:, 5], mv[:, 40:48], gvec(3), ALU.mult, rd, wr)
            wca = small_conv[l]
            for c in range(2):
                for k in range(3):
                    P.op("pool", lambda e, c=c, k=k: e.tensor_scalar(out=diagC[:, c, k, :], in0=ident32[:],
                                                                     scalar1=wca[:, c, 31 + k:32 + k], scalar2=None,
                                                                     op0=ALU.mult),
                         reads=[R("cn32"), R("convw")], writes=[R("diagC")])
            for h in range(8):
                act(sinkf[:, h, :], zeros[0:64, :], AF.Exp, [R("zeros"), R("small")], [R("sinkf")],
                    bias=sm(l, 90 + h, 91 + h)[0:64, :], scale=1.0)

        convw_d = dt_in("convw", [128, L, 2, 34])
        convw = sb("convw_sb", [128, L, 2, 34], F32)
        P.dma("act", d_cs[6], convw[:], convw_d, writes=[R("convw")])
        small_conv = [convw[:, l] for l in range(L)]

        def norm_to_h(l, T, a_idx, b_idx, col, xres):
            ps, rps = psum()
            items = []
            SQE = ("act", "dve", "pool", "act", "dve", "act", "dve", "pool")
            for c in range(8):
                sq, rsq = S16[:, 8 + c, 0:T], R("S16", 8 + c)
                if DEBUG_STAGE != 212:
                    if SQE[c] == "act":
                        act(sq, xT[:, c, 0:T], AF.Square, [xres[c]], [rsq])
                    else:
                        tt(SQE[c], sq, xT[:, c, 0:T], xT[:, c, 0:T], ALU.mult, [xres[c]], [rsq])
                if DEBUG_STAGE != 211:
                    P.op("pe", lambda e, sq=sq, c=c: e.matmul(ps[:, 0:T], lhsT=r1024, rhs=sq, start=(c == 0), stop=(c == 7)),
                         reads=[rsq, R("cbf")], writes=[rps], signal=True)
            if DEBUG_STAGE in (21, 211, 212):
                return
            rsq_, rr0 = S32[:, 12, 0:T], R("S32", 12)
            rstd, rr = S32[:, 13, 0:T], R("S32", 13)
            act(rsq_, ps[:, 0:T], AF.Sqrt, [rps], [rr0], bias=EPS, scale=1.0)
            if DEBUG_STAGE == 22:
                return
            P.op("dve", lambda e: e.reciprocal(out=rstd, in_=rsq_), reads=[rr0], writes=[rr])
            if DEBUG_STAGE == 23:
                return
            for c in range(8):
                tmp, rt = S32[:, 10 + (c % 2), 0:T], R("S32", 10 + (c % 2))
                eng = "dve"
                tt(eng, tmp, xT[:, c, 0:T], rstd, ALU.mult, [xres[c], rr], [rt])
                act(hT[:, c, 0:T], tmp, AF.Identity, [rt, R("dvec", l)], [R("hT", c)],
                    scale=dvec[:, l, a_idx, c, col:col + 1], bias=dvec[:, l, b_idx, c, col:col + 1])

        def post_norm_residual(l, T, g_idx, col, mres, xres):
            ps, rps = psum()
            SQE = ("act", "dve", "pool", "act", "dve", "act", "dve", "pool")
            for c in range(8):
                sq, rsq = S16[:, 8 + c, 0:T], R("S16", 8 + c)
                if SQE[c] == "act":
                    act(sq, S32[:, c, 0:T], AF.Square, [mres[c]], [rsq])
                else:
                    tt(SQE[c], sq, S32[:, c, 0:T], S32[:, c, 0:T], ALU.mult, [mres[c]], [rsq])
                P.op("pe", lambda e, sq=sq, c=c: e.matmul(ps[:, 0:T], lhsT=r1024, rhs=sq, start=(c == 0), stop=(c == 7)),
                     reads=[rsq, R("cbf")], writes=[rps], signal=True)
            rsq_, rr0 = S32[:, 12, 0:T], R("S32", 12)
            rstd, rr = S32[:, 13, 0:T], R("S32", 13)
            act(rsq_, ps[:, 0:T], AF.Sqrt, [rps], [rr0], bias=EPS, scale=1.0)
            P.op("dve", lambda e: e.reciprocal(out=rstd, in_=rsq_), reads=[rr0], writes=[rr])
            for c in range(8):
                tt("dve", S32[:, c, 0:T], S32[:, c, 0:T], rstd, ALU.mult, [mres[c], rr], [mres[c]])
                stt("dve", xT[:, c, 0:T], S32[:, c, 0:T], dvec[:, l, g_idx, c, col:col + 1],
                    xT[:, c, 0:T], ALU.mult, ALU.add, [mres[c], R("dvec", l), xres[c]], [xres[c]])

        def load_x(pas, l, blocks, T):
            xres = [R("xT", c) for c in range(8)]
            nb = len(blocks)
            if l == 0:
                src = xp_d if pas == "p" else xs_d
                for bi, b in enumerate(blocks):
                    P.dma("act", d_xb[bi], xtok(bi), src[b * 128:(b + 1) * 128, :], writes=xtok_res(bi))
                for c in range(8 if DEBUG_STAGE not in (11, 13) else 0):
                    ps, rps = psum()
                    for bi in range(nb):
                        P.op("pe", lambda e, bi=bi, c=c, ps=ps: e.matmul(ps[:, bi * 128:(bi + 1) * 128],
                                                                          lhsT=xtok(bi, c * 128, (c + 1) * 128), rhs=ident32[:],
                                                                          start=True, stop=True),
                             reads=xtok_res(bi) + [R("cn32")], writes=[rps], signal=(bi == nb - 1))
                    if c % 2 == 0:
                        act(xT[:, c, 0:T], ps[:, 0:T], AF.Copy, [rps], [xres[c]])
                    else:
                        P.op("dve", lambda e, c=c, ps=ps: e.tensor_copy(out=xT[:, c, 0:T], in_=ps[:, 0:T]),
                             reads=[rps], writes=[xres[c]])
            else:
                b0 = blocks[0]
                P.dma("act", d_x, xT[:, :, 0:T], xsc_d[pas][:, :, b0 * 128:b0 * 128 + T],
                      reads=[R("xsc", pas)], writes=xres)
            return xres

        def store_x(pas, l, blocks, T, xres):
            b0 = blocks[0]
            if l == 0:
                P.dma("act", d_sc, xsc_d[pas][:, :, b0 * 128:b0 * 128 + T], xT[:, :, 0:T],
                      reads=xres, writes=[R("xsc", pas)])
            else:
                dst = yp_d if pas == "p" else ys_d
                for bi, b in enumerate(blocks):
                    ob = b if pas == "p" else b - 2
                    for half in range(2):
                        ps, rps = psum()
                        for cc in range(4):
                            c = half * 4 + cc
                            P.op("pe", lambda e, bi=bi, c=c, cc=cc, ps=ps: e.matmul(
                                ps[:, cc * 128:(cc + 1) * 128], lhsT=xT[:, c, bi * 128:(bi + 1) * 128], rhs=ident32[:],
                                start=True, stop=True),
                                 reads=[xres[c], R("cn32")], writes=[rps], signal=(cc == 3))
                        if half == 0:
                            act(xtok(bi, 0, 512), ps[:], AF.Copy, [rps], [R("S32", 2 * bi)])
                        else:
                            P.op("dve", lambda e, bi=bi, ps=ps: e.tensor_copy(out=xtok(bi, 512, 1024), in_=ps[:]),
                                 reads=[rps], writes=[R("S32", 2 * bi + 1)])
                    P.dma("act", d_outb[bi], dst[ob * 128:(ob + 1) * 128, :], xtok(bi), reads=xtok_res(bi))

        def ccol(pas, t):
            if pas == "p":
                s, r = t // 256, t % 256
                return s * (256 + 2 * PADC) + PADC + r
            return PADC + t

        def conv_rhs(buf, c, pas, blocks, T, shift):
            t0 = blocks[0] * 128
            if pas == "p" and T == 512:
                base = ccol(pas, t0) + shift
                return buf[:, c, base:base + 2 * (256 + 2 * PADC)].rearrange("p (s w) -> p s w", s=2)[:, :, 0:256]
            base = ccol(pas, t0) + shift
            return buf[:, c, base:base + T]

        def conv_dst(buf, c, pas, blocks, T):
            return conv_rhs(buf, c, pas, blocks, T, 0)

        def as_tile(ap, pas, T):
            if pas == "p" and T == 512:
                return ap.rearrange("p (s w) -> p s w", s=2)
            return ap

        def phase_A(pas, l, blocks):
            T = len(blocks) * 128
            col = 0 if pas == "p" else 1
            if DEBUG_STAGE == 14:
                return
            if DEBUG_STAGE == 12:
                P.op("pool", lambda e: e.memset(ua_c[:, :, 0:4 * (256 + 2 * PADC)], 0.0), writes=[R("ua_c")])
                return
            if pas == "p" and blocks[0] == 0 and DEBUG_STAGE != 13:
                P.op("pool", lambda e: e.memset(ua_c[:, :, 0:4 * (256 + 2 * PADC)], 0.0), writes=[R("ua_c")])
                P.op("pool", lambda e: e.memset(sc_c[:, :, 0:4 * (256 + 2 * PADC)], 0.0), writes=[R("sc_c")])
            xres = load_x(pas, l, blocks, T)
            if DEBUG_STAGE in (1, 11, 13):
                return
            norm_to_h(l, T, 0, 1, col, xres)
            if DEBUG_STAGE in (2, 21, 22, 23, 24, 211, 212):
                return
            hres = [R("hT", c) for c in range(8)]
            t0 = blocks[0] * 128
            if pas == "s":
                P.dma("act", d_rope, ropeT[:, :, 0:T], rope_d[:, :, t0:t0 + T], writes=[R("ropeT")])
            slabs = [W(l, "A0"), W(l, "A1"), W(l, "A2")]

            def proj(ci):
                w, rw = slabs[ci // 4]
                ps, rps = psum()
                mm_group(ps[:, 0:T], rps, [(w[:, k, (ci % 4) * 128:(ci % 4 + 1) * 128], hT[:, k, 0:T], [rw, hres[k]])
                                           for k in range(8)])
                return ps, rps

            for c in range(2):
                psg, rg = proj(c)
                sig, rs_ = S16[:, 10 + c, 0:T], R("S16", 10 + c)
                act(sig, psg[:, 0:T], AF.Sigmoid, [rg], [rs_])
                psv, rv = proj(2 + c)
                tt("dve", conv_dst(ua_c, c, pas, blocks, T), as_tile(psv[:, 0:T], pas, T), as_tile(sig, pas, T),
                   ALU.mult, [rv, rs_], [R("ua_c")])
            for c in range(2):
                psc, rc = proj(4 + c)
                tmp, rt = S32[:, 8 + c, 0:T], R("S32", 8 + c)
                act(tmp, psc[:, 0:T], AF.Copy, [rc], [rt])
                psh, rh = proj(6 + c)
                tt("dve", conv_dst(sc_c, c, pas, blocks, T), as_tile(psh[:, 0:T], pas, T), as_tile(tmp, pas, T),
                   ALU.mult, [rh, rt], [R("sc_c")])
            psk, rk = proj(8)
            if pas == "p":
                act(kT[:, t0:t0 + T], psk[:, 0:T], AF.Copy, [rk], [R("kT")])
            else:
                pss, rss = proj(9)
                t1, r1 = S32[:, 8, 0:T], R("S32", 8)
                t2, r2 = S32[:, 9, 0:T], R("S32", 9)
                tt("dve", t1, psk[:, 0:T], ropeT[:, 0, 0:T], ALU.mult, [rk, R("ropeT")], [r1])
                tt("dve", t2, pss[:, 0:T], ropeT[:, 1, 0:T], ALU.mult, [rss, R("ropeT")], [r2])
                tt("pool", kT[:, t0:t0 + T], t1, t2, ALU.add, [r1, r2], [R("kT")])
            if DEBUG_STAGE == 3:
                wstate["pos"] += 1
                return
            wt, rwt = W(l, "AT")
            for bi, b in enumerate(blocks):
                ps, rps = psum()
                ncol = 512 if pas == "p" else 384
                mm_group(ps[:, 0:ncol], rps, [(hT[:, k, bi * 128:(bi + 1) * 128], wt[:, k, 0:ncol], [rwt, hres[k]])
                                              for k in range(8)])
                vs = None
                if pas == "s" and b in (0, 1):
                    vs = validL
                if pas == "s" and b in (18, 19):
                    vs = validR
                if vs is None:
                    act(uptok[:, b, :], ps[:, 0:256], AF.Copy, [rps], [R("uptok")])
                    P.op("dve", lambda e, b=b, ps=ps: e.tensor_copy(out=vtok[:, b, :], in_=ps[:, 256:384]),
                         reads=[rps], writes=[R("vtok")])
                else:
                    act(uptok[:, b, :], ps[:, 0:256], AF.Identity, [rps, R("small")], [R("uptok")], scale=vs)
                    act(vtok[:, b, :], ps[:, 256:384], AF.Identity, [rps, R("small")], [R("vtok")], scale=vs)
                    for buf, nm in ((ua_c, "ua_c"), (sc_c, "sc_c")):
                        cs = ccol(pas, b * 128)
                        P.op("pool", lambda e, buf=buf, cs=cs, vs=vs: e.tensor_scalar(
                            out=buf[:, :, cs:cs + 128], in0=buf[:, :, cs:cs + 128], scalar1=vs, scalar2=None,
                            op0=ALU.mult), reads=[R(nm), R("small")], writes=[R(nm)])
                if pas == "p":
                    kb = bi % 2
                    P.op("dve", lambda e, kb=kb, ps=ps: e.tensor_copy(out=kvo[:, kb, :], in_=ps[:, 256:512]),
                         reads=[rps], writes=[R("kvo", kb)])
                    P.dma("act", d_kvs[kb][0], nv_d[l, b * 128:(b + 1) * 128, :], kvo[:, kb, 0:128], reads=[R("kvo", kb)])
                    P.dma("act", d_kvs[kb][1], nk_d[l, b * 128:(b + 1) * 128, :], kvo[:, kb, 128:256], reads=[R("kvo", kb)])

        def phase_B(pas, l, blocks):
            T = len(blocks) * 128
            nb = len(blocks)
            col = 0 if pas == "p" else 1
            t0 = blocks[0] * 128
            xres = load_x(pas, l, blocks, T)
            norm_to_h(l, T, 0, 1, col, xres)
            hres = [R("hT", c) for c in range(8)]
            if pas == "s":
                P.dma("act", d_rope, ropeT[:, :, 0:T], rope_d[:, :, t0:t0 + T], writes=[R("ropeT")])
            slabs = [W(l, "B0"), W(l, "B1"), W(l, "B2")]

            def proj(ci):
                w, rw = slabs[ci // 4]
                ps, rps = psum()
                mm_group(ps[:, 0:T], rps, [(w[:, k, (ci % 4) * 128:(ci % 4 + 1) * 128], hT[:, k, 0:T], [rw, hres[k]])
                                           for k in range(8)])
                return ps, rps

            for c in range(2):
                ps, rps = proj(c)
                act(S32[:, 8 + c, 0:T], ps[:, 0:T], AF.Copy, [rps], [R("S32", 8 + c)])
            for c in range(4):
                psq, rq = proj(2 + c)
                if pas == "p":
                    act(qT[:, c, 0:T], psq[:, 0:T], AF.Copy, [rq], [R("qT", c)])
                else:
                    pss, rss = proj(6 + c)
                    t1, r1 = S32[:, 10, 0:T], R("S32", 10)
                    t2, r2 = S32[:, 11, 0:T], R("S32", 11)
                    tt("dve", t1, psq[:, 0:T], ropeT[:, 0, 0:T], ALU.mult, [rq, R("ropeT")], [r1])
                    tt("dve", t2, pss[:, 0:T], ropeT[:, 1, 0:T], ALU.mult, [rss, R("ropeT")], [r2])
                    tt("pool", qT[:, c, 0:T], t1, t2, ALU.add, [r1, r2], [R("qT", c)])
            uar = []
            for c in range(2):
                ps, rps = psum()
                for k in range(31):
                    di = dg_ctr[0] % 8
                    dg_ctr[0] += 1
                    P.op("dve", lambda e, c=c, k=k, di=di: e.tensor_scalar(
                        out=diagA[:, di, :], in0=ident32[:], scalar1=convw[:, l, c, k:k + 1], scalar2=None,
                        op0=ALU.mult), reads=[R("cn32"), R("convw")], writes=[R("diagA", di)])
                    P.op("pe", lambda e, c=c, k=k, di=di, ps=ps: e.matmul(
                        as_tile(ps[:, 0:T], pas, T), lhsT=diagA[:, di, :],
                        rhs=conv_rhs(ua_c, c, pas, blocks, T, k - 15), start=(k == 0), stop=(k == 30)),
                         reads=[R("diagA", di), R("ua_c")], writes=[rps], signal=True)
                u, ru = S32[:, c, 0:T], R("S32", c)
                act(u, ps[:, 0:T], AF.Identity, [rps, R("small")], [ru], bias=sm(l, 80 + c, 81 + c), scale=1.0)
                ub_, rub = S16[:, 18 + c, 0:T], R("S16", 18 + c)
                P.op("pool", lambda e, u=u, ub_=ub_: e.tensor_copy(out=ub_, in_=u), reads=[ru], writes=[rub])
                uar.append((u, ru, ub_, rub))
            psm, rpm = psum()
            mm_group(psm[:, 0:T], rpm, [(r256, uar[c][2], [uar[c][3], R("cbf")]) for c in range(2)])
            for c in range(2):
                u, ru, ub_, rub = uar[c]
                tt("dve", u, u, psm[:, 0:T], ALU.subtract, [ru, rpm], [ru])
                act(ub_, u, AF.Square, [ru], [rub])
            psv, rpv = psum()
            mm_group(psv[:, 0:T], rpv, [(r256, uar[c][2], [uar[c][3], R("cbf")]) for c in range(2)])
            rs_a0, rra0 = S32[:, 2, 0:T], R("S32", 2)
            rs_a, rra = S32[:, 13, 0:T], R("S32", 13)
            act(rs_a0, psv[:, 0:T], AF.Sqrt, [rpv], [rra0], bias=EPS, scale=1.0)
            P.op("dve", lambda e: e.reciprocal(out=rs_a, in_=rs_a0), reads=[rra0], writes=[rra])
            for c in range(2):
                u, ru, ub_, rub = uar[c]
                tt("dve" if c else "pool", u, u, rs_a, ALU.mult, [ru, rra], [ru])
                act(S16[:, 16 + c, 0:T], u, AF.Silu, [ru, R("small")], [R("S16", 16 + c)],
                    scale=sm(l, 82 + c, 83 + c), bias=sm(l, 84 + c, 85 + c))
            wp, rwp = W(l, "wpool")
            for g in range(4):
                ps, rps = psum()
                for bi, b in enumerate(blocks):
                    items = []
                    if pas == "p":
                        first = (b % 2 == 0)
                        kinds = [(b, 3 if first else 4)]
                        kinds.append((b + 1, 1) if first else (b - 1, 0))
                    else:
                        cur = 5 if b == 2 else (6 if b == 17 else 2)
                        kinds = [(b - 1, 0), (b, cur), (b + 1, 1)]
                    for (j, kind) in kinds:
                        items.append((uptok[:, j, g * 64:(g + 1) * 64], pmat[:, kind * 4 + g, :], [R("uptok"), R("pmat")]))
                    mm_group(ps[0:64, bi * 128:(bi + 1) * 128], rps, items)
                pl, rpl = S16[0:64, 8 + (g % 2), 0:T], R("S16", 8 + (g % 2))
                if g % 2:
                    P.op("dve", lambda e, pl=pl, ps=ps: e.tensor_copy(out=pl, in_=ps[0:64, 0:T]), reads=[rps], writes=[rpl])
                else:
                    act(pl, ps[0:64, 0:T], AF.Copy, [rps], [rpl])
                ps2, rps2 = psum()
                mm_group(ps2[0:64, 0:T], rps2, [(wp[0:64, 0, g * 64:(g + 1) * 64], pl, [rwp, rpl])])
                act(S16[0:64, 20 + g, 0:T], ps2[0:64, 0:T], AF.Identity, [rps2, R("small")], [R("S16", 20 + g)],
                    scale=sm(l, 86 + g, 87 + g)[0:64, :])
            for c in range(2):
                ps, rps = psum()
                mm_group(as_tile(ps[:, 0:T], pas, T), rps,
                         [(diagC[:, c, k, :], conv_rhs(sc_c, c, pas, blocks, T, k - 1), [R("diagC"), R("sc_c")])
                          for k in range(3)])
                tt("dve", S16[:, 18 + c, 0:T], ps[:, 0:T], S32[:, 8 + c, 0:T], ALU.mult,
                   [rps, R("S32", 8 + c)], [R("S16", 18 + c)])
            jobs = []
            for bi, b in enumerate(blocks):
                if pas == "p":
                    s0 = (b // 2) * 2
                    keys = [("l", s0, None, 0), ("l", s0 + 1, None, 0)]
                else:
                    keys = []
                    for j, mk in ((b - 1, maskP), (b, None), (b + 1, maskN)):
                        oi = 1 if j in (0, 1) else (2 if j in (18, 19) else 0)
                        keys.append(("l", j, mk, oi))
                    keys += [("c", 0, None, 0), ("c", 1, None, 0)]
                for g in range(2):
                    for ki, ks in enumerate(keys):
                        jobs.append((bi, g, ki, len(keys), ks))

            def kv_of(job):
                bi, g, ki, nk, (kind, j, mk, oi) = job
                if kind == "l":
                    return (kT[g * 64:(g + 1) * 64, j * 128:(j + 1) * 128], R("kT"),
                            vtok[:, j, g * 64:(g + 1) * 64], R("vtok"))
                return (kcT[g * 64:(g + 1) * 64, l, j * 128:(j + 1) * 128], R("kc"),
                        vcs[:, l, j, g * 64:(g + 1) * 64], R("vc"))

            def issue_S(job):
                bi, g, ki, nk, ks = job
                klhs, kres, _, _ = kv_of(job)
                pS, rpS = psum(0, 4)
                qr = qT[g * 64:(g + 1) * 64, :, bi * 128:(bi + 1) * 128]
                mm_group(pS[:].rearrange("p (h q) -> p h q", h=4), rpS,
                         [(klhs, qr, [kres] + [R("qT", c) for c in range(4)])])
                return pS, rpS

            LA = 2
            pending = []
            inflight = {}
            for ji in range(min(LA, len(jobs))):
                inflight[ji] = issue_S(jobs[ji])
            acc = None
            for ji, job in enumerate(jobs):
                bi, g, ki, nk, (kind, j, mk, oi) = job
                if ki == 0:
                    acc = (psum(4, 6), psum(6, 8))
                (po, rpo), (pd, rpd) = acc
                pS, rpS = inflight.pop(ji)
                _, _, vlhs, vres = kv_of(job)
                es = 12 + (ji % 4)
                E, rE = S16[:, es, :], R("S16", es)
                act(E, pS[:], AF.Exp, [rpS], [rE], scale=0.125)
                if mk is not None:
                    E3 = E.rearrange("p (h q) -> p h q", h=4)
                    tt("pool", E3, E3, mk.unsqueeze(1).to_broadcast([128, 4, 128]), ALU.mult, [rE, R("cbf")], [rE])
                if ji + LA < len(jobs):
                    inflight[ji + LA] = issue_S(jobs[ji + LA])
                P.op("pe", lambda e, vlhs=vlhs, E=E, ki=ki, po=po, nk=nk: e.matmul(
                    po[0:64, :], lhsT=vlhs, rhs=E, start=(ki == 0), stop=(ki == nk - 1)),
                     reads=[vres, rE], writes=[rpo], signal=False)
                P.op("pe", lambda e, oi=oi, E=E, ki=ki, pd=pd, nk=nk: e.matmul(
                    pd[0:64, :], lhsT=ones64[oi], rhs=E, start=(ki == 0), stop=(ki == nk - 1)),
                     reads=[R("cbf"), rE], writes=[rpd, rpo], signal=True)
                while pending and pending[0][0] <= ji:
                    pending.pop(0)[1]()
                if ki == nk - 1:
                  def normalize(bi=bi, g=g, po=po, rpo=rpo, pd=pd, rpd=rpd):
                    den, rden = S32[0:64, 3 + g, :], R("S32", 3 + g)
                    rec, rrec = S32[0:64, 5 + g, :], R("S32", 5 + g)
                    tt("dve", den.rearrange("p (h q) -> p h q", h=4), pd[0:64, :].rearrange("p (h q) -> p h q", h=4),
                       sinkf[:, g * 4:(g + 1) * 4, :], ALU.add, [rpd, R("sinkf")], [rden])
                    act(den, den, AF.Ln, [rden], [rden])
                    act(rec, den, AF.Exp, [rden], [rrec], scale=-1.0)
                    for hh in range(4):
                        slot = 24 + g * 4 + hh
                        tt("dve", S16[0:64, slot, bi * 128:(bi + 1) * 128], po[0:64, hh * 128:(hh + 1) * 128],
                           rec[:, hh * 128:(hh + 1) * 128], ALU.mult, [rpo, rrec], [R("S16", slot)])
                  pending.append((ji + 2, normalize))
            while pending:
                pending.pop(0)[1]()
            for c in range(8):
                wg, rwg = W(l, f"g{c}")
                wb, rwb = W(l, f"br{c}")
                gsl = []
                for b_ in range(4):
                    ps, rps = psum()
                    mm_group(ps[:, 0:T], rps, [(wg[:, k, b_ * 128:(b_ + 1) * 128], hT[:, k, 0:T], [rwg, hres[k]])
                                               for k in range(8)])
                    gs = 8 + (c % 2) * 4 + b_
                    act(S16[:, gs, 0:T], ps[:, 0:T], AF.Sigmoid, [rps], [R("S16", gs)])
                    gsl.append(gs)
                brs = []
                ps, rps = psum()
                mm_group(ps[:, 0:T], rps, [(wb[:, 0, k * 128:(k + 1) * 128], S16[:, 16 + k, 0:T], [rwb, R("S16", 16 + k)])
                                           for k in range(2)])
                brs.append((ps, rps))
                ps, rps = psum()
                mm_group(ps[:, 0:T], rps, [(wb[0:64, 0, 256 + g * 128:256 + (g + 1) * 128], S16[0:64, 20 + g, 0:T],
                                            [rwb, R("S16", 20 + g)]) for g in range(4)])
                brs.append((ps, rps))
                ps, rps = psum()
                mm_group(ps[:, 0:T], rps, [(wb[:, 0, 768 + k * 128:768 + (k + 1) * 128], S16[:, 18 + k, 0:T],
                                            [rwb, R("S16", 18 + k)]) for k in range(2)])
                brs.append((ps, rps))
                ps, rps = psum()
                mm_group(ps[:, 0:T], rps, [(wb[0:64, 0, 1024 + h * 128:1024 + (h + 1) * 128], S16[0:64, 24 + h, 0:T],
                                            [rwb, R("S16", 24 + h)]) for h in range(8)])
                brs.append((ps, rps))
                acc, racc = S32[:, 5 + (c % 2), 0:T], R("S32", 5 + (c % 2))
                tmp, rtmp = S32[:, 7, 0:T], R("S32", 7)
                tt("dve", acc, brs[0][0][:, 0:T], S16[:, gsl[0], 0:T], ALU.mult, [brs[0][1], R("S16", gsl[0])], [racc])
                for b_ in range(1, 4):
                    tt("dve", tmp, brs[b_][0][:, 0:T], S16[:, gsl[b_], 0:T], ALU.mult,
                       [brs[b_][1], R("S16", gsl[b_])], [rtmp])
                    if b_ < 3:
                        tt("pool", acc, acc, tmp, ALU.add, [racc, rtmp], [racc])
                    else:
                        tt("pool", S16[:, c, 0:T], acc, tmp, ALU.add, [racc, rtmp], [R("S16", c)])
            wos = [W(l, "wo0"), W(l, "wo1")]
            mres = [R("S32", c) for c in range(8)]
            for c in range(8):
                w, rw = wos[c // 4]
                ps, rps = psum()
                mm_group(ps[:, 0:T], rps, [(w[:, k, (c % 4) * 128:(c % 4 + 1) * 128], S16[:, k, 0:T], [rw, R("S16", k)])
                                           for k in range(8)])
                act(S32[:, c, 0:T], ps[:, 0:T], AF.Copy, [rps], [mres[c]])
            post_norm_residual(l, T, 2, col, mres, xres)
            norm_to_h(l, T, 3, 4, col, xres)
            for i in range(8):
                w, rw = W(l, f"f1_{i}")
                for ci in range(4):
                    j = i * 4 + ci
                    ps, rps = psum()
                    mm_group(ps[:, 0:T], rps, [(w[:, k, ci * 128:(ci + 1) * 128], hT[:, k, 0:T], [rw, hres[k]])
                                               for k in range(8)])
                    tmp, rtmp = S32[:, 8 + (j % 4), 0:T], R("S32", 8 + (j % 4))
                    act(tmp, ps[:, 0:T], AF.Relu, [rps], [rtmp])
                    tt("dve" if j % 2 else "pool", S16[:, j, 0:T], tmp, tmp, ALU.mult, [rtmp], [R("S16", j)])
            for c in range(8):
                w, rw = W(l, f"f2_{c}")
                ps, rps = psum()
                mm_group(ps[:, 0:T], rps, [(w[:, k, :], S16[:, k, 0:T], [rw, R("S16", k)]) for k in range(32)])
                act(S32[:, c, 0:T], ps[:, 0:T], AF.Copy, [rps], [mres[c]])
            post_norm_residual(l, T, 5, col, mres, xres)
            store_x(pas, l, blocks, T, xres)

        for si, (kind, pas, l, blocks) in enumerate(schedule(do_sample)[:limit]):
            if kind == "setup" and l == 1:
                while l1_cast:
                    issue_cast(1, l1_cast.pop(0))
            if l == 0 and si >= 4 and l1_cast:
                issue_cast(1, l1_cast.pop(0))
            if kind == "setup":
                layer_setup(l)
            elif kind == "A":
                phase_A(pas, l, blocks)
            else:
                phase_B(pas, l, blocks)
        assert DEBUG_STAGE or wstate["pos"] == len(seq), (wstate["pos"], len(seq))
        P.wait_all("act", d_outb + d_kvs[0] + d_kvs[1] + [d_sc, d_x, d_rope] + d_xb + d_cs)
        P.wait_all("sp", d_ring + d_cast)
        P.wait_all("pool", d_cast + d_cs)
        print("sbuf bytes remaining", nc.sbuf_bytes_remaining)
        print("instructions", P.n_inst, "waits", P.n_wait)
        P.emit()
    return nc


def tiles_of(blocks):
    return [blocks[i:i + 4] for i in range(0, len(blocks), 4)]


def schedule(do_sample):
    out = []
    for l in range(L):
        out.append(("setup", None, l, None))
        for t in tiles_of(list(range(8))):
            out.append(("A", "p", l, t))
        for t in tiles_of(list(range(8))):
            out.append(("B", "p", l, t))
        if do_sample:
            lo, hi = (0, 20) if l == 0 else (1, 19)
            for t in tiles_of(list(range(lo, hi))):
                out.append(("A", "s", l, t))
            lo, hi = (1, 19) if l == 0 else (2, 18)
            for t in tiles_of(list(range(lo, hi))):
                out.append(("B", "s", l, t))
    return out


def slab_sequence(do_sample, limit=None):
    seq = []
    for (kind, pas, l, blocks) in schedule(do_sample)[:limit]:
        if kind == "setup":
            seq += [(l, f"ada{i}") for i in range(12)]
        elif kind == "A":
            seq += [(l, "A0"), (l, "A1"), (l, "A2"), (l, "AT")]
        else:
            seq += [(l, "B0"), (l, "B1"), (l, "B2"), (l, "wpool")]
            for c in range(8):
                seq += [(l, f"g{c}"), (l, f"br{c}")]
            seq += [(l, "wo0"), (l, "wo1")]
            seq += [(l, f"f1_{i}") for i in range(8)]
            seq += [(l, f"f2_{c}") for c in range(8)]
    return seq


_NC_CACHE = {}


def host_inputs(inp, core):
    b, half = core // 2, core % 2
    start = half * 2048
    m = {}
    m["xp"] = np.ascontiguousarray(inp["x_prompt"][4 * core:4 * core + 4].reshape(1024, D))
    xs = np.zeros((WIN, D), np.float32)
    lo, hi = start - 256, start + 2304
    slo, shi = max(lo, 0), min(hi, 4096)
    xs[slo - lo:shi - lo] = inp["x_sample"][b, slo:shi]
    m["xs"] = xs
    validL = 1.0 if lo >= 0 else 0.0
    validR = 1.0 if hi <= 4096 else 0.0
    cn = np.zeros((128, 3 * 128 + 3 * 64 + 256), np.float32)
    cn[:, 0:128] = np.eye(128)
    kk = np.arange(128)[:, None]
    qq = np.arange(128)[None, :]
    cn[:, 128:256] = (kk >= qq)
    cn[:, 256:384] = (kk <= qq)
    cn[:, 384:448] = 1.0
    cn[:, 448:512] = validL
    cn[:, 512:576] = validR
    cn[:, 576:704] = 1.0 / 1024
    cn[:, 704:832] = 1.0 / 256
    m["cnst"] = cn
    PM = pool_mats()
    pm = np.zeros((7, 4, 128, 128), np.float32)
    pm[0:5] = PM
    pm[5] = PM[3] if half == 0 else PM[2]
    pm[6] = PM[4] if half == 1 else PM[2]
    m["pmat"] = np.ascontiguousarray(pm.reshape(28, 128, 128).transpose(1, 0, 2).reshape(128, 28 * 128))
    sm = np.zeros((128, 16 + L * 128), np.float32)
    cond = np.stack([inp["c_ctx"], inp["c"][b]], axis=1)
    sm[:, 0:16] = cond.reshape(8, 128, 2).transpose(1, 0, 2).reshape(128, 16)
    for l in range(L):
        o = 16 + l * 128
        sm[:, o:o + 48] = inp["b_ada"][l].reshape(48, 128).T
        for i, nm in enumerate(("g_pre_mix", "g_post_mix", "g_pre_ffn", "g_post_ffn")):
            sm[:, o + 48 + 8 * i:o + 56 + 8 * i] = inp[nm][l].reshape(8, 128).T
        sm[:, o + 80:o + 82] = inp["b_conv_a"][l].reshape(2, 128).T
        sm[:, o + 82:o + 84] = inp["ln_g_a"][l].reshape(2, 128).T
        sm[:, o + 84:o + 86] = inp["ln_b_a"][l].reshape(2, 128).T
        sm[0:64, o + 86:o + 90] = inp["pool_scale"][l].reshape(4, 64).T
        sm[:, o + 90:o + 98] = inp["sink"][l][None, :]
    sm[:, 16 + 98] = validL
    sm[:, 16 + 99] = validR
    m["small"] = sm
    cw = np.zeros((128, L, 2, 34), np.float32)
    for l in range(L):
        cw[:, l, :, 0:31] = inp["w_conv_a"][l].reshape(31, 2, 128).transpose(2, 1, 0)
        cw[:, l, :, 31:34] = inp["w_sc"][l].reshape(3, 2, 128).transpose(2, 1, 0)
    m["convw"] = cw
    pos = np.arange(lo, hi)
    posc = np.clip(pos, 0, 4095)
    row, colp = posc // 64, posc % 64
    p = np.arange(128)
    d = p % 64
    r = d % 32
    fi = (r % 16).astype(np.float64)
    freq = 10000.0 ** (-fi / 16.0)
    psel = np.where((d < 32)[:, None], row[None, :], colp[None, :]).astype(np.float64)
    ang = psel * freq[:, None]
    sgn = np.where(r < 16, -1.0, 1.0)[:, None]
    rp = np.zeros((128, 2, WIN), np.float32)
    rp[:, 0] = np.cos(ang)
    rp[:, 1] = np.sin(ang) * sgn
    m["rope"] = rp
    m["kcT"] = np.ascontiguousarray(inp["cache_k"][b].reshape(L, 256, 128).transpose(2, 0, 1))
    m["vc"] = np.ascontiguousarray(inp["cache_v"][b].reshape(L, 2, 128, 128).transpose(2, 0, 1, 3))
    return m


def kernel(**inputs):
    inp = {k: np.asarray(v) for k, v in inputs.items()}
    if "nc" not in _NC_CACHE:
        _NC_CACHE["nc"] = build_program(True)
    nc = _NC_CACHE["nc"]
    packs = [host_pack_layer(inp, l) for l in range(L)]
    in_maps = []
    for core in range(8):
        m = host_inputs(inp, core)
        for l in range(L):
            m[f"wpk{l}"] = packs[l]
        in_maps.append(m)
    res = run_bass_kernel_spmd(nc, in_maps, core_ids=list(range(8)))
    rs = res.results
    yp = np.concatenate([rs[c]["yp"].reshape(4, 256, D) for c in range(8)], axis=0)
    ys = np.stack([np.concatenate([rs[2 * b]["ys"], rs[2 * b + 1]["ys"]], axis=0) for b in range(4)], axis=0)
    nk = np.concatenate([rs[c]["nk"].reshape(L, 4, 256, 2, 64).transpose(1, 0, 2, 3, 4) for c in range(8)], axis=0)
    nv = np.concatenate([rs[c]["nv"].reshape(L, 4, 256, 2, 64).transpose(1, 0, 2, 3, 4) for c in range(8)], axis=0)
    return (yp.astype(np.float32), ys.astype(np.float32), nk.astype(np.float32), nv.astype(np.float32))
```

```python
from contextlib import ExitStack
import numpy as np
import concourse.bass as bass
import concourse.mybir as mybir
from concourse.bass_utils import run_bass_kernel_spmd

F32 = mybir.dt.float32
BF16 = mybir.dt.bfloat16
AF = mybir.ActivationFunctionType
ALU = mybir.AluOpType

ENGS = ("pe", "act", "dve", "pool", "sp")
DEBUG_STAGE = 0
D = 1024
L = 2
NB_S = 20
WIN = NB_S * 128
PADC = 16
EPS = 1e-6
SLAB = 4096


class Sem:
    __slots__ = ("h", "count", "name")

    def __init__(self, h, name):
        self.h = h
        self.count = 0
        self.name = name


class Res:
    __slots__ = ("name", "w", "r", "excl")

    def __init__(self, name=""):
        self.name = name
        self.w = None
        self.r = {}
        self.excl = False


class Prog:
    def __init__(self, nc, stack):
        self.nc = nc
        self.stack = stack
        self.streams = {e: [] for e in ENGS}
        self.esem = {e: self.new_sem("eng_" + e) for e in ENGS}
        self.waited = {e: {} for e in ENGS}
        self.n_inst = 0
        self.n_wait = 0
        self._res = {}

    def new_sem(self, name):
        return Sem(self.stack.enter_context(self.nc.semaphore(name)), name)

    def R(self, *key):
        r = self._res.get(key)
        if r is None:
            r = Res(str(key))
            self._res[key] = r
        return r

    def _collect(self, eng, reads, writes):
        deps = {}

        def add(t, kind):
            if t is None:
                return
            s, v, e = t
            if e == eng and eng == "pe":
                return
            if deps.get(s, 0) < v:
                deps[s] = v

        for r in reads:
            add(r.w, "raw")
            if r.excl:
                for s, (v, e) in r.r.items():
                    if e != eng:
                        add((s, v, e), "rar")
        for w in writes:
            add(w.w, "waw")
            for s, (v, e) in w.r.items():
                add((s, v, e), "war")
        waits = []
        wd = self.waited[eng]
        for s, v in deps.items():
            if wd.get(s, 0) < v:
                wd[s] = v
                waits.append((s.h, v))
        return waits

    def _mark(self, ticket, reads, writes):
        s, v, e = ticket
        for r in reads:
            cur = r.r.get(s)
            if cur is None or cur[0] < v:
                r.r[s] = (v, e)
        for w in writes:
            w.w = ticket
            w.r = {}

    def op(self, eng, fn, reads=(), writes=(), signal=True):
        waits = self._collect(eng, reads, writes)
        sem = self.esem[eng]
        ticket = (sem, sem.count + 1, eng)
        if signal:
            sem.count += 1
        self.streams[eng].append((waits, fn, (sem.h, 1) if signal else None))
        self._mark(ticket, reads, writes)
        self.n_inst += 1
        self.n_wait += len(waits)
        return ticket

    def dma(self, eng, dsem, out_ap, in_ap, reads=(), writes=(), **kw):
        waits = self._collect(eng, reads, writes)
        dsem.count += 16
        ticket = (dsem, dsem.count, "dma")

        def fn(e, out_ap=out_ap, in_ap=in_ap):
            return e.dma_start(out=out_ap, in_=in_ap, **kw)

        self.streams[eng].append((waits, fn, (dsem.h, 16)))
        self._mark(ticket, reads, writes)
        self.n_inst += 1
        self.n_wait += len(waits)
        return ticket

    def wait_all(self, eng, sems):
        waits = [(s.h, s.count) for s in sems if s.count > 0]
        self.streams[eng].append((waits, None, None))

    def emit(self):
        streams = self.streams

        def run(e_obj, lst):
            for waits, fn, inc in lst:
                for (h, v) in waits:
                    e_obj.wait_ge(h, v)
                if fn is None:
                    continue
                ins = fn(e_obj)
                if inc is not None:
                    ins.then_inc(inc[0], inc[1])

        with self.nc.Block() as block:
            @block.tensor
            def _(t):
                run(t, streams["pe"])

            @block.scalar
            def _(s):
                run(s, streams["act"])

            @block.vector
            def _(v):
                run(v, streams["dve"])

            @block.gpsimd
            def _(g):
                run(g, streams["pool"])

            @block.sync
            def _(sp):
                run(sp, streams["sp"])


def pack_spec():
    spec = []
    for i in range(12):
        spec.append((f"ada{i}", 8, 512))
    spec += [("A0", 8, 512), ("A1", 8, 512), ("A2", 8, 256), ("AT", 8, 512)]
    spec += [("B0", 8, 512), ("B1", 8, 512), ("B2", 8, 256)]
    spec.append(("wpool", 1, 256))
    for c in range(8):
        spec.append((f"g{c}", 8, 512))
        spec.append((f"br{c}", 1, 2048))
    spec += [("wo0", 8, 512), ("wo1", 8, 512)]
    for i in range(8):
        spec.append((f"f1_{i}", 8, 512))
    for c in range(8):
        spec.append((f"f2_{c}", 32, 128))
    offs = {}
    o = 0
    for name, kc, n in spec:
        offs[name] = (o, kc, n)
        o += kc * n
    return spec, offs, o


PACK_SPEC, PACK_OFFS, PACK_EL = pack_spec()


def rope_swap_idx():
    idx = np.zeros(64, np.int64)
    for d in range(64):
        blk, r = d // 32, d % 32
        partner = r + 16 if r < 16 else r - 16
        idx[d] = blk * 32 + partner
    return idx


def host_pack_layer(inp, l):
    w_in = inp["w_in"][l]
    out = np.zeros((128, PACK_EL), np.float32)

    def put(name, arr):
        o, kc, n = PACK_OFFS[name]
        assert arr.shape == (128, kc, n), (name, arr.shape)
        out[:, o:o + kc * n] = arr.reshape(128, kc * n)

    def kmaj(w):
        K, n = w.shape
        return np.ascontiguousarray(w.reshape(K // 128, 128, n).transpose(1, 0, 2))

    wada = inp["w_ada"][l]
    for i in range(12):
        put(f"ada{i}", kmaj(wada[:, i * 512:(i + 1) * 512]))
    sw = rope_swap_idx()
    kcols = np.arange(2048, 2176)
    kscols = np.concatenate([2048 + g * 64 + sw for g in range(2)])
    colsA = np.concatenate([np.arange(256, 512), np.arange(0, 256), np.arange(1024, 1280),
                            np.arange(1280, 1536), kcols, kscols])
    WA = w_in[:, colsA]
    put("A0", kmaj(WA[:, 0:512])); put("A1", kmaj(WA[:, 512:1024])); put("A2", kmaj(WA[:, 1024:1280]))
    colsT = np.concatenate([np.arange(512, 768), np.arange(2176, 2304), np.arange(2048, 2176)])
    put("AT", kmaj(w_in[:, colsT]))
    qcols = np.concatenate([np.concatenate([1536 + i * 64 + np.arange(64), 1536 + (4 + i) * 64 + np.arange(64)])
                            for i in range(4)])
    qscols = np.concatenate([np.concatenate([1536 + i * 64 + sw, 1536 + (4 + i) * 64 + sw]) for i in range(4)])
    colsB = np.concatenate([np.arange(768, 1024), qcols, qscols])
    WB = w_in[:, colsB]
    put("B0", kmaj(WB[:, 0:512])); put("B1", kmaj(WB[:, 512:1024])); put("B2", kmaj(WB[:, 1024:1280]))
    wp = np.zeros((128, 1, 256), np.float32)
    for g in range(4):
        wp[0:64, 0, g * 64:(g + 1) * 64] = inp["w_pool"][l, g]
    put("wpool", wp)
    for c in range(8):
        gcols = np.concatenate([2304 + b * 1024 + c * 128 + np.arange(128) for b in range(4)])
        put(f"g{c}", kmaj(w_in[:, gcols]))
        br = np.zeros((128, 1, 2048), np.float32)
        cs = slice(c * 128, (c + 1) * 128)
        for k in range(2):
            br[:, 0, k * 128:(k + 1) * 128] = inp["w_a_out"][l, k * 128:(k + 1) * 128, cs]
            br[:, 0, 768 + k * 128:768 + (k + 1) * 128] = inp["w_c_out"][l, k * 128:(k + 1) * 128, cs]
        for g in range(4):
            br[0:64, 0, 256 + g * 128:256 + (g + 1) * 128] = inp["w_b_out"][l, g * 64:(g + 1) * 64, cs]
        for h in range(8):
            br[0:64, 0, 1024 + h * 128:1024 + (h + 1) * 128] = inp["w_d_out"][l, h * 64:(h + 1) * 64, cs]
        put(f"br{c}", br)
    wo = inp["w_o"][l]
    put("wo0", kmaj(wo[:, 0:512])); put("wo1", kmaj(wo[:, 512:1024]))
    w1 = inp["w_ff1"][l]
    for i in range(8):
        put(f"f1_{i}", kmaj(w1[:, i * 512:(i + 1) * 512]))
    w2 = inp["w_ff2"][l]
    for c in range(8):
        put(f"f2_{c}", kmaj(w2[:, c * 128:(c + 1) * 128]))
    return out


def pool_mats():
    sizes = (2, 4, 8, 16)
    M = np.zeros((5, 4, 128, 128), np.float64)
    for g, w in enumerate(sizes):
        hw = w // 2
        for t in range(128):
            for kind in range(5):
                for tp_rel in range(t - hw, t + hw):
                    if kind in (0, 1, 2):
                        cnt = w
                    elif kind == 3:
                        cnt = (t + hw) - max(t - hw, 0)
                    else:
                        cnt = min(t + hw, 128) - (t - hw)
                    if kind == 0 and tp_rel < 0:
                        M[kind, g, tp_rel + 128, t] += 1.0 / cnt
                    elif kind == 1 and tp_rel >= 128:
                        M[kind, g, tp_rel - 128, t] += 1.0 / cnt
                    elif kind >= 2 and 0 <= tp_rel < 128:
                        M[kind, g, tp_rel, t] += 1.0 / cnt
                if kind >= 2:
                    M[kind, g, t, t] -= 1.0
    return M.astype(np.float32)


def build_program(do_sample=True, limit=None):
    nc = bass.Bass("TRN2", target_bir_lowering=False)
    dt_in = lambda name, shape: nc.dram_tensor(name, list(shape), F32, kind="ExternalInput").ap()
    xp_d = dt_in("xp", [1024, D])
    xs_d = dt_in("xs", [WIN, D])
    wpk_d = [dt_in(f"wpk{l}", [128, PACK_EL]) for l in range(L)]
    cnst_d = dt_in("cnst", [128, 3 * 128 + 3 * 64 + 128 * 2])
    pmat_d = dt_in("pmat", [128, 7 * 4 * 128])
    small_d = dt_in("small", [128, 16 + L * 128])
    rope_d = dt_in("rope", [128, 2, WIN])
    kc_d = dt_in("kcT", [128, L, 256])
    vc_d = dt_in("vc", [128, L, 2, 128])
    yp_d = nc.dram_tensor("yp", [1024, D], F32, kind="ExternalOutput").ap()
    ys_d = nc.dram_tensor("ys", [2048, D], F32, kind="ExternalOutput").ap()
    nk_d = nc.dram_tensor("nk", [L, 1024, 128], F32, kind="ExternalOutput").ap()
    nv_d = nc.dram_tensor("nv", [L, 1024, 128], F32, kind="ExternalOutput").ap()
    wbf_d = [nc.dram_tensor(f"wbf{l}", [128, PACK_EL], BF16).ap() for l in range(L)]
    xsc_d = {"p": nc.dram_tensor("xscp", [128, 8, 1024], F32).ap(),
             "s": nc.dram_tensor("xscs", [128, 8, WIN], F32).ap()}

    with ExitStack() as st:
        P = Prog(nc, st)
        R = P.R

        def sb(name, shape, dt):
            return st.enter_context(nc.sbuf_tensor(name, list(shape), dt))

        ring_n = 5
        ring = sb("ring", [128, ring_n, SLAB], BF16)
        xTs = [sb("xTa", [128, 8, 512], F32), sb("xTb", [128, 8, 512], F32)]
        XS = {"i": 0, "pref": None}
        hT = sb("hT", [128, 8, 512], BF16)
        S16 = sb("S16", [128, 32, 512], BF16)
        S32 = sb("S32", [128, 14, 512], F32)

        def xtok(bi, lo=0, hi=D):
            return S32[:, 2 * bi:2 * bi + 2, :].rearrange("p a b -> p (a b)")[:, lo:hi]

        def xtok_res(bi):
            return [R("S32", 2 * bi), R("S32", 2 * bi + 1)]
        ua_c = sb("ua_c", [128, 2, WIN + 2 * PADC + 64], BF16)
        sc_c = sb("sc_c", [128, 2, WIN + 2 * PADC + 64], BF16)
        kT = sb("kT", [128, WIN], BF16)
        vtok = sb("vtok", [128, NB_S, 128], BF16)
        uptok = sb("uptok", [128, NB_S, 256], BF16)
        qT = sb("qT", [128, 4, 512], BF16)
        kvo = sb("kvo", [128, 1, 256], F32)
        ropeT = sb("ropeT", [128, 2, 512], F32)
        kcT = sb("kcTs", [128, L, 256], BF16)
        vcs = sb("vcs", [128, L, 2, 128], BF16)
        cn32 = sb("cn32", [128, 128], F32)
        cbf = sb("cbf", [128, 3 * 128 + 3 * 64 + 256], BF16)
        pmat = sb("pmat_sb", [128, 28, 128], BF16)
        small = sb("small_sb", [128, 16 + L * 128], F32)
        condb = sb("condb", [128, 8, 2], BF16)
        modv = sb("modv", [128, L, 48, 2], F32)
        dvec = sb("dvec", [128, L, 6, 8, 2], F32)
        diagA = sb("diagA", [128, 8, 128], BF16)
        diagC = sb("diagC", [128, 2, 3, 128], BF16)
        sinkf = sb("sinkf", [64, 8], F32)
        psb = [st.enter_context(nc.psum_tensor(f"ps{i}", [128, 512], F32)) for i in range(8)]

        ident32 = cn32
        identb = cbf[:, 0:128]
        maskP = cbf[:, 128:256]
        maskN = cbf[:, 256:384]
        ones64 = [cbf[:, 384 + i * 64:384 + (i + 1) * 64] for i in range(3)]
        r1024 = cbf[:, 576:704]
        r256 = cbf[:, 704:832]

        NCH = 8
        d_cast = [P.new_sem(f"d_cast{l}_{i}") for l in range(L) for i in range(NCH)]
        d_ring = [P.new_sem(f"d_ring{i}") for i in range(ring_n)]
        d_x = P.new_sem("d_x")
        d_xf = [P.new_sem("d_xf0"), P.new_sem("d_xf1")]
        d_scs = [P.new_sem("d_sc0"), P.new_sem("d_sc1")]
        d_xb = [P.new_sem(f"d_xb{i}") for i in range(4)]
        d_cs = [P.new_sem(f"d_c{i}") for i in range(8)]
        d_rope = P.new_sem("d_rope")
        d_sc = P.new_sem("d_sc")
        d_outb = [P.new_sem(f"d_out{i}") for i in range(4)]
        d_kvs = [[P.new_sem(f"d_kv{i}{j}") for j in range(2)] for i in range(1)]

        ps_ctr = {}
        dg_ctr = [0]

        def psum(lo=0, hi=8):
            k = (lo, hi)
            n = ps_ctr.get(k, 0)
            ps_ctr[k] = n + 1
            i = lo + (n % (hi - lo))
            r = R("ps", i)
            r.excl = True
            return psb[i], r

        ncn = 3 * 128 + 3 * 64 + 256
        P.dma("pool", d_cs[0], cbf[:], cnst_d[:, 0:ncn], writes=[R("cbf")])
        P.dma("act", d_cs[1], cn32[:], cnst_d[:, 0:128], writes=[R("cn32")])
        P.dma("pool", d_cs[2], pmat[:], pmat_d.rearrange("p (a b) -> p a b", b=128), writes=[R("pmat")])
        P.dma("act", d_cs[3], small[:], small_d, writes=[R("small")])
        P.dma("pool", d_cs[4], kcT[:], kc_d, writes=[R("kc")])
        P.dma("pool", d_cs[5], vcs[:], vc_d, writes=[R("vc")])
        P.op("pool", lambda e: e.memset(ua_c[:], 0.0), writes=[R("ua_c")])
        P.op("pool", lambda e: e.memset(sc_c[:], 0.0), writes=[R("sc_c")])
        ch = PACK_EL // NCH

        def issue_cast(l, i):
            hi_ = PACK_EL if i == NCH - 1 else (i + 1) * ch
            P.dma("pool", d_cast[l * NCH + i], wbf_d[l][:, i * ch:hi_], wpk_d[l][:, i * ch:hi_],
                  writes=[R("wbf", l, i)], max_dma_last_dim=16384)

        for i in range(NCH):
            issue_cast(0, i)
        l1_cast = list(range(NCH))

        def wbf_res(l, o, n):
            lo_i = min(NCH - 1, o // ch)
            hi_i = min(NCH - 1, (o + n - 1) // ch)
            return [R("wbf", l, i) for i in range(lo_i, hi_i + 1)]

        def sm(l, a, b):
            return small[:, 16 + l * 128 + a:16 + l * 128 + b]

        validL = small[:, 16 + 98:16 + 99]
        validR = small[:, 16 + 99:16 + 100]
        P.op("act", lambda e: e.activation(out=condb[:], in_=small[:, 0:16].rearrange("p (c t) -> p c t", t=2),
                                           func=AF.Silu), reads=[R("small")], writes=[R("condb")])

        seq = slab_sequence(do_sample, limit)
        wstate = {"next_load": 0, "pos": 0}

        def ensure_loaded(upto):
            while wstate["next_load"] <= upto and wstate["next_load"] < len(seq):
                s = wstate["next_load"]
                l, name = seq[s]
                o, kc, n = PACK_OFFS[name]
                slot = s % ring_n
                P.dma("sp", d_ring[slot], ring[:, slot, 0:kc * n], wbf_d[l][:, o:o + kc * n],
                      reads=wbf_res(l, o, kc * n), writes=[R("ring", slot)])
                wstate["next_load"] += 1

        def W(l, name):
            s = wstate["pos"]
            assert seq[s] == (l, name), (s, seq[s], l, name)
            wstate["pos"] += 1
            ensure_loaded(s + ring_n - 3)
            o, kc, n = PACK_OFFS[name]
            slot = s % ring_n
            return ring[:, slot, 0:kc * n].rearrange("p (k n) -> p k n", n=n), R("ring", slot)

        def mm_group(out_ap, out_res, items, extra_reads=(), signal_each=False):
            n = len(items)
            for i, (lhsT, rhs, rs) in enumerate(items):
                P.op("pe", lambda e, lhsT=lhsT, rhs=rhs, i=i: e.matmul(out_ap, lhsT=lhsT, rhs=rhs,
                                                                         start=(i == 0), stop=(i == n - 1)),
                     reads=list(rs) + list(extra_reads), writes=[out_res], signal=(signal_each or i == n - 1))

        def act(out, in_, func, reads, writes, **kw):
            P.op("act", lambda e: e.activation(out=out, in_=in_, func=func, **kw), reads=reads, writes=writes)

        def tt(eng, out, in0, in1, op, reads, writes):
            P.op(eng, lambda e: e.tensor_tensor(out=out, in0=in0, in1=in1, op=op), reads=reads, writes=writes)

        def stt(eng, out, in0, scalar, in1, op0, op1, reads, writes):
            P.op(eng, lambda e: e.scalar_tensor_tensor(out=out, in0=in0, scalar=scalar, in1=in1, op0=op0, op1=op1),
                 reads=reads, writes=writes)

        def layer_setup(l):
            ps, rps = psum()
            for i in range(12):
                w, rw = W(l, f"ada{i}")
                for mi in range(4):
                    m = i * 4 + mi
                    mm_group(ps[:, 2 * m:2 * m + 2], rps,
                             [(w[:, k, mi * 128:(mi + 1) * 128], condb[:, k, :], [rw, R("condb")]) for k in range(8)])
            bada = sm(l, 0, 48)
            P.op("dve", lambda e: e.tensor_tensor(out=modv[:, l], in0=ps[:, 0:96].rearrange("p (m t) -> p m t", t=2),
                                                  in1=bada.unsqueeze(2).to_broadcast([128, 48, 2]), op=ALU.add),
                 reads=[rps, R("small")], writes=[R("modv", l)])
            def gvec(i):
                return sm(l, 48 + 8 * i, 56 + 8 * i).unsqueeze(2).to_broadcast([128, 8, 2])
            mv = modv[:, l]
            dv = dvec[:, l]
            rd, wr = [R("modv", l), R("small")], [R("dvec", l)]
            stt("dve", dv[:, 0], mv[:, 8:16], 1.0, gvec(0), ALU.add, ALU.mult, rd, wr)
            P.op("dve", lambda e: e.tensor_copy(out=dv[:, 1], in_=mv[:, 0:8]), reads=rd, writes=wr)
            tt("dve", dv[:, 2], mv[:, 16:24], gvec(1), ALU.mult, rd, wr)
            stt("dve", dv[:, 3], mv[:, 32:40], 1.0, gvec(2), ALU.add, ALU.mult, rd, wr)
            P.op("dve", lambda e: e.tensor_copy(out=dv[:, 4], in_=mv[:, 24:32]), reads=rd, writes=wr)
            tt("dve", dv[:, 5], mv[:, 40:48], gvec(3), ALU.mult, rd, wr)
            wca = small_conv[l]
            for c in range(2):
                for k in range(3):
                    P.op("pool", lambda e, c=c, k=k: e.tensor_scalar(out=diagC[:, c, k, :], in0=ident32[:],
                                                                     scalar1=wca[:, c, 31 + k:32 + k], scalar2=None,
                                                                     op0=ALU.mult),
                         reads=[R("cn32"), R("convw")], writes=[R("diagC")])
            act(sinkf[:], sm(l, 90, 98)[0:64, :], AF.Exp, [R("small")], [R("sinkf")])

        convw_d = dt_in("convw", [128, L, 2, 34])
        convw = sb("convw_sb", [128, L, 2, 34], F32)
        P.dma("act", d_cs[6], convw[:], convw_d, writes=[R("convw")])
        small_conv = [convw[:, l] for l in range(L)]

        def norm_to_h(l, T, a_idx, b_idx, col, xres):
            ps, rps = psum()
            items = []
            SQE = ("act", "dve", "pool", "act", "dve", "act", "dve", "pool")
            for c in range(8):
                sq, rsq = S16[:, 8 + c, 0:T], R("S16", 8 + c)
                if DEBUG_STAGE != 212:
                    if SQE[c] == "act":
                        act(sq, XT()[:, c, 0:T], AF.Square, [xres[c]], [rsq])
                    else:
                        tt(SQE[c], sq, XT()[:, c, 0:T], XT()[:, c, 0:T], ALU.mult, [xres[c]], [rsq])
                if DEBUG_STAGE != 211:
                    P.op("pe", lambda e, sq=sq, c=c: e.matmul(ps[:, 0:T], lhsT=r1024, rhs=sq, start=(c == 0), stop=(c == 7)),
                         reads=[rsq, R("cbf")], writes=[rps], signal=True)
            if DEBUG_STAGE in (21, 211, 212):
                return
            rsq_, rr0 = S32[:, 12, 0:T], R("S32", 12)
            rstd, rr = S32[:, 13, 0:T], R("S32", 13)
            act(rsq_, ps[:, 0:T], AF.Sqrt, [rps], [rr0], bias=EPS, scale=1.0)
            if DEBUG_STAGE == 22:
                return
            P.op("dve", lambda e: e.reciprocal(out=rstd, in_=rsq_), reads=[rr0], writes=[rr])
            if DEBUG_STAGE == 23:
                return
            for c in range(8):
                tmp, rt = S32[:, 10 + (c % 2), 0:T], R("S32", 10 + (c % 2))
                eng = "dve"
                tt(eng, tmp, XT()[:, c, 0:T], rstd, ALU.mult, [xres[c], rr], [rt])
                act(hT[:, c, 0:T], tmp, AF.Identity, [rt, R("dvec", l)], [R("hT", c)],
                    scale=dvec[:, l, a_idx, c, col:col + 1], bias=dvec[:, l, b_idx, c, col:col + 1])

        def post_norm_residual(l, T, g_idx, col, mres, xres):
            ps, rps = psum()
            SQE = ("act", "dve", "pool", "act", "dve", "act", "dve", "pool")
            for c in range(8):
                sq, rsq = S16[:, 8 + c, 0:T], R("S16", 8 + c)
                if SQE[c] == "act":
                    act(sq, S32[:, c, 0:T], AF.Square, [mres[c]], [rsq])
                else:
                    tt(SQE[c], sq, S32[:, c, 0:T], S32[:, c, 0:T], ALU.mult, [mres[c]], [rsq])
                P.op("pe", lambda e, sq=sq, c=c: e.matmul(ps[:, 0:T], lhsT=r1024, rhs=sq, start=(c == 0), stop=(c == 7)),
                     reads=[rsq, R("cbf")], writes=[rps], signal=True)
            rsq_, rr0 = S32[:, 12, 0:T], R("S32", 12)
            rstd, rr = S32[:, 13, 0:T], R("S32", 13)
            act(rsq_, ps[:, 0:T], AF.Sqrt, [rps], [rr0], bias=EPS, scale=1.0)
            P.op("dve", lambda e: e.reciprocal(out=rstd, in_=rsq_), reads=[rr0], writes=[rr])
            for c in range(8):
                tt("dve", S32[:, c, 0:T], S32[:, c, 0:T], rstd, ALU.mult, [mres[c], rr], [mres[c]])
                stt("dve", XT()[:, c, 0:T], S32[:, c, 0:T], dvec[:, l, g_idx, c, col:col + 1],
                    XT()[:, c, 0:T], ALU.mult, ALU.add, [mres[c], R("dvec", l), xres[c]], [xres[c]])

        def XT():
            return xTs[XS["i"]]

        def issue_tok_dma(pas, blocks):
            src = xp_d if pas == "p" else xs_d
            for bi, b in enumerate(blocks):
                P.dma("act", d_xb[bi], xtok(bi), src[b * 128:(b + 1) * 128, :], writes=xtok_res(bi))

        def issue_feat_dma(pas, blocks, idx):
            T = len(blocks) * 128
            b0 = blocks[0]
            P.dma("act", d_xf[idx], xTs[idx][:, :, 0:T], xsc_d[pas][:, :, b0 * 128:b0 * 128 + T],
                  reads=[R("xsc", pas, 0), R("xsc", pas, 1)], writes=[R("xT", idx, c) for c in range(8)])

        def load_x(kind, pas, l, blocks, T, nxt):
            XS["i"] ^= 1
            cur = XS["i"]
            xres = [R("xT", cur, c) for c in range(8)]
            nb = len(blocks)
            key = (kind, pas, l, tuple(blocks))
            tokmode = (l == 0 and kind == "A")
            if XS["pref"] != key:
                if tokmode:
                    issue_tok_dma(pas, blocks)
                else:
                    issue_feat_dma(pas, blocks, cur)
            XS["pref"] = None
            if tokmode:
                for c in range(8):
                    ps, rps = psum()
                    for bi in range(nb):
                        P.op("pe", lambda e, bi=bi, c=c, ps=ps: e.matmul(ps[:, bi * 128:(bi + 1) * 128],
                                                                          lhsT=xtok(bi, c * 128, (c + 1) * 128), rhs=ident32[:],
                                                                          start=True, stop=True),
                             reads=xtok_res(bi) + [R("cn32")], writes=[rps], signal=(bi == nb - 1))
                    if c % 2 == 0:
                        act(XT()[:, c, 0:T], ps[:, 0:T], AF.Copy, [rps], [xres[c]])
                    else:
                        P.op("dve", lambda e, c=c, ps=ps, xb=XT(): e.tensor_copy(out=xb[:, c, 0:T], in_=ps[:, 0:T]),
                             reads=[rps], writes=[xres[c]])
                b0 = blocks[0]
                P.dma("act", d_scs[cur], xsc_d[pas][:, :, b0 * 128:b0 * 128 + T], XT()[:, :, 0:T],
                      reads=xres, writes=[R("xsc", pas, cur)])
            if nxt is not None:
                nkind, npas, nl, nblocks = nxt
                ntok = (nl == 0 and nkind == "A")
                if ntok and tokmode:
                    issue_tok_dma(npas, nblocks)
                    XS["pref"] = (nkind, npas, nl, tuple(nblocks))
                elif not ntok:
                    issue_feat_dma(npas, nblocks, cur ^ 1)
                    XS["pref"] = (nkind, npas, nl, tuple(nblocks))
            return xres

        def store_x(pas, l, blocks, T, xres):
            b0 = blocks[0]
            if l == 0:
                P.dma("act", d_scs[XS["i"]], xsc_d[pas][:, :, b0 * 128:b0 * 128 + T], XT()[:, :, 0:T],
                      reads=xres, writes=[R("xsc", pas, XS["i"])])
            else:
                dst = yp_d if pas == "p" else ys_d
                for bi, b in enumerate(blocks):
                    ob = b if pas == "p" else b - 2
                    for half in range(2):
                        ps, rps = psum()
                        for cc in range(4):
                            c = half * 4 + cc
                            P.op("pe", lambda e, bi=bi, c=c, cc=cc, ps=ps, xb=XT(): e.matmul(
                                ps[:, cc * 128:(cc + 1) * 128], lhsT=xb[:, c, bi * 128:(bi + 1) * 128], rhs=ident32[:],
                                start=True, stop=True),
                                 reads=[xres[c], R("cn32")], writes=[rps], signal=(cc == 3))
                        if half == 0:
                            act(xtok(bi, 0, 512), ps[:], AF.Copy, [rps], [R("S32", 2 * bi)])
                        else:
                            P.op("dve", lambda e, bi=bi, ps=ps: e.tensor_copy(out=xtok(bi, 512, 1024), in_=ps[:]),
                                 reads=[rps], writes=[R("S32", 2 * bi + 1)])
                    P.dma("act", d_outb[bi], dst[ob * 128:(ob + 1) * 128, :], xtok(bi), reads=xtok_res(bi))

        def ccol(pas, t):
            if pas == "p":
                s, r = t // 256, t % 256
                return s * (256 + 2 * PADC) + PADC + r
            return PADC + t

        def conv_rhs(buf, c, pas, blocks, T, shift):
            t0 = blocks[0] * 128
            if pas == "p" and T == 512:
                base = ccol(pas, t0) + shift
                return buf[:, c, base:base + 2 * (256 + 2 * PADC)].rearrange("p (s w) -> p s w", s=2)[:, :, 0:256]
            base = ccol(pas, t0) + shift
            return buf[:, c, base:base + T]

        def conv_dst(buf, c, pas, blocks, T):
            return conv_rhs(buf, c, pas, blocks, T, 0)

        def as_tile(ap, pas, T):
            if pas == "p" and T == 512:
                return ap.rearrange("p (s w) -> p s w", s=2)
            return ap

        def phase_A(pas, l, blocks, nxt=None):
            T = len(blocks) * 128
            col = 0 if pas == "p" else 1
            if DEBUG_STAGE == 14:
                return
            if DEBUG_STAGE == 12:
                P.op("pool", lambda e: e.memset(ua_c[:, :, 0:4 * (256 + 2 * PADC)], 0.0), writes=[R("ua_c")])
                return
            if pas == "p" and blocks[0] == 0 and DEBUG_STAGE != 13:
                P.op("pool", lambda e: e.memset(ua_c[:, :, 0:4 * (256 + 2 * PADC)], 0.0), writes=[R("ua_c")])
                P.op("pool", lambda e: e.memset(sc_c[:, :, 0:4 * (256 + 2 * PADC)], 0.0), writes=[R("sc_c")])
            xres = load_x("A", pas, l, blocks, T, nxt)
            norm_to_h(l, T, 0, 1, col, xres)
            if DEBUG_STAGE in (2, 21, 22, 23, 24, 211, 212):
                return
            hres = [R("hT", c) for c in range(8)]
            t0 = blocks[0] * 128
            if pas == "s":
                P.dma("act", d_rope, ropeT[:, :, 0:T], rope_d[:, :, t0:t0 + T], writes=[R("ropeT")])
            slabs = [W(l, "A0"), W(l, "A1"), W(l, "A2")]

            def proj(ci):
                w, rw = slabs[ci // 4]
                ps, rps = psum()
                mm_group(ps[:, 0:T], rps, [(w[:, k, (ci % 4) * 128:(ci % 4 + 1) * 128], hT[:, k, 0:T], [rw, hres[k]])
                                           for k in range(8)])
                return ps, rps

            for c in range(2):
                psg, rg = proj(c)
                sig, rs_ = S16[:, 10 + c, 0:T], R("S16", 10 + c)
                act(sig, psg[:, 0:T], AF.Sigmoid, [rg], [rs_])
                psv, rv = proj(2 + c)
                tt("dve", conv_dst(ua_c, c, pas, blocks, T), as_tile(psv[:, 0:T], pas, T), as_tile(sig, pas, T),
                   ALU.mult, [rv, rs_], [R("ua_c")])
            for c in range(2):
                psc, rc = proj(4 + c)
                tmp, rt = S32[:, 8 + c, 0:T], R("S32", 8 + c)
                act(tmp, psc[:, 0:T], AF.Copy, [rc], [rt])
                psh, rh = proj(6 + c)
                tt("dve", conv_dst(sc_c, c, pas, blocks, T), as_tile(psh[:, 0:T], pas, T), as_tile(tmp, pas, T),
                   ALU.mult, [rh, rt], [R("sc_c")])
            psk, rk = proj(8)
            if pas == "p":
                act(kT[:, t0:t0 + T], psk[:, 0:T], AF.Copy, [rk], [R("kT")])
            else:
                pss, rss = proj(9)
                t1, r1 = S32[:, 8, 0:T], R("S32", 8)
                t2, r2 = S32[:, 9, 0:T], R("S32", 9)
                tt("dve", t1, psk[:, 0:T], ropeT[:, 0, 0:T], ALU.mult, [rk, R("ropeT")], [r1])
                tt("dve", t2, pss[:, 0:T], ropeT[:, 1, 0:T], ALU.mult, [rss, R("ropeT")], [r2])
                tt("pool", kT[:, t0:t0 + T], t1, t2, ALU.add, [r1, r2], [R("kT")])
            if DEBUG_STAGE == 3:
                wstate["pos"] += 1
                return
            wt, rwt = W(l, "AT")
            for bi, b in enumerate(blocks):
                ps, rps = psum()
                ncol = 512 if pas == "p" else 384
                mm_group(ps[:, 0:ncol], rps, [(hT[:, k, bi * 128:(bi + 1) * 128], wt[:, k, 0:ncol], [rwt, hres[k]])
                                              for k in range(8)])
                vs = None
                if pas == "s" and b in (0, 1):
                    vs = validL
                if pas == "s" and b in (18, 19):
                    vs = validR
                if vs is None:
                    act(uptok[:, b, :], ps[:, 0:256], AF.Copy, [rps], [R("uptok")])
                    P.op("dve", lambda e, b=b, ps=ps: e.tensor_copy(out=vtok[:, b, :], in_=ps[:, 256:384]),
                         reads=[rps], writes=[R("vtok")])
                else:
                    act(uptok[:, b, :], ps[:, 0:256], AF.Identity, [rps, R("small")], [R("uptok")], scale=vs)
                    act(vtok[:, b, :], ps[:, 256:384], AF.Identity, [rps, R("small")], [R("vtok")], scale=vs)
                    for buf, nm in ((ua_c, "ua_c"), (sc_c, "sc_c")):
                        cs = ccol(pas, b * 128)
                        P.op("pool", lambda e, buf=buf, cs=cs, vs=vs: e.tensor_scalar(
                            out=buf[:, :, cs:cs + 128], in0=buf[:, :, cs:cs + 128], scalar1=vs, scalar2=None,
                            op0=ALU.mult), reads=[R(nm), R("small")], writes=[R(nm)])
                if pas == "p":
                    kb = 0
                    P.op("dve", lambda e, kb=kb, ps=ps: e.tensor_copy(out=kvo[:, kb, :], in_=ps[:, 256:512]),
                         reads=[rps], writes=[R("kvo", kb)])
                    P.dma("act", d_kvs[kb][0], nv_d[l, b * 128:(b + 1) * 128, :], kvo[:, kb, 0:128], reads=[R("kvo", kb)])
                    P.dma("act", d_kvs[kb][1], nk_d[l, b * 128:(b + 1) * 128, :], kvo[:, kb, 128:256], reads=[R("kvo", kb)])

        def phase_B(pas, l, blocks, nxt=None):
            T = len(blocks) * 128
            nb = len(blocks)
            col = 0 if pas == "p" else 1
            t0 = blocks[0] * 128
            xres = load_x("B", pas, l, blocks, T, nxt)
            norm_to_h(l, T, 0, 1, col, xres)
            hres = [R("hT", c) for c in range(8)]
            if pas == "s":
                P.dma("act", d_rope, ropeT[:, :, 0:T], rope_d[:, :, t0:t0 + T], writes=[R("ropeT")])
            slabs = [W(l, "B0"), W(l, "B1"), W(l, "B2")]

            def proj(ci):
                w, rw = slabs[ci // 4]
                ps, rps = psum()
                mm_group(ps[:, 0:T], rps, [(w[:, k, (ci % 4) * 128:(ci % 4 + 1) * 128], hT[:, k, 0:T], [rw, hres[k]])
                                           for k in range(8)])
                return ps, rps

            for c in range(2):
                ps, rps = proj(c)
                act(S32[:, 8 + c, 0:T], ps[:, 0:T], AF.Copy, [rps], [R("S32", 8 + c)])
            for c in range(4):
                psq, rq = proj(2 + c)
                if pas == "p":
                    act(qT[:, c, 0:T], psq[:, 0:T], AF.Copy, [rq], [R("qT", c)])
                else:
                    pss, rss = proj(6 + c)
                    t1, r1 = S32[:, 10, 0:T], R("S32", 10)
                    t2, r2 = S32[:, 11, 0:T], R("S32", 11)
                    tt("dve", t1, psq[:, 0:T], ropeT[:, 0, 0:T], ALU.mult, [rq, R("ropeT")], [r1])
                    tt("dve", t2, pss[:, 0:T], ropeT[:, 1, 0:T], ALU.mult, [rss, R("ropeT")], [r2])
                    tt("pool", qT[:, c, 0:T], t1, t2, ALU.add, [r1, r2], [R("qT", c)])
            uar = []
            for c in range(2):
                ps, rps = psum()
                for k in range(31):
                    di = dg_ctr[0] % 8
                    dg_ctr[0] += 1
                    P.op("dve", lambda e, c=c, k=k, di=di: e.tensor_scalar(
                        out=diagA[:, di, :], in0=ident32[:], scalar1=convw[:, l, c, k:k + 1], scalar2=None,
                        op0=ALU.mult), reads=[R("cn32"), R("convw")], writes=[R("diagA", di)])
                    P.op("pe", lambda e, c=c, k=k, di=di, ps=ps: e.matmul(
                        as_tile(ps[:, 0:T], pas, T), lhsT=diagA[:, di, :],
                        rhs=conv_rhs(ua_c, c, pas, blocks, T, k - 15), start=(k == 0), stop=(k == 30)),
                         reads=[R("diagA", di), R("ua_c")], writes=[rps], signal=True)
                u, ru = S32[:, c, 0:T], R("S32", c)
                act(u, ps[:, 0:T], AF.Identity, [rps, R("small")], [ru], bias=sm(l, 80 + c, 81 + c), scale=1.0)
                ub_, rub = S16[:, 18 + c, 0:T], R("S16", 18 + c)
                P.op("pool", lambda e, u=u, ub_=ub_: e.tensor_copy(out=ub_, in_=u), reads=[ru], writes=[rub])
                uar.append((u, ru, ub_, rub))
            psm, rpm = psum()
            mm_group(psm[:, 0:T], rpm, [(r256, uar[c][2], [uar[c][3], R("cbf")]) for c in range(2)])
            for c in range(2):
                u, ru, ub_, rub = uar[c]
                tt("dve", u, u, psm[:, 0:T], ALU.subtract, [ru, rpm], [ru])
                act(ub_, u, AF.Square, [ru], [rub])
            psv, rpv = psum()
            mm_group(psv[:, 0:T], rpv, [(r256, uar[c][2], [uar[c][3], R("cbf")]) for c in range(2)])
            rs_a0, rra0 = S32[:, 2, 0:T], R("S32", 2)
            rs_a, rra = S32[:, 13, 0:T], R("S32", 13)
            act(rs_a0, psv[:, 0:T], AF.Sqrt, [rpv], [rra0], bias=EPS, scale=1.0)
            P.op("dve", lambda e: e.reciprocal(out=rs_a, in_=rs_a0), reads=[rra0], writes=[rra])
            for c in range(2):
                u, ru, ub_, rub = uar[c]
                tt("dve" if c else "pool", u, u, rs_a, ALU.mult, [ru, rra], [ru])
                act(S16[:, 16 + c, 0:T], u, AF.Silu, [ru, R("small")], [R("S16", 16 + c)],
                    scale=sm(l, 82 + c, 83 + c), bias=sm(l, 84 + c, 85 + c))
            wp, rwp = W(l, "wpool")
            for g in range(4):
                ps, rps = psum()
                for bi, b in enumerate(blocks):
                    items = []
                    if pas == "p":
                        first = (b % 2 == 0)
                        kinds = [(b, 3 if first else 4)]
                        kinds.append((b + 1, 1) if first else (b - 1, 0))
                    else:
                        cur = 5 if b == 2 else (6 if b == 17 else 2)
                        kinds = [(b - 1, 0), (b, cur), (b + 1, 1)]
                    for (j, kind) in kinds:
                        items.append((uptok[:, j, g * 64:(g + 1) * 64], pmat[:, kind * 4 + g, :], [R("uptok"), R("pmat")]))
                    mm_group(ps[0:64, bi * 128:(bi + 1) * 128], rps, items)
                pl, rpl = S16[0:64, 8 + (g % 2), 0:T], R("S16", 8 + (g % 2))
                if g % 2:
                    P.op("dve", lambda e, pl=pl, ps=ps: e.tensor_copy(out=pl, in_=ps[0:64, 0:T]), reads=[rps], writes=[rpl])
                else:
                    act(pl, ps[0:64, 0:T], AF.Copy, [rps], [rpl])
                ps2, rps2 = psum()
                mm_group(ps2[0:64, 0:T], rps2, [(wp[0:64, 0, g * 64:(g + 1) * 64], pl, [rwp, rpl])])
                act(S16[0:64, 20 + g, 0:T], ps2[0:64, 0:T], AF.Identity, [rps2, R("small")], [R("S16", 20 + g)],
                    scale=sm(l, 86 + g, 87 + g)[0:64, :])
            for c in range(2):
                ps, rps = psum()
                mm_group(as_tile(ps[:, 0:T], pas, T), rps,
                         [(diagC[:, c, k, :], conv_rhs(sc_c, c, pas, blocks, T, k - 1), [R("diagC"), R("sc_c")])
                          for k in range(3)])
                tt("dve", S16[:, 18 + c, 0:T], ps[:, 0:T], S32[:, 8 + c, 0:T], ALU.mult,
                   [rps, R("S32", 8 + c)], [R("S16", 18 + c)])
            jobs = []
            for bi, b in enumerate(blocks):
                if pas == "p":
                    s0 = (b // 2) * 2
                    keys = [("l", s0, None, 0), ("l", s0 + 1, None, 0)]
                else:
                    keys = []
                    for j, mk in ((b - 1, maskP), (b, None), (b + 1, maskN)):
                        oi = 1 if j in (0, 1) else (2 if j in (18, 19) else 0)
                        keys.append(("l", j, mk, oi))
                    keys += [("c", 0, None, 0), ("c", 1, None, 0)]
                for g in range(2):
                    for ki, ks in enumerate(keys):
                        jobs.append((bi, g, ki, len(keys), ks))

            def kv_of(job):
                bi, g, ki, nk, (kind, j, mk, oi) = job
                if kind == "l":
                    return (kT[g * 64:(g + 1) * 64, j * 128:(j + 1) * 128], R("kT"),
                            vtok[:, j, g * 64:(g + 1) * 64], R("vtok"))
                return (kcT[g * 64:(g + 1) * 64, l, j * 128:(j + 1) * 128], R("kc"),
                        vcs[:, l, j, g * 64:(g + 1) * 64], R("vc"))

            def issue_S(job):
                bi, g, ki, nk, ks = job
                klhs, kres, _, _ = kv_of(job)
                pS, rpS = psum(0, 4)
                qr = qT[g * 64:(g + 1) * 64, :, bi * 128:(bi + 1) * 128]
                mm_group(pS[:].rearrange("p (h q) -> p h q", h=4), rpS,
                         [(klhs, qr, [kres] + [R("qT", c) for c in range(4)])])
                return pS, rpS

            LA = 2
            pending = []
            inflight = {}
            for ji in range(min(LA, len(jobs))):
                inflight[ji] = issue_S(jobs[ji])
            acc = None
            for ji, job in enumerate(jobs):
                bi, g, ki, nk, (kind, j, mk, oi) = job
                if ki == 0:
                    acc = (psum(4, 6), psum(6, 8))
                (po, rpo), (pd, rpd) = acc
                pS, rpS = inflight.pop(ji)
                _, _, vlhs, vres = kv_of(job)
                es = 12 + (ji % 4)
                E, rE = S16[:, es, :], R("S16", es)
                act(E, pS[:], AF.Exp, [rpS], [rE], scale=0.125)
                if mk is not None:
                    E3 = E.rearrange("p (h q) -> p h q", h=4)
                    tt("pool", E3, E3, mk.unsqueeze(1).to_broadcast([128, 4, 128]), ALU.mult, [rE, R("cbf")], [rE])
                if ji + LA < len(jobs):
                    inflight[ji + LA] = issue_S(jobs[ji + LA])
                P.op("pe", lambda e, vlhs=vlhs, E=E, ki=ki, po=po, nk=nk: e.matmul(
                    po[0:64, :], lhsT=vlhs, rhs=E, start=(ki == 0), stop=(ki == nk - 1)),
                     reads=[vres, rE], writes=[rpo], signal=False)
                P.op("pe", lambda e, oi=oi, E=E, ki=ki, pd=pd, nk=nk: e.matmul(
                    pd[0:64, :], lhsT=ones64[oi], rhs=E, start=(ki == 0), stop=(ki == nk - 1)),
                     reads=[R("cbf"), rE], writes=[rpd, rpo], signal=True)
                while pending and pending[0][0] <= ji:
                    pending.pop(0)[1]()
                if ki == nk - 1:
                  def normalize(bi=bi, g=g, po=po, rpo=rpo, pd=pd, rpd=rpd):
                    den, rden = S32[0:64, 3 + g, :], R("S32", 3 + g)
                    rec, rrec = S32[0:64, 5 + g, :], R("S32", 5 + g)
                    tt("dve", den.rearrange("p (h q) -> p h q", h=4), pd[0:64, :].rearrange("p (h q) -> p h q", h=4),
                       sinkf[:, g * 4:(g + 1) * 4].unsqueeze(2).to_broadcast([64, 4, 128]), ALU.add,
                       [rpd, R("sinkf")], [rden])
                    act(den, den, AF.Ln, [rden], [rden])
                    act(rec, den, AF.Exp, [rden], [rrec], scale=-1.0)
                    for hh in range(4):
                        slot = 24 + g * 4 + hh
                        tt("dve", S16[0:64, slot, bi * 128:(bi + 1) * 128], po[0:64, hh * 128:(hh + 1) * 128],
                           rec[:, hh * 128:(hh + 1) * 128], ALU.mult, [rpo, rrec], [R("S16", slot)])
                  pending.append((ji + 2, normalize))
            while pending:
                pending.pop(0)[1]()
            for c in range(8):
                wg, rwg = W(l, f"g{c}")
                wb, rwb = W(l, f"br{c}")
                gsl = []
                for b_ in range(4):
                    ps, rps = psum()
                    mm_group(ps[:, 0:T], rps, [(wg[:, k, b_ * 128:(b_ + 1) * 128], hT[:, k, 0:T], [rwg, hres[k]])
                                               for k in range(8)])
                    gs = 8 + (c % 2) * 4 + b_
                    act(S16[:, gs, 0:T], ps[:, 0:T], AF.Sigmoid, [rps], [R("S16", gs)])
                    gsl.append(gs)
                brs = []
                ps, rps = psum()
                mm_group(ps[:, 0:T], rps, [(wb[:, 0, k * 128:(k + 1) * 128], S16[:, 16 + k, 0:T], [rwb, R("S16", 16 + k)])
                                           for k in range(2)])
                brs.append((ps, rps))
                ps, rps = psum()
                mm_group(ps[:, 0:T], rps, [(wb[0:64, 0, 256 + g * 128:256 + (g + 1) * 128], S16[0:64, 20 + g, 0:T],
                                            [rwb, R("S16", 20 + g)]) for g in range(4)])
                brs.append((ps, rps))
                ps, rps = psum()
                mm_group(ps[:, 0:T], rps, [(wb[:, 0, 768 + k * 128:768 + (k + 1) * 128], S16[:, 18 + k, 0:T],
                                            [rwb, R("S16", 18 + k)]) for k in range(2)])
                brs.append((ps, rps))
                ps, rps = psum()
                mm_group(ps[:, 0:T], rps, [(wb[0:64, 0, 1024 + h * 128:1024 + (h + 1) * 128], S16[0:64, 24 + h, 0:T],
                                            [rwb, R("S16", 24 + h)]) for h in range(8)])
                brs.append((ps, rps))
                acc, racc = S32[:, 5 + (c % 2), 0:T], R("S32", 5 + (c % 2))
                tmp, rtmp = S32[:, 7, 0:T], R("S32", 7)
                tt("dve", acc, brs[0][0][:, 0:T], S16[:, gsl[0], 0:T], ALU.mult, [brs[0][1], R("S16", gsl[0])], [racc])
                for b_ in range(1, 4):
                    tt("dve", tmp, brs[b_][0][:, 0:T], S16[:, gsl[b_], 0:T], ALU.mult,
                       [brs[b_][1], R("S16", gsl[b_])], [rtmp])
                    if b_ < 3:
                        tt("pool", acc, acc, tmp, ALU.add, [racc, rtmp], [racc])
                    else:
                        tt("pool", S16[:, c, 0:T], acc, tmp, ALU.add, [racc, rtmp], [R("S16", c)])
            wos = [W(l, "wo0"), W(l, "wo1")]
            mres = [R("S32", c) for c in range(8)]
            for c in range(8):
                w, rw = wos[c // 4]
                ps, rps = psum()
                mm_group(ps[:, 0:T], rps, [(w[:, k, (c % 4) * 128:(c % 4 + 1) * 128], S16[:, k, 0:T], [rw, R("S16", k)])
                                           for k in range(8)])
                act(S32[:, c, 0:T], ps[:, 0:T], AF.Copy, [rps], [mres[c]])
            post_norm_residual(l, T, 2, col, mres, xres)
            norm_to_h(l, T, 3, 4, col, xres)
            for i in range(8):
                w, rw = W(l, f"f1_{i}")
                for ci in range(4):
                    j = i * 4 + ci
                    ps, rps = psum()
                    mm_group(ps[:, 0:T], rps, [(w[:, k, ci * 128:(ci + 1) * 128], hT[:, k, 0:T], [rw, hres[k]])
                                               for k in range(8)])
                    tmp, rtmp = S32[:, 8 + (j % 4), 0:T], R("S32", 8 + (j % 4))
                    act(tmp, ps[:, 0:T], AF.Relu, [rps], [rtmp])
                    tt("dve" if j % 2 else "pool", S16[:, j, 0:T], tmp, tmp, ALU.mult, [rtmp], [R("S16", j)])
            for c in range(8):
                w, rw = W(l, f"f2_{c}")
                ps, rps = psum()
                mm_group(ps[:, 0:T], rps, [(w[:, k, :], S16[:, k, 0:T], [rw, R("S16", k)]) for k in range(32)])
                act(S32[:, c, 0:T], ps[:, 0:T], AF.Copy, [rps], [mres[c]])
            post_norm_residual(l, T, 5, col, mres, xres)
            store_x(pas, l, blocks, T, xres)

        sched_all = schedule(do_sample)[:limit]
        for si, (kind, pas, l, blocks) in enumerate(sched_all):
            if kind == "setup" and l == 1:
                while l1_cast:
                    issue_cast(1, l1_cast.pop(0))
            if l == 0 and si >= 4 and l1_cast:
                issue_cast(1, l1_cast.pop(0))
            if kind == "setup":
                layer_setup(l)
            else:
                nxt = None
                for it in sched_all[si + 1:]:
                    if it[0] != "setup":
                        nxt = it
                        break
                (phase_A if kind == "A" else phase_B)(pas, l, blocks, nxt)
        assert DEBUG_STAGE or wstate["pos"] == len(seq), (wstate["pos"], len(seq))
        P.wait_all("act", d_outb + d_kvs[0] + [d_sc, d_x, d_rope] + d_xf + d_scs + d_xb + d_cs)
        P.wait_all("sp", d_ring + d_cast)
        P.wait_all("pool", d_cast + d_cs)
        print("sbuf bytes remaining", nc.sbuf_bytes_remaining)
        print("instructions", P.n_inst, "waits", P.n_wait)
        P.emit()
    return nc


def tiles_of(blocks):
    return [blocks[i:i + 4] for i in range(0, len(blocks), 4)]


def schedule(do_sample):
    out = []
    for l in range(L):
        out.append(("setup", None, l, None))
        for t in tiles_of(list(range(8))):
            out.append(("A", "p", l, t))
        for t in tiles_of(list(range(8))):
            out.append(("B", "p", l, t))
        if do_sample:
            lo, hi = (0, 20) if l == 0 else (1, 19)
            for t in tiles_of(list(range(lo, hi))):
                out.append(("A", "s", l, t))
            lo, hi = (1, 19) if l == 0 else (2, 18)
            for t in tiles_of(list(range(lo, hi))):
                out.append(("B", "s", l, t))
    return out


def slab_sequence(do_sample, limit=None):
    seq = []
    for (kind, pas, l, blocks) in schedule(do_sample)[:limit]:
        if kind == "setup":
            seq += [(l, f"ada{i}") for i in range(12)]
        elif kind == "A":
            seq += [(l, "A0"), (l, "A1"), (l, "A2"), (l, "AT")]
        else:
            seq += [(l, "B0"), (l, "B1"), (l, "B2"), (l, "wpool")]
            for c in range(8):
                seq += [(l, f"g{c}"), (l, f"br{c}")]
            seq += [(l, "wo0"), (l, "wo1")]
            seq += [(l, f"f1_{i}") for i in range(8)]
            seq += [(l, f"f2_{c}") for c in range(8)]
    return seq


_NC_CACHE = {}


def host_inputs(inp, core):
    b, half = core // 2, core % 2
    start = half * 2048
    m = {}
    m["xp"] = np.ascontiguousarray(inp["x_prompt"][4 * core:4 * core + 4].reshape(1024, D))
    xs = np.zeros((WIN, D), np.float32)
    lo, hi = start - 256, start + 2304
    slo, shi = max(lo, 0), min(hi, 4096)
    xs[slo - lo:shi - lo] = inp["x_sample"][b, slo:shi]
    m["xs"] = xs
    validL = 1.0 if lo >= 0 else 0.0
    validR = 1.0 if hi <= 4096 else 0.0
    cn = np.zeros((128, 3 * 128 + 3 * 64 + 256), np.float32)
    cn[:, 0:128] = np.eye(128)
    kk = np.arange(128)[:, None]
    qq = np.arange(128)[None, :]
    cn[:, 128:256] = (kk >= qq)
    cn[:, 256:384] = (kk <= qq)
    cn[:, 384:448] = 1.0
    cn[:, 448:512] = validL
    cn[:, 512:576] = validR
    cn[:, 576:704] = 1.0 / 1024
    cn[:, 704:832] = 1.0 / 256
    m["cnst"] = cn
    PM = pool_mats()
    pm = np.zeros((7, 4, 128, 128), np.float32)
    pm[0:5] = PM
    pm[5] = PM[3] if half == 0 else PM[2]
    pm[6] = PM[4] if half == 1 else PM[2]
    m["pmat"] = np.ascontiguousarray(pm.reshape(28, 128, 128).transpose(1, 0, 2).reshape(128, 28 * 128))
    sm = np.zeros((128, 16 + L * 128), np.float32)
    cond = np.stack([inp["c_ctx"], inp["c"][b]], axis=1)
    sm[:, 0:16] = cond.reshape(8, 128, 2).transpose(1, 0, 2).reshape(128, 16)
    for l in range(L):
        o = 16 + l * 128
        sm[:, o:o + 48] = inp["b_ada"][l].reshape(48, 128).T
        for i, nm in enumerate(("g_pre_mix", "g_post_mix", "g_pre_ffn", "g_post_ffn")):
            sm[:, o + 48 + 8 * i:o + 56 + 8 * i] = inp[nm][l].reshape(8, 128).T
        sm[:, o + 80:o + 82] = inp["b_conv_a"][l].reshape(2, 128).T
        sm[:, o + 82:o + 84] = inp["ln_g_a"][l].reshape(2, 128).T
        sm[:, o + 84:o + 86] = inp["ln_b_a"][l].reshape(2, 128).T
        sm[0:64, o + 86:o + 90] = inp["pool_scale"][l].reshape(4, 64).T
        sm[:, o + 90:o + 98] = inp["sink"][l][None, :]
    sm[:, 16 + 98] = validL
    sm[:, 16 + 99] = validR
    m["small"] = sm
    cw = np.zeros((128, L, 2, 34), np.float32)
    for l in range(L):
        cw[:, l, :, 0:31] = inp["w_conv_a"][l].reshape(31, 2, 128).transpose(2, 1, 0)
        cw[:, l, :, 31:34] = inp["w_sc"][l].reshape(3, 2, 128).transpose(2, 1, 0)
    m["convw"] = cw
    pos = np.arange(lo, hi)
    posc = np.clip(pos, 0, 4095)
    row, colp = posc // 64, posc % 64
    p = np.arange(128)
    d = p % 64
    r = d % 32
    fi = (r % 16).astype(np.float64)
    freq = 10000.0 ** (-fi / 16.0)
    psel = np.where((d < 32)[:, None], row[None, :], colp[None, :]).astype(np.float64)
    ang = psel * freq[:, None]
    sgn = np.where(r < 16, -1.0, 1.0)[:, None]
    rp = np.zeros((128, 2, WIN), np.float32)
    rp[:, 0] = np.cos(ang)
    rp[:, 1] = np.sin(ang) * sgn
    m["rope"] = rp
    m["kcT"] = np.ascontiguousarray(inp["cache_k"][b].reshape(L, 256, 128).transpose(2, 0, 1))
    m["vc"] = np.ascontiguousarray(inp["cache_v"][b].reshape(L, 2, 128, 128).transpose(2, 0, 1, 3))
    return m


def kernel(**inputs):
    inp = {k: np.asarray(v) for k, v in inputs.items()}
    if "nc" not in _NC_CACHE:
        _NC_CACHE["nc"] = build_program(True)
    nc = _NC_CACHE["nc"]
    packs = [host_pack_layer(inp, l) for l in range(L)]
    in_maps = []
    for core in range(8):
        m = host_inputs(inp, core)
        for l in range(L):
            m[f"wpk{l}"] = packs[l]
        in_maps.append(m)
    res = run_bass_kernel_spmd(nc, in_maps, core_ids=list(range(8)))
    rs = res.results
    yp = np.concatenate([rs[c]["yp"].reshape(4, 256, D) for c in range(8)], axis=0)
    ys = np.stack([np.concatenate([rs[2 * b]["ys"], rs[2 * b + 1]["ys"]], axis=0) for b in range(4)], axis=0)
    nk = np.concatenate([rs[c]["nk"].reshape(L, 4, 256, 2, 64).transpose(1, 0, 2, 3, 4) for c in range(8)], axis=0)
    nv = np.concatenate([rs[c]["nv"].reshape(L, 4, 256, 2, 64).transpose(1, 0, 2, 3, 4) for c in range(8)], axis=0)
    return (yp.astype(np.float32), ys.astype(np.float32), nk.astype(np.float32), nv.astype(np.float32))
```

```python
from contextlib import ExitStack
import numpy as np
import concourse.bass as bass
import concourse.mybir as mybir
from concourse.bass_utils import run_bass_kernel_spmd

F32 = mybir.dt.float32
BF16 = mybir.dt.bfloat16
AF = mybir.ActivationFunctionType
ALU = mybir.AluOpType

ENGS = ("pe", "act", "dve", "pool", "sp")
DEBUG_STAGE = 0
D = 1024
L = 2
NB_S = 20
WIN = NB_S * 128
PADC = 16
EPS = 1e-6
SLAB = 4096


class Sem:
    __slots__ = ("h", "count", "name")

    def __init__(self, h, name):
        self.h = h
        self.count = 0
        self.name = name


class Res:
    __slots__ = ("name", "w", "r", "excl")

    def __init__(self, name=""):
        self.name = name
        self.w = None
        self.r = {}
        self.excl = False


class Prog:
    def __init__(self, nc, stack):
        self.nc = nc
        self.stack = stack
        self.streams = {e: [] for e in ENGS}
        self.esem = {e: self.new_sem("eng_" + e) for e in ENGS}
        self.waited = {e: {} for e in ENGS}
        self.n_inst = 0
        self.n_wait = 0
        self._res = {}

    def new_sem(self, name):
        return Sem(self.stack.enter_context(self.nc.semaphore(name)), name)

    def R(self, *key):
        r = self._res.get(key)
        if r is None:
            r = Res(str(key))
            self._res[key] = r
        return r

    def _collect(self, eng, reads, writes):
        deps = {}

        def add(t, kind):
            if t is None:
                return
            s, v, e = t
            if e == eng and eng == "pe":
                return
            if deps.get(s, 0) < v:
                deps[s] = v

        for r in reads:
            add(r.w, "raw")
            if r.excl:
                for s, (v, e) in r.r.items():
                    if e != eng:
                        add((s, v, e), "rar")
        for w in writes:
            add(w.w, "waw")
            for s, (v, e) in w.r.items():
                add((s, v, e), "war")
        waits = []
        wd = self.waited[eng]
        for s, v in deps.items():
            if wd.get(s, 0) < v:
                wd[s] = v
                waits.append((s.h, v))
        return waits

    def _mark(self, ticket, reads, writes):
        s, v, e = ticket
        for r in reads:
            cur = r.r.get(s)
            if cur is None or cur[0] < v:
                r.r[s] = (v, e)
        for w in writes:
            w.w = ticket
            w.r = {}

    def op(self, eng, fn, reads=(), writes=(), signal=True):
        waits = self._collect(eng, reads, writes)
        sem = self.esem[eng]
        ticket = (sem, sem.count + 1, eng)
        if signal:
            sem.count += 1
        self.streams[eng].append((waits, fn, (sem.h, 1) if signal else None))
        self._mark(ticket, reads, writes)
        self.n_inst += 1
        self.n_wait += len(waits)
        return ticket

    def dma(self, eng, dsem, out_ap, in_ap, reads=(), writes=(), **kw):
        waits = self._collect(eng, reads, writes)
        dsem.count += 16
        ticket = (dsem, dsem.count, "dma")

        def fn(e, out_ap=out_ap, in_ap=in_ap):
            return e.dma_start(out=out_ap, in_=in_ap, **kw)

        self.streams[eng].append((waits, fn, (dsem.h, 16)))
        self._mark(ticket, reads, writes)
        self.n_inst += 1
        self.n_wait += len(waits)
        return ticket

    def wait_all(self, eng, sems):
        waits = [(s.h, s.count) for s in sems if s.count > 0]
        self.streams[eng].append((waits, None, None))

    def emit(self):
        streams = self.streams

        def run(e_obj, lst):
            for waits, fn, inc in lst:
                for (h, v) in waits:
                    e_obj.wait_ge(h, v)
                if fn is None:
                    continue
                ins = fn(e_obj)
                if inc is not None:
                    ins.then_inc(inc[0], inc[1])

        with self.nc.Block() as block:
            @block.tensor
            def _(t):
                run(t, streams["pe"])

            @block.scalar
            def _(s):
                run(s, streams["act"])

            @block.vector
            def _(v):
                run(v, streams["dve"])

            @block.gpsimd
            def _(g):
                run(g, streams["pool"])

            @block.sync
            def _(sp):
                run(sp, streams["sp"])


def pack_spec():
    spec = []
    for i in range(12):
        spec.append((f"ada{i}", 8, 512))
    spec += [("A0", 8, 512), ("A1", 8, 512), ("A2", 8, 256), ("AT", 8, 512)]
    spec += [("B0", 8, 512), ("B1", 8, 512), ("B2", 8, 256)]
    spec.append(("wpool", 1, 256))
    for c in range(8):
        spec.append((f"g{c}", 8, 512))
        spec.append((f"br{c}", 1, 2048))
    spec += [("wo0", 8, 512), ("wo1", 8, 512)]
    for i in range(8):
        spec.append((f"f1_{i}", 8, 512))
    for c in range(8):
        spec.append((f"f2_{c}", 32, 128))
    offs = {}
    o = 0
    for name, kc, n in spec:
        offs[name] = (o, kc, n)
        o += kc * n
    return spec, offs, o


PACK_SPEC, PACK_OFFS, PACK_EL = pack_spec()


def rope_swap_idx():
    idx = np.zeros(64, np.int64)
    for d in range(64):
        blk, r = d // 32, d % 32
        partner = r + 16 if r < 16 else r - 16
        idx[d] = blk * 32 + partner
    return idx


def host_pack_layer(inp, l):
    w_in = inp["w_in"][l]
    out = np.zeros((128, PACK_EL), np.float32)

    def put(name, arr):
        o, kc, n = PACK_OFFS[name]
        assert arr.shape == (128, kc, n), (name, arr.shape)
        out[:, o:o + kc * n] = arr.reshape(128, kc * n)

    def kmaj(w):
        K, n = w.shape
        return np.ascontiguousarray(w.reshape(K // 128, 128, n).transpose(1, 0, 2))

    wada = inp["w_ada"][l]
    for i in range(12):
        put(f"ada{i}", kmaj(wada[:, i * 512:(i + 1) * 512]))
    sw = rope_swap_idx()
    kcols = np.arange(2048, 2176)
    kscols = np.concatenate([2048 + g * 64 + sw for g in range(2)])
    colsA = np.concatenate([np.arange(256, 512), np.arange(0, 256), np.arange(1024, 1280),
                            np.arange(1280, 1536), kcols, kscols])
    WA = w_in[:, colsA]
    put("A0", kmaj(WA[:, 0:512])); put("A1", kmaj(WA[:, 512:1024])); put("A2", kmaj(WA[:, 1024:1280]))
    colsT = np.concatenate([np.arange(512, 768), np.arange(2176, 2304), np.arange(2048, 2176)])
    put("AT", kmaj(w_in[:, colsT]))
    qcols = np.concatenate([np.concatenate([1536 + i * 64 + np.arange(64), 1536 + (4 + i) * 64 + np.arange(64)])
                            for i in range(4)])
    qscols = np.concatenate([np.concatenate([1536 + i * 64 + sw, 1536 + (4 + i) * 64 + sw]) for i in range(4)])
    colsB = np.concatenate([np.arange(768, 1024), qcols, qscols])
    WB = w_in[:, colsB]
    put("B0", kmaj(WB[:, 0:512])); put("B1", kmaj(WB[:, 512:1024])); put("B2", kmaj(WB[:, 1024:1280]))
    wp = np.zeros((128, 1, 256), np.float32)
    for g in range(4):
        wp[0:64, 0, g * 64:(g + 1) * 64] = inp["w_pool"][l, g]
    put("wpool", wp)
    for c in range(8):
        gcols = np.concatenate([2304 + b * 1024 + c * 128 + np.arange(128) for b in range(4)])
        put(f"g{c}", kmaj(w_in[:, gcols]))
        br = np.zeros((128, 1, 2048), np.float32)
        cs = slice(c * 128, (c + 1) * 128)
        for k in range(2):
            br[:, 0, k * 128:(k + 1) * 128] = inp["w_a_out"][l, k * 128:(k + 1) * 128, cs]
            br[:, 0, 768 + k * 128:768 + (k + 1) * 128] = inp["w_c_out"][l, k * 128:(k + 1) * 128, cs]
        for g in range(4):
            br[0:64, 0, 256 + g * 128:256 + (g + 1) * 128] = inp["w_b_out"][l, g * 64:(g + 1) * 64, cs]
        for h in range(8):
            br[0:64, 0, 1024 + h * 128:1024 + (h + 1) * 128] = inp["w_d_out"][l, h * 64:(h + 1) * 64, cs]
        put(f"br{c}", br)
    wo = inp["w_o"][l]
    put("wo0", kmaj(wo[:, 0:512])); put("wo1", kmaj(wo[:, 512:1024]))
    w1 = inp["w_ff1"][l]
    for i in range(8):
        put(f"f1_{i}", kmaj(w1[:, i * 512:(i + 1) * 512]))
    w2 = inp["w_ff2"][l]
    for c in range(8):
        put(f"f2_{c}", kmaj(w2[:, c * 128:(c + 1) * 128]))
    return out


def pool_mats():
    sizes = (2, 4, 8, 16)
    M = np.zeros((5, 4, 128, 128), np.float64)
    for g, w in enumerate(sizes):
        hw = w // 2
        for t in range(128):
            for kind in range(5):
                for tp_rel in range(t - hw, t + hw):
                    if kind in (0, 1, 2):
                        cnt = w
                    elif kind == 3:
                        cnt = (t + hw) - max(t - hw, 0)
                    else:
                        cnt = min(t + hw, 128) - (t - hw)
                    if kind == 0 and tp_rel < 0:
                        M[kind, g, tp_rel + 128, t] += 1.0 / cnt
                    elif kind == 1 and tp_rel >= 128:
                        M[kind, g, tp_rel - 128, t] += 1.0 / cnt
                    elif kind >= 2 and 0 <= tp_rel < 128:
                        M[kind, g, tp_rel, t] += 1.0 / cnt
                if kind >= 2:
                    M[kind, g, t, t] -= 1.0
    return M.astype(np.float32)


def build_program(do_sample=True, limit=None):
    nc = bass.Bass("TRN2", target_bir_lowering=False)
    dt_in = lambda name, shape: nc.dram_tensor(name, list(shape), F32, kind="ExternalInput").ap()
    xp_d = dt_in("xp", [1024, D])
    xs_d = dt_in("xs", [WIN, D])
    wpk_d = [dt_in(f"wpk{l}", [128, PACK_EL]) for l in range(L)]
    cnst_d = dt_in("cnst", [128, 3 * 128 + 3 * 64 + 128 * 2])
    pmat_d = dt_in("pmat", [128, 7 * 4 * 128])
    small_d = dt_in("small", [128, 16 + L * 128])
    rope_d = dt_in("rope", [128, 2, WIN])
    kc_d = dt_in("kcT", [128, L, 256])
    vc_d = dt_in("vc", [128, L, 2, 128])
    yp_d = nc.dram_tensor("yp", [1024, D], F32, kind="ExternalOutput").ap()
    ys_d = nc.dram_tensor("ys", [2048, D], F32, kind="ExternalOutput").ap()
    nk_d = nc.dram_tensor("nk", [L, 1024, 128], F32, kind="ExternalOutput").ap()
    nv_d = nc.dram_tensor("nv", [L, 1024, 128], F32, kind="ExternalOutput").ap()
    wbf_d = [nc.dram_tensor(f"wbf{l}", [128, PACK_EL], BF16).ap() for l in range(L)]
    xsc_d = {"p": nc.dram_tensor("xscp", [128, 8, 1024], F32).ap(),
             "s": nc.dram_tensor("xscs", [128, 8, WIN], F32).ap()}

    with ExitStack() as st:
        P = Prog(nc, st)
        R = P.R

        def sb(name, shape, dt):
            return st.enter_context(nc.sbuf_tensor(name, list(shape), dt))

        ring_n = 5
        ring = sb("ring", [128, ring_n, SLAB], BF16)
        xTs = [sb("xTa", [128, 8, 512], F32), sb("xTb", [128, 8, 512], F32)]
        XS = {"i": 0, "cur": 0, "pref": None, "early": None}
        hT = sb("hT", [128, 8, 512], BF16)
        S16 = sb("S16", [128, 32, 512], BF16)
        S32 = sb("S32", [128, 14, 512], F32)

        def xtok(bi, lo=0, hi=D):
            return S32[:, 2 * bi:2 * bi + 2, :].rearrange("p a b -> p (a b)")[:, lo:hi]

        def xtok_res(bi):
            return [R("S32", 2 * bi), R("S32", 2 * bi + 1)]
        ua_c = sb("ua_c", [128, 2, WIN + 2 * PADC + 64], BF16)
        sc_c = sb("sc_c", [128, 2, WIN + 2 * PADC + 64], BF16)
        kT = sb("kT", [128, WIN], BF16)
        vtok = sb("vtok", [128, NB_S, 128], BF16)
        uptok = sb("uptok", [128, NB_S, 256], BF16)
        qT = sb("qT", [128, 4, 512], BF16)
        kvo = sb("kvo", [128, 1, 256], F32)
        ropeT = sb("ropeT", [128, 2, 512], F32)
        kcT = sb("kcTs", [128, L, 256], BF16)
        vcs = sb("vcs", [128, L, 2, 128], BF16)
        cn32 = sb("cn32", [128, 128], F32)
        cbf = sb("cbf", [128, 3 * 128 + 3 * 64 + 256], BF16)
        pmat = sb("pmat_sb", [128, 28, 128], BF16)
        small = sb("small_sb", [128, 16 + L * 128], F32)
        condb = sb("condb", [128, 8, 2], BF16)
        modv = sb("modv", [128, L, 48, 2], F32)
        dvec = sb("dvec", [128, L, 6, 8, 2], F32)
        diagA = sb("diagA", [128, 8, 128], BF16)
        diagC = sb("diagC", [128, 2, 3, 128], BF16)
        sinkf = sb("sinkf", [64, 8], F32)
        psb = [st.enter_context(nc.psum_tensor(f"ps{i}", [128, 512], F32)) for i in range(8)]

        ident32 = cn32
        identb = cbf[:, 0:128]
        maskP = cbf[:, 128:256]
        maskN = cbf[:, 256:384]
        ones64 = [cbf[:, 384 + i * 64:384 + (i + 1) * 64] for i in range(3)]
        r1024 = cbf[:, 576:704]
        r256 = cbf[:, 704:832]

        NCH = 8
        d_cast = [P.new_sem(f"d_cast{l}_{i}") for l in range(L) for i in range(NCH)]
        d_ring = [P.new_sem(f"d_ring{i}") for i in range(ring_n)]
        d_x = P.new_sem("d_x")
        d_xf = [P.new_sem("d_xf0"), P.new_sem("d_xf1")]
        d_scs = [P.new_sem("d_sc0"), P.new_sem("d_sc1")]
        d_xb = [P.new_sem(f"d_xb{i}") for i in range(4)]
        d_cs = [P.new_sem(f"d_c{i}") for i in range(8)]
        d_rope = P.new_sem("d_rope")
        d_sc = P.new_sem("d_sc")
        d_outb = [P.new_sem(f"d_out{i}") for i in range(4)]
        d_kvs = [[P.new_sem(f"d_kv{i}{j}") for j in range(2)] for i in range(1)]

        ps_ctr = {}
        dg_ctr = [0]

        def psum(lo=0, hi=8):
            k = (lo, hi)
            n = ps_ctr.get(k, 0)
            ps_ctr[k] = n + 1
            i = lo + (n % (hi - lo))
            r = R("ps", i)
            r.excl = True
            return psb[i], r

        ncn = 3 * 128 + 3 * 64 + 256
        P.dma("pool", d_cs[0], cbf[:], cnst_d[:, 0:ncn], writes=[R("cbf")])
        P.dma("act", d_cs[1], cn32[:], cnst_d[:, 0:128], writes=[R("cn32")])
        P.dma("pool", d_cs[2], pmat[:], pmat_d.rearrange("p (a b) -> p a b", b=128), writes=[R("pmat")])
        P.dma("act", d_cs[3], small[:], small_d, writes=[R("small")])
        P.dma("pool", d_cs[4], kcT[:], kc_d, writes=[R("kc")])
        P.dma("pool", d_cs[5], vcs[:], vc_d, writes=[R("vc")])
        P.op("pool", lambda e: e.memset(ua_c[:], 0.0), writes=[R("ua_c")])
        P.op("pool", lambda e: e.memset(sc_c[:], 0.0), writes=[R("sc_c")])
        ch = PACK_EL // NCH

        def issue_cast(l, i):
            hi_ = PACK_EL if i == NCH - 1 else (i + 1) * ch
            P.dma("pool", d_cast[l * NCH + i], wbf_d[l][:, i * ch:hi_], wpk_d[l][:, i * ch:hi_],
                  writes=[R("wbf", l, i)], max_dma_last_dim=16384)

        for i in range(NCH):
            issue_cast(0, i)
        l1_cast = list(range(NCH))

        def wbf_res(l, o, n):
            lo_i = min(NCH - 1, o // ch)
            hi_i = min(NCH - 1, (o + n - 1) // ch)
            return [R("wbf", l, i) for i in range(lo_i, hi_i + 1)]

        def sm(l, a, b):
            return small[:, 16 + l * 128 + a:16 + l * 128 + b]

        validL = small[:, 16 + 98:16 + 99]
        validR = small[:, 16 + 99:16 + 100]
        P.op("act", lambda e: e.activation(out=condb[:], in_=small[:, 0:16].rearrange("p (c t) -> p c t", t=2),
                                           func=AF.Silu), reads=[R("small")], writes=[R("condb")])

        seq = slab_sequence(do_sample, limit)
        wstate = {"next_load": 0, "pos": 0}

        def ensure_loaded(upto):
            while wstate["next_load"] <= upto and wstate["next_load"] < len(seq):
                s = wstate["next_load"]
                l, name = seq[s]
                o, kc, n = PACK_OFFS[name]
                slot = s % ring_n
                P.dma("sp", d_ring[slot], ring[:, slot, 0:kc * n], wbf_d[l][:, o:o + kc * n],
                      reads=wbf_res(l, o, kc * n), writes=[R("ring", slot)])
                wstate["next_load"] += 1

        def W(l, name):
            s = wstate["pos"]
            assert seq[s] == (l, name), (s, seq[s], l, name)
            wstate["pos"] += 1
            ensure_loaded(s + ring_n - 3)
            o, kc, n = PACK_OFFS[name]
            slot = s % ring_n
            return ring[:, slot, 0:kc * n].rearrange("p (k n) -> p k n", n=n), R("ring", slot)

        def mm_group(out_ap, out_res, items, extra_reads=(), signal_each=False):
            n = len(items)
            for i, (lhsT, rhs, rs) in enumerate(items):
                P.op("pe", lambda e, lhsT=lhsT, rhs=rhs, i=i: e.matmul(out_ap, lhsT=lhsT, rhs=rhs,
                                                                         start=(i == 0), stop=(i == n - 1)),
                     reads=list(rs) + list(extra_reads), writes=[out_res], signal=(signal_each or i == n - 1))

        def act(out, in_, func, reads, writes, **kw):
            P.op("act", lambda e: e.activation(out=out, in_=in_, func=func, **kw), reads=reads, writes=writes)

        def tt(eng, out, in0, in1, op, reads, writes):
            P.op(eng, lambda e: e.tensor_tensor(out=out, in0=in0, in1=in1, op=op), reads=reads, writes=writes)

        def stt(eng, out, in0, scalar, in1, op0, op1, reads, writes):
            P.op(eng, lambda e: e.scalar_tensor_tensor(out=out, in0=in0, scalar=scalar, in1=in1, op0=op0, op1=op1),
                 reads=reads, writes=writes)

        def layer_setup(l):
            ps, rps = psum()
            for i in range(12):
                w, rw = W(l, f"ada{i}")
                for mi in range(4):
                    m = i * 4 + mi
                    mm_group(ps[:, 2 * m:2 * m + 2], rps,
                             [(w[:, k, mi * 128:(mi + 1) * 128], condb[:, k, :], [rw, R("condb")]) for k in range(8)])
            bada = sm(l, 0, 48)
            P.op("dve", lambda e: e.tensor_tensor(out=modv[:, l], in0=ps[:, 0:96].rearrange("p (m t) -> p m t", t=2),
                                                  in1=bada.unsqueeze(2).to_broadcast([128, 48, 2]), op=ALU.add),
                 reads=[rps, R("small")], writes=[R("modv", l)])
            def gvec(i):
                return sm(l, 48 + 8 * i, 56 + 8 * i).unsqueeze(2).to_broadcast([128, 8, 2])
            mv = modv[:, l]
            dv = dvec[:, l]
            rd, wr = [R("modv", l), R("small")], [R("dvec", l)]
            stt("dve", dv[:, 0], mv[:, 8:16], 1.0, gvec(0), ALU.add, ALU.mult, rd, wr)
            P.op("dve", lambda e: e.tensor_copy(out=dv[:, 1], in_=mv[:, 0:8]), reads=rd, writes=wr)
            tt("dve", dv[:, 2], mv[:, 16:24], gvec(1), ALU.mult, rd, wr)
            stt("dve", dv[:, 3], mv[:, 32:40], 1.0, gvec(2), ALU.add, ALU.mult, rd, wr)
            P.op("dve", lambda e: e.tensor_copy(out=dv[:, 4], in_=mv[:, 24:32]), reads=rd, writes=wr)
            tt("dve", dv[:, 5], mv[:, 40:48], gvec(3), ALU.mult, rd, wr)
            wca = small_conv[l]
            for c in range(2):
                for k in range(3):
                    P.op("pool", lambda e, c=c, k=k: e.tensor_scalar(out=diagC[:, c, k, :], in0=ident32[:],
                                                                     scalar1=wca[:, c, 31 + k:32 + k], scalar2=None,
                                                                     op0=ALU.mult),
                         reads=[R("cn32"), R("convw")], writes=[R("diagC")])
            act(sinkf[:], sm(l, 90, 98)[0:64, :], AF.Exp, [R("small")], [R("sinkf")])

        convw_d = dt_in("convw", [128, L, 2, 34])
        convw = sb("convw_sb", [128, L, 2, 34], F32)
        P.dma("act", d_cs[6], convw[:], convw_d, writes=[R("convw")])
        small_conv = [convw[:, l] for l in range(L)]

        def norm_sq(T, xres, xb, sqalt):
            SQE = ("act", "dve", "pool", "act", "dve", "act", "dve", "pool")
            out = []
            for c in range(8):
                if sqalt:
                    sq, rsq = (qT[:, c, 0:T], R("qT", c)) if c < 4 else (hT[:, c, 0:T], R("hT", c))
                else:
                    sq, rsq = S16[:, 8 + c, 0:T], R("S16", 8 + c)
                if SQE[c] == "act":
                    act(sq, xb[:, c, 0:T], AF.Square, [xres[c]], [rsq])
                else:
                    tt(SQE[c], sq, xb[:, c, 0:T], xb[:, c, 0:T], ALU.mult, [xres[c]], [rsq])
                out.append((sq, rsq))
            return out

        def norm_rest(l, T, a_idx, b_idx, col, xres, xb, sqs):
            ps, rps = psum()
            for c in range(8):
                sq, rsq = sqs[c]
                P.op("pe", lambda e, sq=sq, c=c: e.matmul(ps[:, 0:T], lhsT=r1024, rhs=sq, start=(c == 0), stop=(c == 7)),
                     reads=[rsq, R("cbf")], writes=[rps], signal=(c == 7))
            rsq_, rr0 = S32[:, 12, 0:T], R("S32", 12)
            rstd, rr = S32[:, 13, 0:T], R("S32", 13)
            act(rsq_, ps[:, 0:T], AF.Sqrt, [rps], [rr0], bias=EPS, scale=1.0)
            P.op("dve", lambda e: e.reciprocal(out=rstd, in_=rsq_), reads=[rr0], writes=[rr])
            for c in range(8):
                tmp, rt = S32[:, 10 + (c % 2), 0:T], R("S32", 10 + (c % 2))
                tt("dve", tmp, xb[:, c, 0:T], rstd, ALU.mult, [xres[c], rr], [rt])
                act(hT[:, c, 0:T], tmp, AF.Identity, [rt, R("dvec", l)], [R("hT", c)],
                    scale=dvec[:, l, a_idx, c, col:col + 1], bias=dvec[:, l, b_idx, c, col:col + 1])

        def norm_to_h(l, T, a_idx, b_idx, col, xres, xb=None):
            if xb is None:
                xb = XT()
            sqs = norm_sq(T, xres, xb, False)
            norm_rest(l, T, a_idx, b_idx, col, xres, xb, sqs)

        def post_norm_residual(l, T, g_idx, col, mres, xres):
            ps, rps = psum()
            SQE = ("act", "dve", "pool", "act", "dve", "act", "dve", "pool")
            for c in range(8):
                sq, rsq = S16[:, 8 + c, 0:T], R("S16", 8 + c)
                if SQE[c] == "act":
                    act(sq, S32[:, c, 0:T], AF.Square, [mres[c]], [rsq])
                else:
                    tt(SQE[c], sq, S32[:, c, 0:T], S32[:, c, 0:T], ALU.mult, [mres[c]], [rsq])
                P.op("pe", lambda e, sq=sq, c=c: e.matmul(ps[:, 0:T], lhsT=r1024, rhs=sq, start=(c == 0), stop=(c == 7)),
                     reads=[rsq, R("cbf")], writes=[rps], signal=True)
            rsq_, rr0 = S32[:, 12, 0:T], R("S32", 12)
            rstd, rr = S32[:, 13, 0:T], R("S32", 13)
            act(rsq_, ps[:, 0:T], AF.Sqrt, [rps], [rr0], bias=EPS, scale=1.0)
            P.op("dve", lambda e: e.reciprocal(out=rstd, in_=rsq_), reads=[rr0], writes=[rr])
            for c in range(8):
                tt("dve", S32[:, c, 0:T], S32[:, c, 0:T], rstd, ALU.mult, [mres[c], rr], [mres[c]])
                stt("dve", XT()[:, c, 0:T], S32[:, c, 0:T], dvec[:, l, g_idx, c, col:col + 1],
                    XT()[:, c, 0:T], ALU.mult, ALU.add, [mres[c], R("dvec", l), xres[c]], [xres[c]])

        def XT():
            return xTs[XS["cur"]]

        def issue_tok_dma(pas, blocks):
            src = xp_d if pas == "p" else xs_d
            for bi, b in enumerate(blocks):
                P.dma("act", d_xb[bi], xtok(bi), src[b * 128:(b + 1) * 128, :], writes=xtok_res(bi))

        def issue_feat_dma(pas, blocks, idx):
            T = len(blocks) * 128
            b0 = blocks[0]
            P.dma("act", d_xf[idx], xTs[idx][:, :, 0:T], xsc_d[pas][:, :, b0 * 128:b0 * 128 + T],
                  reads=[R("xsc", pas, 0), R("xsc", pas, 1)], writes=[R("xT", idx, c) for c in range(8)])

        def load_x(kind, pas, l, blocks, T, nxt, do_prefetch=True, set_cur=True):
            XS["i"] ^= 1
            cur = XS["i"]
            if set_cur:
                XS["cur"] = cur
            xres = [R("xT", cur, c) for c in range(8)]
            nb = len(blocks)
            key = (kind, pas, l, tuple(blocks))
            tokmode = (l == 0 and kind == "A")
            if XS["pref"] != key:
                if tokmode:
                    issue_tok_dma(pas, blocks)
                else:
                    issue_feat_dma(pas, blocks, cur)
            XS["pref"] = None
            if tokmode:
                for c in range(8):
                    ps, rps = psum()
                    for bi in range(nb):
                        P.op("pe", lambda e, bi=bi, c=c, ps=ps: e.matmul(ps[:, bi * 128:(bi + 1) * 128],
                                                                          lhsT=xtok(bi, c * 128, (c + 1) * 128), rhs=ident32[:],
                                                                          start=True, stop=True),
                             reads=xtok_res(bi) + [R("cn32")], writes=[rps], signal=(bi == nb - 1))
                    if c % 2 == 0:
                        act(xTs[cur][:, c, 0:T], ps[:, 0:T], AF.Copy, [rps], [xres[c]])
                    else:
                        P.op("dve", lambda e, c=c, ps=ps, xb=xTs[cur]: e.tensor_copy(out=xb[:, c, 0:T], in_=ps[:, 0:T]),
                             reads=[rps], writes=[xres[c]])
                b0 = blocks[0]
                P.dma("act", d_scs[cur], xsc_d[pas][:, :, b0 * 128:b0 * 128 + T], xTs[cur][:, :, 0:T],
                      reads=xres, writes=[R("xsc", pas, cur)])
            if do_prefetch:
                prefetch_next(kind, l, nxt)
            return xres

        def prefetch_next(kind, l, nxt):
            cur = XS["i"]
            tokmode = (l == 0 and kind == "A")
            if nxt is not None:
                nkind, npas, nl, nblocks = nxt
                ntok = (nl == 0 and nkind == "A")
                if ntok and tokmode:
                    issue_tok_dma(npas, nblocks)
                    XS["pref"] = (nkind, npas, nl, tuple(nblocks))
                elif not ntok:
                    issue_feat_dma(npas, nblocks, cur ^ 1)
                    XS["pref"] = (nkind, npas, nl, tuple(nblocks))

        def store_x(pas, l, blocks, T, xres):
            b0 = blocks[0]
            if l == 0:
                P.dma("act", d_scs[XS["cur"]], xsc_d[pas][:, :, b0 * 128:b0 * 128 + T], XT()[:, :, 0:T],
                      reads=xres, writes=[R("xsc", pas, XS["cur"])])
            else:
                dst = yp_d if pas == "p" else ys_d
                for bi, b in enumerate(blocks):
                    ob = b if pas == "p" else b - 2
                    for half in range(2):
                        ps, rps = psum()
                        for cc in range(4):
                            c = half * 4 + cc
                            P.op("pe", lambda e, bi=bi, c=c, cc=cc, ps=ps, xb=XT(): e.matmul(
                                ps[:, cc * 128:(cc + 1) * 128], lhsT=xb[:, c, bi * 128:(bi + 1) * 128], rhs=ident32[:],
                                start=True, stop=True),
                                 reads=[xres[c], R("cn32")], writes=[rps], signal=(cc == 3))
                        if half == 0:
                            act(xtok(bi, 0, 512), ps[:], AF.Copy, [rps], [R("S32", 2 * bi)])
                        else:
                            P.op("dve", lambda e, bi=bi, ps=ps: e.tensor_copy(out=xtok(bi, 512, 1024), in_=ps[:]),
                                 reads=[rps], writes=[R("S32", 2 * bi + 1)])
                    P.dma("act", d_outb[bi], dst[ob * 128:(ob + 1) * 128, :], xtok(bi), reads=xtok_res(bi))

        def ccol(pas, t):
            if pas == "p":
                s, r = t // 256, t % 256
                return s * (256 + 2 * PADC) + PADC + r
            return PADC + t

        def conv_rhs(buf, c, pas, blocks, T, shift):
            t0 = blocks[0] * 128
            if pas == "p" and T == 512:
                base = ccol(pas, t0) + shift
                return buf[:, c, base:base + 2 * (256 + 2 * PADC)].rearrange("p (s w) -> p s w", s=2)[:, :, 0:256]
            base = ccol(pas, t0) + shift
            return buf[:, c, base:base + T]

        def conv_dst(buf, c, pas, blocks, T):
            return conv_rhs(buf, c, pas, blocks, T, 0)

        def as_tile(ap, pas, T):
            if pas == "p" and T == 512:
                return ap.rearrange("p (s w) -> p s w", s=2)
            return ap

        def phase_front(kind, pas, l, blocks, T, nxt, col):
            key = (kind, pas, l, tuple(blocks))
            if XS["early"] is not None and XS["early"][0] == key:
                xres = XS["early"][1]
                XS["early"] = None
                XS["cur"] = XS["i"]
                prefetch_next(kind, l, nxt)
                return xres
            xres = load_x(kind, pas, l, blocks, T, nxt)
            norm_to_h(l, T, 0, 1, col, xres)
            return xres

        def early_front(nxt, l_cur):
            if nxt is None:
                return
            nkind, npas, nl, nblocks = nxt
            if nl != l_cur or (nl == 0 and nkind == "A"):
                return
            key = (nkind, npas, nl, tuple(nblocks))
            if XS["pref"] != key:
                return
            nT = len(nblocks) * 128
            ncol = 0 if npas == "p" else 1
            xres_n = load_x(nkind, npas, nl, nblocks, nT, None, do_prefetch=False, set_cur=False)
            xbn = xTs[XS["i"]]
            sqs = norm_sq(nT, xres_n, xbn, True)
            XS["early"] = (key, xres_n)
            XS["early2"] = lambda: norm_rest(nl, nT, 0, 1, ncol, xres_n, xbn, sqs)

        def phase_A(pas, l, blocks, nxt=None):
            T = len(blocks) * 128
            col = 0 if pas == "p" else 1
            if DEBUG_STAGE == 14:
                return
            if DEBUG_STAGE == 12:
                P.op("pool", lambda e: e.memset(ua_c[:, :, 0:4 * (256 + 2 * PADC)], 0.0), writes=[R("ua_c")])
                return
            if pas == "p" and blocks[0] == 0 and DEBUG_STAGE != 13:
                P.op("pool", lambda e: e.memset(ua_c[:, :, 0:4 * (256 + 2 * PADC)], 0.0), writes=[R("ua_c")])
                P.op("pool", lambda e: e.memset(sc_c[:, :, 0:4 * (256 + 2 * PADC)], 0.0), writes=[R("sc_c")])
            xres = phase_front("A", pas, l, blocks, T, nxt, col)
            if DEBUG_STAGE in (2, 21, 22, 23, 24, 211, 212):
                return
            hres = [R("hT", c) for c in range(8)]
            t0 = blocks[0] * 128
            if pas == "s":
                P.dma("act", d_rope, ropeT[:, :, 0:T], rope_d[:, :, t0:t0 + T], writes=[R("ropeT")])
            slabs = [W(l, "A0"), W(l, "A1"), W(l, "A2")]

            def proj(ci):
                w, rw = slabs[ci // 4]
                ps, rps = psum()
                mm_group(ps[:, 0:T], rps, [(w[:, k, (ci % 4) * 128:(ci % 4 + 1) * 128], hT[:, k, 0:T], [rw, hres[k]])
                                           for k in range(8)])
                return ps, rps

            for c in range(2):
                psg, rg = proj(c)
                sig, rs_ = S16[:, 10 + c, 0:T], R("S16", 10 + c)
                act(sig, psg[:, 0:T], AF.Sigmoid, [rg], [rs_])
                psv, rv = proj(2 + c)
                tt("dve", conv_dst(ua_c, c, pas, blocks, T), as_tile(psv[:, 0:T], pas, T), as_tile(sig, pas, T),
                   ALU.mult, [rv, rs_], [R("ua_c")])
            for c in range(2):
                psc, rc = proj(4 + c)
                tmp, rt = S32[:, 8 + c, 0:T], R("S32", 8 + c)
                act(tmp, psc[:, 0:T], AF.Copy, [rc], [rt])
                psh, rh = proj(6 + c)
                tt("dve", conv_dst(sc_c, c, pas, blocks, T), as_tile(psh[:, 0:T], pas, T), as_tile(tmp, pas, T),
                   ALU.mult, [rh, rt], [R("sc_c")])
            psk, rk = proj(8)
            if pas == "p":
                act(kT[:, t0:t0 + T], psk[:, 0:T], AF.Copy, [rk], [R("kT")])
            else:
                pss, rss = proj(9)
                t1, r1 = S32[:, 8, 0:T], R("S32", 8)
                t2, r2 = S32[:, 9, 0:T], R("S32", 9)
                tt("dve", t1, psk[:, 0:T], ropeT[:, 0, 0:T], ALU.mult, [rk, R("ropeT")], [r1])
                tt("dve", t2, pss[:, 0:T], ropeT[:, 1, 0:T], ALU.mult, [rss, R("ropeT")], [r2])
                tt("pool", kT[:, t0:t0 + T], t1, t2, ALU.add, [r1, r2], [R("kT")])
            if DEBUG_STAGE == 3:
                wstate["pos"] += 1
                return
            wt, rwt = W(l, "AT")
            for bi, b in enumerate(blocks):
                ps, rps = psum()
                ncol = 512 if pas == "p" else 384
                mm_group(ps[:, 0:ncol], rps, [(hT[:, k, bi * 128:(bi + 1) * 128], wt[:, k, 0:ncol], [rwt, hres[k]])
                                              for k in range(8)])
                vs = None
                if pas == "s" and b in (0, 1):
                    vs = validL
                if pas == "s" and b in (18, 19):
                    vs = validR
                if vs is None:
                    act(uptok[:, b, :], ps[:, 0:256], AF.Copy, [rps], [R("uptok")])
                    P.op("dve", lambda e, b=b, ps=ps: e.tensor_copy(out=vtok[:, b, :], in_=ps[:, 256:384]),
                         reads=[rps], writes=[R("vtok")])
                else:
                    act(uptok[:, b, :], ps[:, 0:256], AF.Identity, [rps, R("small")], [R("uptok")], scale=vs)
                    act(vtok[:, b, :], ps[:, 256:384], AF.Identity, [rps, R("small")], [R("vtok")], scale=vs)
                    for buf, nm in ((ua_c, "ua_c"), (sc_c, "sc_c")):
                        cs = ccol(pas, b * 128)
                        P.op("pool", lambda e, buf=buf, cs=cs, vs=vs: e.tensor_scalar(
                            out=buf[:, :, cs:cs + 128], in0=buf[:, :, cs:cs + 128], scalar1=vs, scalar2=None,
                            op0=ALU.mult), reads=[R(nm), R("small")], writes=[R(nm)])
                if pas == "p":
                    kb = 0
                    P.op("dve", lambda e, kb=kb, ps=ps: e.tensor_copy(out=kvo[:, kb, :], in_=ps[:, 256:512]),
                         reads=[rps], writes=[R("kvo", kb)])
                    P.dma("act", d_kvs[kb][0], nv_d[l, b * 128:(b + 1) * 128, :], kvo[:, kb, 0:128], reads=[R("kvo", kb)])
                    P.dma("act", d_kvs[kb][1], nk_d[l, b * 128:(b + 1) * 128, :], kvo[:, kb, 128:256], reads=[R("kvo", kb)])

        def phase_B(pas, l, blocks, nxt=None):
            T = len(blocks) * 128
            nb = len(blocks)
            col = 0 if pas == "p" else 1
            t0 = blocks[0] * 128
            xres = phase_front("B", pas, l, blocks, T, nxt, col)
            hres = [R("hT", c) for c in range(8)]
            if pas == "s":
                P.dma("act", d_rope, ropeT[:, :, 0:T], rope_d[:, :, t0:t0 + T], writes=[R("ropeT")])
            slabs = [W(l, "B0"), W(l, "B1"), W(l, "B2")]

            def proj(ci):
                w, rw = slabs[ci // 4]
                ps, rps = psum()
                mm_group(ps[:, 0:T], rps, [(w[:, k, (ci % 4) * 128:(ci % 4 + 1) * 128], hT[:, k, 0:T], [rw, hres[k]])
                                           for k in range(8)])
                return ps, rps

            for c in range(2):
                ps, rps = proj(c)
                act(S32[:, 8 + c, 0:T], ps[:, 0:T], AF.Copy, [rps], [R("S32", 8 + c)])
            for c in range(4):
                psq, rq = proj(2 + c)
                if pas == "p":
                    act(qT[:, c, 0:T], psq[:, 0:T], AF.Copy, [rq], [R("qT", c)])
                else:
                    pss, rss = proj(6 + c)
                    t1, r1 = S32[:, 10, 0:T], R("S32", 10)
                    t2, r2 = S32[:, 11, 0:T], R("S32", 11)
                    tt("dve", t1, psq[:, 0:T], ropeT[:, 0, 0:T], ALU.mult, [rq, R("ropeT")], [r1])
                    tt("dve", t2, pss[:, 0:T], ropeT[:, 1, 0:T], ALU.mult, [rss, R("ropeT")], [r2])
                    tt("pool", qT[:, c, 0:T], t1, t2, ALU.add, [r1, r2], [R("qT", c)])
            uar = []
            for c in range(2):
                ps, rps = psum()
                for k in range(31):
                    di = dg_ctr[0] % 8
                    dg_ctr[0] += 1
                    P.op("dve", lambda e, c=c, k=k, di=di: e.tensor_scalar(
                        out=diagA[:, di, :], in0=ident32[:], scalar1=convw[:, l, c, k:k + 1], scalar2=None,
                        op0=ALU.mult), reads=[R("cn32"), R("convw")], writes=[R("diagA", di)])
                    P.op("pe", lambda e, c=c, k=k, di=di, ps=ps: e.matmul(
                        as_tile(ps[:, 0:T], pas, T), lhsT=diagA[:, di, :],
                        rhs=conv_rhs(ua_c, c, pas, blocks, T, k - 15), start=(k == 0), stop=(k == 30)),
                         reads=[R("diagA", di), R("ua_c")], writes=[rps], signal=True)
                u, ru = S32[:, c, 0:T], R("S32", c)
                act(u, ps[:, 0:T], AF.Identity, [rps, R("small")], [ru], bias=sm(l, 80 + c, 81 + c), scale=1.0)
                ub_, rub = S16[:, 18 + c, 0:T], R("S16", 18 + c)
                P.op("pool", lambda e, u=u, ub_=ub_: e.tensor_copy(out=ub_, in_=u), reads=[ru], writes=[rub])
                uar.append((u, ru, ub_, rub))
            psm, rpm = psum()
            mm_group(psm[:, 0:T], rpm, [(r256, uar[c][2], [uar[c][3], R("cbf")]) for c in range(2)])
            for c in range(2):
                u, ru, ub_, rub = uar[c]
                tt("dve", u, u, psm[:, 0:T], ALU.subtract, [ru, rpm], [ru])
                act(ub_, u, AF.Square, [ru], [rub])
            psv, rpv = psum()
            mm_group(psv[:, 0:T], rpv, [(r256, uar[c][2], [uar[c][3], R("cbf")]) for c in range(2)])
            rs_a0, rra0 = S32[:, 2, 0:T], R("S32", 2)
            rs_a, rra = S32[:, 13, 0:T], R("S32", 13)
            act(rs_a0, psv[:, 0:T], AF.Sqrt, [rpv], [rra0], bias=EPS, scale=1.0)
            P.op("dve", lambda e: e.reciprocal(out=rs_a, in_=rs_a0), reads=[rra0], writes=[rra])
            for c in range(2):
                u, ru, ub_, rub = uar[c]
                tt("dve" if c else "pool", u, u, rs_a, ALU.mult, [ru, rra], [ru])
                act(S16[:, 16 + c, 0:T], u, AF.Silu, [ru, R("small")], [R("S16", 16 + c)],
                    scale=sm(l, 82 + c, 83 + c), bias=sm(l, 84 + c, 85 + c))
            wp, rwp = W(l, "wpool")
            for g in range(4):
                ps, rps = psum()
                for bi, b in enumerate(blocks):
                    items = []
                    if pas == "p":
                        first = (b % 2 == 0)
                        kinds = [(b, 3 if first else 4)]
                        kinds.append((b + 1, 1) if first else (b - 1, 0))
                    else:
                        cur = 5 if b == 2 else (6 if b == 17 else 2)
                        kinds = [(b - 1, 0), (b, cur), (b + 1, 1)]
                    for (j, kind) in kinds:
                        items.append((uptok[:, j, g * 64:(g + 1) * 64], pmat[:, kind * 4 + g, :], [R("uptok"), R("pmat")]))
                    mm_group(ps[0:64, bi * 128:(bi + 1) * 128], rps, items)
                pl, rpl = S16[0:64, 8 + (g % 2), 0:T], R("S16", 8 + (g % 2))
                if g % 2:
                    P.op("dve", lambda e, pl=pl, ps=ps: e.tensor_copy(out=pl, in_=ps[0:64, 0:T]), reads=[rps], writes=[rpl])
                else:
                    act(pl, ps[0:64, 0:T], AF.Copy, [rps], [rpl])
                ps2, rps2 = psum()
                mm_group(ps2[0:64, 0:T], rps2, [(wp[0:64, 0, g * 64:(g + 1) * 64], pl, [rwp, rpl])])
                act(S16[0:64, 20 + g, 0:T], ps2[0:64, 0:T], AF.Identity, [rps2, R("small")], [R("S16", 20 + g)],
                    scale=sm(l, 86 + g, 87 + g)[0:64, :])
            for c in range(2):
                ps, rps = psum()
                mm_group(as_tile(ps[:, 0:T], pas, T), rps,
                         [(diagC[:, c, k, :], conv_rhs(sc_c, c, pas, blocks, T, k - 1), [R("diagC"), R("sc_c")])
                          for k in range(3)])
                tt("dve", S16[:, 18 + c, 0:T], ps[:, 0:T], S32[:, 8 + c, 0:T], ALU.mult,
                   [rps, R("S32", 8 + c)], [R("S16", 18 + c)])
            jobs = []
            for bi, b in enumerate(blocks):
                if pas == "p":
                    s0 = (b // 2) * 2
                    keys = [("l", s0, None, 0), ("l", s0 + 1, None, 0)]
                else:
                    keys = []
                    for j, mk in ((b - 1, maskP), (b, None), (b + 1, maskN)):
                        oi = 1 if j in (0, 1) else (2 if j in (18, 19) else 0)
                        keys.append(("l", j, mk, oi))
                    keys += [("c", 0, None, 0), ("c", 1, None, 0)]
                for g in range(2):
                    for ki, ks in enumerate(keys):
                        jobs.append((bi, g, ki, len(keys), ks))

            def kv_of(job):
                bi, g, ki, nk, (kind, j, mk, oi) = job
                if kind == "l":
                    return (kT[g * 64:(g + 1) * 64, j * 128:(j + 1) * 128], R("kT"),
                            vtok[:, j, g * 64:(g + 1) * 64], R("vtok"))
                return (kcT[g * 64:(g + 1) * 64, l, j * 128:(j + 1) * 128], R("kc"),
                        vcs[:, l, j, g * 64:(g + 1) * 64], R("vc"))

            def issue_S(job):
                bi, g, ki, nk, ks = job
                klhs, kres, _, _ = kv_of(job)
                pS, rpS = psum(0, 4)
                qr = qT[g * 64:(g + 1) * 64, :, bi * 128:(bi + 1) * 128]
                mm_group(pS[:].rearrange("p (h q) -> p h q", h=4), rpS,
                         [(klhs, qr, [kres] + [R("qT", c) for c in range(4)])])
                return pS, rpS

            LA = 2
            pending = []
            inflight = {}
            for ji in range(min(LA, len(jobs))):
                inflight[ji] = issue_S(jobs[ji])
            acc = None
            for ji, job in enumerate(jobs):
                bi, g, ki, nk, (kind, j, mk, oi) = job
                if ki == 0:
                    acc = (psum(4, 6), psum(6, 8))
                (po, rpo), (pd, rpd) = acc
                pS, rpS = inflight.pop(ji)
                _, _, vlhs, vres = kv_of(job)
                es = 12 + (ji % 4)
                E, rE = S16[:, es, :], R("S16", es)
                act(E, pS[:], AF.Exp, [rpS], [rE], scale=0.125)
                if mk is not None:
                    E3 = E.rearrange("p (h q) -> p h q", h=4)
                    tt("pool", E3, E3, mk.unsqueeze(1).to_broadcast([128, 4, 128]), ALU.mult, [rE, R("cbf")], [rE])
                if ji + LA < len(jobs):
                    inflight[ji + LA] = issue_S(jobs[ji + LA])
                P.op("pe", lambda e, vlhs=vlhs, E=E, ki=ki, po=po, nk=nk: e.matmul(
                    po[0:64, :], lhsT=vlhs, rhs=E, start=(ki == 0), stop=(ki == nk - 1)),
                     reads=[vres, rE], writes=[rpo], signal=False)
                P.op("pe", lambda e, oi=oi, E=E, ki=ki, pd=pd, nk=nk: e.matmul(
                    pd[0:64, :], lhsT=ones64[oi], rhs=E, start=(ki == 0), stop=(ki == nk - 1)),
                     reads=[R("cbf"), rE], writes=[rpd, rpo], signal=True)
                while pending and pending[0][0] <= ji:
                    pending.pop(0)[1]()
                if ki == nk - 1:
                  def normalize(bi=bi, g=g, po=po, rpo=rpo, pd=pd, rpd=rpd):
                    den, rden = S32[0:64, 3 + g, :], R("S32", 3 + g)
                    rec, rrec = S32[0:64, 5 + g, :], R("S32", 5 + g)
                    tt("dve", den.rearrange("p (h q) -> p h q", h=4), pd[0:64, :].rearrange("p (h q) -> p h q", h=4),
                       sinkf[:, g * 4:(g + 1) * 4].unsqueeze(2).to_broadcast([64, 4, 128]), ALU.add,
                       [rpd, R("sinkf")], [rden])
                    act(den, den, AF.Ln, [rden], [rden])
                    act(rec, den, AF.Exp, [rden], [rrec], scale=-1.0)
                    for hh in range(4):
                        slot = 24 + g * 4 + hh
                        tt("dve", S16[0:64, slot, bi * 128:(bi + 1) * 128], po[0:64, hh * 128:(hh + 1) * 128],
                           rec[:, hh * 128:(hh + 1) * 128], ALU.mult, [rpo, rrec], [R("S16", slot)])
                  pending.append((ji + 2, normalize))
            while pending:
                pending.pop(0)[1]()
            for c in range(8):
                wg, rwg = W(l, f"g{c}")
                wb, rwb = W(l, f"br{c}")
                gsl = []
                for b_ in range(4):
                    ps, rps = psum()
                    mm_group(ps[:, 0:T], rps, [(wg[:, k, b_ * 128:(b_ + 1) * 128], hT[:, k, 0:T], [rwg, hres[k]])
                                               for k in range(8)])
                    gs = 8 + (c % 2) * 4 + b_
                    act(S16[:, gs, 0:T], ps[:, 0:T], AF.Sigmoid, [rps], [R("S16", gs)])
                    gsl.append(gs)
                brs = []
                ps, rps = psum()
                mm_group(ps[:, 0:T], rps, [(wb[:, 0, k * 128:(k + 1) * 128], S16[:, 16 + k, 0:T], [rwb, R("S16", 16 + k)])
                                           for k in range(2)])
                brs.append((ps, rps))
                ps, rps = psum()
                mm_group(ps[:, 0:T], rps, [(wb[0:64, 0, 256 + g * 128:256 + (g + 1) * 128], S16[0:64, 20 + g, 0:T],
                                            [rwb, R("S16", 20 + g)]) for g in range(4)])
                brs.append((ps, rps))
                ps, rps = psum()
                mm_group(ps[:, 0:T], rps, [(wb[:, 0, 768 + k * 128:768 + (k + 1) * 128], S16[:, 18 + k, 0:T],
                                            [rwb, R("S16", 18 + k)]) for k in range(2)])
                brs.append((ps, rps))
                ps, rps = psum()
                mm_group(ps[:, 0:T], rps, [(wb[0:64, 0, 1024 + h * 128:1024 + (h + 1) * 128], S16[0:64, 24 + h, 0:T],
                                            [rwb, R("S16", 24 + h)]) for h in range(8)])
                brs.append((ps, rps))
                acc, racc = S32[:, 5 + (c % 2), 0:T], R("S32", 5 + (c % 2))
                tmp, rtmp = S32[:, 7, 0:T], R("S32", 7)
                tt("dve", acc, brs[0][0][:, 0:T], S16[:, gsl[0], 0:T], ALU.mult, [brs[0][1], R("S16", gsl[0])], [racc])
                for b_ in range(1, 4):
                    tt("dve", tmp, brs[b_][0][:, 0:T], S16[:, gsl[b_], 0:T], ALU.mult,
                       [brs[b_][1], R("S16", gsl[b_])], [rtmp])
                    if b_ < 3:
                        tt("pool", acc, acc, tmp, ALU.add, [racc, rtmp], [racc])
                    else:
                        tt("pool", S16[:, c, 0:T], acc, tmp, ALU.add, [racc, rtmp], [R("S16", c)])
            wos = [W(l, "wo0"), W(l, "wo1")]
            mres = [R("S32", c) for c in range(8)]
            for c in range(8):
                w, rw = wos[c // 4]
                ps, rps = psum()
                mm_group(ps[:, 0:T], rps, [(w[:, k, (c % 4) * 128:(c % 4 + 1) * 128], S16[:, k, 0:T], [rw, R("S16", k)])
                                           for k in range(8)])
                act(S32[:, c, 0:T], ps[:, 0:T], AF.Copy, [rps], [mres[c]])
            post_norm_residual(l, T, 2, col, mres, xres)
            norm_to_h(l, T, 3, 4, col, xres)
            for i in range(8):
                w, rw = W(l, f"f1_{i}")
                for ci in range(4):
                    j = i * 4 + ci
                    ps, rps = psum()
                    mm_group(ps[:, 0:T], rps, [(w[:, k, ci * 128:(ci + 1) * 128], hT[:, k, 0:T], [rw, hres[k]])
                                               for k in range(8)])
                    tmp, rtmp = S32[:, 8 + (j % 4), 0:T], R("S32", 8 + (j % 4))
                    act(tmp, ps[:, 0:T], AF.Relu, [rps], [rtmp])
                    tt("dve" if j % 2 else "pool", S16[:, j, 0:T], tmp, tmp, ALU.mult, [rtmp], [R("S16", j)])
            XS["early2"] = None
            early_front(nxt, l)
            for c in range(8):
                if c == 2 and XS["early2"] is not None:
                    XS["early2"]()
                    XS["early2"] = None
                w, rw = W(l, f"f2_{c}")
                ps, rps = psum()
                mm_group(ps[:, 0:T], rps, [(w[:, k, :], S16[:, k, 0:T], [rw, R("S16", k)]) for k in range(32)])
                act(S32[:, c, 0:T], ps[:, 0:T], AF.Copy, [rps], [mres[c]])
            post_norm_residual(l, T, 5, col, mres, xres)
            store_x(pas, l, blocks, T, xres)

        sched_all = schedule(do_sample)[:limit]
        for si, (kind, pas, l, blocks) in enumerate(sched_all):
            if kind == "setup" and l == 1:
                while l1_cast:
                    issue_cast(1, l1_cast.pop(0))
            if l == 0 and si >= 4 and l1_cast:
                issue_cast(1, l1_cast.pop(0))
            if kind == "setup":
                layer_setup(l)
            else:
                nxt = None
                for it in sched_all[si + 1:]:
                    if it[0] != "setup":
                        nxt = it
                        break
                (phase_A if kind == "A" else phase_B)(pas, l, blocks, nxt)
        assert DEBUG_STAGE or wstate["pos"] == len(seq), (wstate["pos"], len(seq))
        P.wait_all("act", d_outb + d_kvs[0] + [d_sc, d_x, d_rope] + d_xf + d_scs + d_xb + d_cs)
        P.wait_all("sp", d_ring + d_cast)
        P.wait_all("pool", d_cast + d_cs)
        print("sbuf bytes remaining", nc.sbuf_bytes_remaining)
        print("instructions", P.n_inst, "waits", P.n_wait)
        P.emit()
    return nc


def tiles_of(blocks):
    return [blocks[i:i + 4] for i in range(0, len(blocks), 4)]


def schedule(do_sample):
    out = []
    for l in range(L):
        out.append(("setup", None, l, None))
        for t in tiles_of(list(range(8))):
            out.append(("A", "p", l, t))
        for t in tiles_of(list(range(8))):
            out.append(("B", "p", l, t))
        if do_sample:
            lo, hi = (0, 20) if l == 0 else (1, 19)
            for t in tiles_of(list(range(lo, hi))):
                out.append(("A", "s", l, t))
            lo, hi = (1, 19) if l == 0 else (2, 18)
            for t in tiles_of(list(range(lo, hi))):
                out.append(("B", "s", l, t))
    return out


def slab_sequence(do_sample, limit=None):
    seq = []
    for (kind, pas, l, blocks) in schedule(do_sample)[:limit]:
        if kind == "setup":
            seq += [(l, f"ada{i}") for i in range(12)]
        elif kind == "A":
            seq += [(l, "A0"), (l, "A1"), (l, "A2"), (l, "AT")]
        else:
            seq += [(l, "B0"), (l, "B1"), (l, "B2"), (l, "wpool")]
            for c in range(8):
                seq += [(l, f"g{c}"), (l, f"br{c}")]
            seq += [(l, "wo0"), (l, "wo1")]
            seq += [(l, f"f1_{i}") for i in range(8)]
            seq += [(l, f"f2_{c}") for c in range(8)]
    return seq


_NC_CACHE = {}


def host_inputs(inp, core):
    b, half = core // 2, core % 2
    start = half * 2048
    m = {}
    m["xp"] = np.ascontiguousarray(inp["x_prompt"][4 * core:4 * core + 4].reshape(1024, D))
    xs = np.zeros((WIN, D), np.float32)
    lo, hi = start - 256, start + 2304
    slo, shi = max(lo, 0), min(hi, 4096)
    xs[slo - lo:shi - lo] = inp["x_sample"][b, slo:shi]
    m["xs"] = xs
    validL = 1.0 if lo >= 0 else 0.0
    validR = 1.0 if hi <= 4096 else 0.0
    cn = np.zeros((128, 3 * 128 + 3 * 64 + 256), np.float32)
    cn[:, 0:128] = np.eye(128)
    kk = np.arange(128)[:, None]
    qq = np.arange(128)[None, :]
    cn[:, 128:256] = (kk >= qq)
    cn[:, 256:384] = (kk <= qq)
    cn[:, 384:448] = 1.0
    cn[:, 448:512] = validL
    cn[:, 512:576] = validR
    cn[:, 576:704] = 1.0 / 1024
    cn[:, 704:832] = 1.0 / 256
    m["cnst"] = cn
    PM = pool_mats()
    pm = np.zeros((7, 4, 128, 128), np.float32)
    pm[0:5] = PM
    pm[5] = PM[3] if half == 0 else PM[2]
    pm[6] = PM[4] if half == 1 else PM[2]
    m["pmat"] = np.ascontiguousarray(pm.reshape(28, 128, 128).transpose(1, 0, 2).reshape(128, 28 * 128))
    sm = np.zeros((128, 16 + L * 128), np.float32)
    cond = np.stack([inp["c_ctx"], inp["c"][b]], axis=1)
    sm[:, 0:16] = cond.reshape(8, 128, 2).transpose(1, 0, 2).reshape(128, 16)
    for l in range(L):
        o = 16 + l * 128
        sm[:, o:o + 48] = inp["b_ada"][l].reshape(48, 128).T
        for i, nm in enumerate(("g_pre_mix", "g_post_mix", "g_pre_ffn", "g_post_ffn")):
            sm[:, o + 48 + 8 * i:o + 56 + 8 * i] = inp[nm][l].reshape(8, 128).T
        sm[:, o + 80:o + 82] = inp["b_conv_a"][l].reshape(2, 128).T
        sm[:, o + 82:o + 84] = inp["ln_g_a"][l].reshape(2, 128).T
        sm[:, o + 84:o + 86] = inp["ln_b_a"][l].reshape(2, 128).T
        sm[0:64, o + 86:o + 90] = inp["pool_scale"][l].reshape(4, 64).T
        sm[:, o + 90:o + 98] = inp["sink"][l][None, :]
    sm[:, 16 + 98] = validL
    sm[:, 16 + 99] = validR
    m["small"] = sm
    cw = np.zeros((128, L, 2, 34), np.float32)
    for l in range(L):
        cw[:, l, :, 0:31] = inp["w_conv_a"][l].reshape(31, 2, 128).transpose(2, 1, 0)
        cw[:, l, :, 31:34] = inp["w_sc"][l].reshape(3, 2, 128).transpose(2, 1, 0)
    m["convw"] = cw
    pos = np.arange(lo, hi)
    posc = np.clip(pos, 0, 4095)
    row, colp = posc // 64, posc % 64
    p = np.arange(128)
    d = p % 64
    r = d % 32
    fi = (r % 16).astype(np.float64)
    freq = 10000.0 ** (-fi / 16.0)
    psel = np.where((d < 32)[:, None], row[None, :], colp[None, :]).astype(np.float64)
    ang = psel * freq[:, None]
    sgn = np.where(r < 16, -1.0, 1.0)[:, None]
    rp = np.zeros((128, 2, WIN), np.float32)
    rp[:, 0] = np.cos(ang)
    rp[:, 1] = np.sin(ang) * sgn
    m["rope"] = rp
    m["kcT"] = np.ascontiguousarray(inp["cache_k"][b].reshape(L, 256, 128).transpose(2, 0, 1))
    m["vc"] = np.ascontiguousarray(inp["cache_v"][b].reshape(L, 2, 128, 128).transpose(2, 0, 1, 3))
    return m


def kernel(**inputs):
    inp = {k: np.asarray(v) for k, v in inputs.items()}
    if "nc" not in _NC_CACHE:
        _NC_CACHE["nc"] = build_program(True)
    nc = _NC_CACHE["nc"]
    packs = [host_pack_layer(inp, l) for l in range(L)]
    in_maps = []
    for core in range(8):
        m = host_inputs(inp, core)
        for l in range(L):
            m[f"wpk{l}"] = packs[l]
        in_maps.append(m)
    res = run_bass_kernel_spmd(nc, in_maps, core_ids=list(range(8)))
    rs = res.results
    yp = np.concatenate([rs[c]["yp"].reshape(4, 256, D) for c in range(8)], axis=0)
    ys = np.stack([np.concatenate([rs[2 * b]["ys"], rs[2 * b + 1]["ys"]], axis=0) for b in range(4)], axis=0)
    nk = np.concatenate([rs[c]["nk"].reshape(L, 4, 256, 2, 64).transpose(1, 0, 2, 3, 4) for c in range(8)], axis=0)
    nv = np.concatenate([rs[c]["nv"].reshape(L, 4, 256, 2, 64).transpose(1, 0, 2, 3, 4) for c in range(8)], axis=0)
    return (yp.astype(np.float32), ys.astype(np.float32), nk.astype(np.float32), nv.astype(np.float32))
```

```python
from contextlib import ExitStack
import numpy as np
import concourse.bass as bass
import concourse.mybir as mybir
from concourse.bass_utils import run_bass_kernel_spmd

F32 = mybir.dt.float32
BF16 = mybir.dt.bfloat16
AF = mybir.ActivationFunctionType
ALU = mybir.AluOpType

ENGS = ("pe", "act", "dve", "pool", "sp")
DEBUG_STAGE = 0
D = 1024
L = 2
NB_S = 20
WIN = NB_S * 128
PADC = 16
EPS = 1e-6
SLAB = 4096


class Sem:
    __slots__ = ("h", "count", "name")

    def __init__(self, h, name):
        self.h = h
        self.count = 0
        self.name = name


class Res:
    __slots__ = ("name", "w", "r", "excl")

    def __init__(self, name=""):
        self.name = name
        self.w = None
        self.r = {}
        self.excl = False


class Prog:
    def __init__(self, nc, stack):
        self.nc = nc
        self.stack = stack
        self.streams = {e: [] for e in ENGS}
        self.esem = {e: self.new_sem("eng_" + e) for e in ENGS}
        self.waited = {e: {} for e in ENGS}
        self.n_inst = 0
        self.n_wait = 0
        self._res = {}

    def new_sem(self, name):
        return Sem(self.stack.enter_context(self.nc.semaphore(name)), name)

    def R(self, *key):
        r = self._res.get(key)
        if r is None:
            r = Res(str(key))
            self._res[key] = r
        return r

    def _collect(self, eng, reads, writes):
        deps = {}

        def add(t, kind):
            if t is None:
                return
            s, v, e = t
            if e == eng and eng == "pe":
                return
            if deps.get(s, 0) < v:
                deps[s] = v

        for r in reads:
            add(r.w, "raw")
            if r.excl:
                for s, (v, e) in r.r.items():
                    if e != eng:
                        add((s, v, e), "rar")
        for w in writes:
            add(w.w, "waw")
            for s, (v, e) in w.r.items():
                add((s, v, e), "war")
        waits = []
        wd = self.waited[eng]
        for s, v in deps.items():
            if wd.get(s, 0) < v:
                wd[s] = v
                waits.append((s.h, v))
        return waits

    def _mark(self, ticket, reads, writes):
        s, v, e = ticket
        for r in reads:
            cur = r.r.get(s)
            if cur is None or cur[0] < v:
                r.r[s] = (v, e)
        for w in writes:
            w.w = ticket
            w.r = {}

    def op(self, eng, fn, reads=(), writes=(), signal=True):
        waits = self._collect(eng, reads, writes)
        sem = self.esem[eng]
        ticket = (sem, sem.count + 1, eng)
        if signal:
            sem.count += 1
        self.streams[eng].append((waits, fn, (sem.h, 1) if signal else None))
        self._mark(ticket, reads, writes)
        self.n_inst += 1
        self.n_wait += len(waits)
        return ticket

    def dma(self, eng, dsem, out_ap, in_ap, reads=(), writes=(), **kw):
        waits = self._collect(eng, reads, writes)
        dsem.count += 16
        ticket = (dsem, dsem.count, "dma")

        def fn(e, out_ap=out_ap, in_ap=in_ap):
            return e.dma_start(out=out_ap, in_=in_ap, **kw)

        self.streams[eng].append((waits, fn, (dsem.h, 16)))
        self._mark(ticket, reads, writes)
        self.n_inst += 1
        self.n_wait += len(waits)
        return ticket

    def wait_all(self, eng, sems):
        waits = [(s.h, s.count) for s in sems if s.count > 0]
        self.streams[eng].append((waits, None, None))

    def emit(self):
        streams = self.streams

        def run(e_obj, lst):
            for waits, fn, inc in lst:
                for (h, v) in waits:
                    e_obj.wait_ge(h, v)
                if fn is None:
                    continue
                ins = fn(e_obj)
                if inc is not None:
                    ins.then_inc(inc[0], inc[1])

        with self.nc.Block() as block:
            @block.tensor
            def _(t):
                run(t, streams["pe"])

            @block.scalar
            def _(s):
                run(s, streams["act"])

            @block.vector
            def _(v):
                run(v, streams["dve"])

            @block.gpsimd
            def _(g):
                run(g, streams["pool"])

            @block.sync
            def _(sp):
                run(sp, streams["sp"])


def pack_spec():
    spec = []
    for i in range(12):
        spec.append((f"ada{i}", 8, 512))
    spec += [("A0", 8, 512), ("A1", 8, 512), ("A2", 8, 256), ("AT", 8, 512)]
    spec += [("B0", 8, 512), ("B1", 8, 512), ("B2", 8, 256)]
    spec.append(("wpool", 1, 256))
    for c in range(8):
        spec.append((f"g{c}", 8, 512))
        spec.append((f"br{c}", 1, 2048))
    spec += [("wo0", 8, 512), ("wo1", 8, 512)]
    for i in range(8):
        spec.append((f"f1_{i}", 8, 512))
    for c in range(8):
        spec.append((f"f2_{c}", 32, 128))
    offs = {}
    o = 0
    for name, kc, n in spec:
        offs[name] = (o, kc, n)
        o += kc * n
    return spec, offs, o


PACK_SPEC, PACK_OFFS, PACK_EL = pack_spec()


def rope_swap_idx():
    idx = np.zeros(64, np.int64)
    for d in range(64):
        blk, r = d // 32, d % 32
        partner = r + 16 if r < 16 else r - 16
        idx[d] = blk * 32 + partner
    return idx


def host_pack_layer(inp, l):
    w_in = inp["w_in"][l]
    out = np.zeros((128, PACK_EL), np.float32)

    def put(name, arr):
        o, kc, n = PACK_OFFS[name]
        assert arr.shape == (128, kc, n), (name, arr.shape)
        out[:, o:o + kc * n] = arr.reshape(128, kc * n)

    def kmaj(w):
        K, n = w.shape
        return np.ascontiguousarray(w.reshape(K // 128, 128, n).transpose(1, 0, 2))

    wada = inp["w_ada"][l]
    for i in range(12):
        put(f"ada{i}", kmaj(wada[:, i * 512:(i + 1) * 512]))
    sw = rope_swap_idx()
    kcols = np.arange(2048, 2176)
    kscols = np.concatenate([2048 + g * 64 + sw for g in range(2)])
    colsA = np.concatenate([np.arange(256, 512), np.arange(0, 256), np.arange(1024, 1280),
                            np.arange(1280, 1536), kcols, kscols])
    WA = w_in[:, colsA]
    put("A0", kmaj(WA[:, 0:512])); put("A1", kmaj(WA[:, 512:1024])); put("A2", kmaj(WA[:, 1024:1280]))
    colsT = np.concatenate([np.arange(512, 768), np.arange(2176, 2304), np.arange(2048, 2176)])
    put("AT", kmaj(w_in[:, colsT]))
    qcols = np.concatenate([np.concatenate([1536 + i * 64 + np.arange(64), 1536 + (4 + i) * 64 + np.arange(64)])
                            for i in range(4)])
    qscols = np.concatenate([np.concatenate([1536 + i * 64 + sw, 1536 + (4 + i) * 64 + sw]) for i in range(4)])
    colsB = np.concatenate([np.arange(768, 1024), qcols, qscols])
    WB = w_in[:, colsB]
    put("B0", kmaj(WB[:, 0:512])); put("B1", kmaj(WB[:, 512:1024])); put("B2", kmaj(WB[:, 1024:1280]))
    wp = np.zeros((128, 1, 256), np.float32)
    for g in range(4):
        wp[0:64, 0, g * 64:(g + 1) * 64] = inp["w_pool"][l, g]
    put("wpool", wp)
    for c in range(8):
        gcols = np.concatenate([2304 + b * 1024 + c * 128 + np.arange(128) for b in range(4)])
        put(f"g{c}", kmaj(w_in[:, gcols]))
        br = np.zeros((128, 1, 2048), np.float32)
        cs = slice(c * 128, (c + 1) * 128)
        for k in range(2):
            br[:, 0, k * 128:(k + 1) * 128] = inp["w_a_out"][l, k * 128:(k + 1) * 128, cs]
            br[:, 0, 768 + k * 128:768 + (k + 1) * 128] = inp["w_c_out"][l, k * 128:(k + 1) * 128, cs]
        for g in range(4):
            br[0:64, 0, 256 + g * 128:256 + (g + 1) * 128] = inp["w_b_out"][l, g * 64:(g + 1) * 64, cs]
        for h in range(8):
            br[0:64, 0, 1024 + h * 128:1024 + (h + 1) * 128] = inp["w_d_out"][l, h * 64:(h + 1) * 64, cs]
        put(f"br{c}", br)
    wo = inp["w_o"][l]
    put("wo0", kmaj(wo[:, 0:512])); put("wo1", kmaj(wo[:, 512:1024]))
    w1 = inp["w_ff1"][l]
    for i in range(8):
        put(f"f1_{i}", kmaj(w1[:, i * 512:(i + 1) * 512]))
    w2 = inp["w_ff2"][l]
    for c in range(8):
        put(f"f2_{c}", kmaj(w2[:, c * 128:(c + 1) * 128]))
    return out


def pool_mats():
    sizes = (2, 4, 8, 16)
    M = np.zeros((5, 4, 128, 128), np.float64)
    for g, w in enumerate(sizes):
        hw = w // 2
        for t in range(128):
            for kind in range(5):
                for tp_rel in range(t - hw, t + hw):
                    if kind in (0, 1, 2):
                        cnt = w
                    elif kind == 3:
                        cnt = (t + hw) - max(t - hw, 0)
                    else:
                        cnt = min(t + hw, 128) - (t - hw)
                    if kind == 0 and tp_rel < 0:
                        M[kind, g, tp_rel + 128, t] += 1.0 / cnt
                    elif kind == 1 and tp_rel >= 128:
                        M[kind, g, tp_rel - 128, t] += 1.0 / cnt
                    elif kind >= 2 and 0 <= tp_rel < 128:
                        M[kind, g, tp_rel, t] += 1.0 / cnt
                if kind >= 2:
                    M[kind, g, t, t] -= 1.0
    return M.astype(np.float32)


def build_program(do_sample=True, limit=None):
    nc = bass.Bass("TRN2", target_bir_lowering=False)
    dt_in = lambda name, shape: nc.dram_tensor(name, list(shape), F32, kind="ExternalInput").ap()
    xp_d = dt_in("xp", [1024, D])
    xs_d = dt_in("xs", [WIN, D])
    wpk_d = [dt_in(f"wpk{l}", [128, PACK_EL]) for l in range(L)]
    cnst_d = dt_in("cnst", [128, 3 * 128 + 3 * 64 + 128 * 2])
    pmat_d = dt_in("pmat", [128, 7 * 4 * 128])
    small_d = dt_in("small", [128, 16 + L * 128])
    rope_d = dt_in("rope", [128, 2, WIN])
    kc_d = dt_in("kcT", [128, L, 256])
    vc_d = dt_in("vc", [128, L, 2, 128])
    yp_d = nc.dram_tensor("yp", [1024, D], F32, kind="ExternalOutput").ap()
    ys_d = nc.dram_tensor("ys", [2048, D], F32, kind="ExternalOutput").ap()
    nk_d = nc.dram_tensor("nk", [L, 1024, 128], F32, kind="ExternalOutput").ap()
    nv_d = nc.dram_tensor("nv", [L, 1024, 128], F32, kind="ExternalOutput").ap()
    wbf_d = [nc.dram_tensor(f"wbf{l}", [128, PACK_EL], BF16).ap() for l in range(L)]
    xsc_d = {"p": nc.dram_tensor("xscp", [128, 8, 1024], F32).ap(),
             "s": nc.dram_tensor("xscs", [128, 8, WIN], F32).ap()}

    with ExitStack() as st:
        P = Prog(nc, st)
        R = P.R

        def sb(name, shape, dt):
            return st.enter_context(nc.sbuf_tensor(name, list(shape), dt))

        ring_n = 5
        ring = sb("ring", [128, ring_n, SLAB], BF16)
        xTs = [sb("xTa", [128, 8, 512], F32), sb("xTb", [128, 8, 512], F32)]
        XS = {"i": 0, "pref": None}
        hT = sb("hT", [128, 8, 512], BF16)
        S16 = sb("S16", [128, 32, 512], BF16)
        S32 = sb("S32", [128, 14, 512], F32)

        def xtok(bi, lo=0, hi=D):
            return S32[:, 2 * bi:2 * bi + 2, :].rearrange("p a b -> p (a b)")[:, lo:hi]

        def xtok_res(bi):
            return [R("S32", 2 * bi), R("S32", 2 * bi + 1)]
        ua_c = sb("ua_c", [128, 2, WIN + 2 * PADC + 64], BF16)
        sc_c = sb("sc_c", [128, 2, WIN + 2 * PADC + 64], BF16)
        kT = sb("kT", [128, WIN], BF16)
        vtok = sb("vtok", [128, NB_S, 128], BF16)
        uptok = sb("uptok", [128, NB_S, 256], BF16)
        qT = sb("qT", [128, 4, 512], BF16)
        kvo = sb("kvo", [128, 1, 256], F32)
        ropeT = sb("ropeT", [128, 2, 512], F32)
        kcT = sb("kcTs", [128, L, 256], BF16)
        vcs = sb("vcs", [128, L, 2, 128], BF16)
        cn32 = sb("cn32", [128, 128], F32)
        cbf = sb("cbf", [128, 3 * 128 + 3 * 64 + 256], BF16)
        pmat = sb("pmat_sb", [128, 28, 128], BF16)
        small = sb("small_sb", [128, 16 + L * 128], F32)
        condb = sb("condb", [128, 8, 2], BF16)
        modv = sb("modv", [128, L, 48, 2], F32)
        dvec = sb("dvec", [128, L, 6, 8, 2], F32)
        diagA = sb("diagA", [128, 8, 128], BF16)
        diagC = sb("diagC", [128, 2, 3, 128], BF16)
        sinkf = sb("sinkf", [64, 8], F32)
        psb = [st.enter_context(nc.psum_tensor(f"ps{i}", [128, 512], F32)) for i in range(8)]

        ident32 = cn32
        identb = cbf[:, 0:128]
        maskP = cbf[:, 128:256]
        maskN = cbf[:, 256:384]
        ones64 = [cbf[:, 384 + i * 64:384 + (i + 1) * 64] for i in range(3)]
        r1024 = cbf[:, 576:704]
        r256 = cbf[:, 704:832]

        NCH = 8
        d_cast = [P.new_sem(f"d_cast{l}_{i}") for l in range(L) for i in range(NCH)]
        d_ring = [P.new_sem(f"d_ring{i}") for i in range(ring_n)]
        d_x = P.new_sem("d_x")
        d_xf = [P.new_sem("d_xf0"), P.new_sem("d_xf1")]
        d_scs = [P.new_sem("d_sc0"), P.new_sem("d_sc1")]
        d_xb = [P.new_sem(f"d_xb{i}") for i in range(4)]
        d_cs = [P.new_sem(f"d_c{i}") for i in range(8)]
        d_rope = P.new_sem("d_rope")
        d_sc = P.new_sem("d_sc")
        d_outb = [P.new_sem(f"d_out{i}") for i in range(4)]
        d_kvs = [[P.new_sem(f"d_kv{i}{j}") for j in range(2)] for i in range(1)]

        ps_ctr = {}
        dg_ctr = [0]

        def psum(lo=0, hi=8):
            k = (lo, hi)
            n = ps_ctr.get(k, 0)
            ps_ctr[k] = n + 1
            i = lo + (n % (hi - lo))
            r = R("ps", i)
            r.excl = True
            return psb[i], r

        ncn = 3 * 128 + 3 * 64 + 256
        P.dma("pool", d_cs[0], cbf[:], cnst_d[:, 0:ncn], writes=[R("cbf")])
        P.dma("act", d_cs[1], cn32[:], cnst_d[:, 0:128], writes=[R("cn32")])
        P.dma("pool", d_cs[2], pmat[:], pmat_d.rearrange("p (a b) -> p a b", b=128), writes=[R("pmat")])
        P.dma("act", d_cs[3], small[:], small_d, writes=[R("small")])
        P.dma("pool", d_cs[4], kcT[:], kc_d, writes=[R("kc")])
        P.dma("pool", d_cs[5], vcs[:], vc_d, writes=[R("vc")])
        P.op("pool", lambda e: e.memset(ua_c[:], 0.0), writes=[R("ua_c")])
        P.op("pool", lambda e: e.memset(sc_c[:], 0.0), writes=[R("sc_c")])
        ch = PACK_EL // NCH

        def issue_cast(l, i):
            hi_ = PACK_EL if i == NCH - 1 else (i + 1) * ch
            P.dma("pool", d_cast[l * NCH + i], wbf_d[l][:, i * ch:hi_], wpk_d[l][:, i * ch:hi_],
                  writes=[R("wbf", l, i)], max_dma_last_dim=16384)

        for i in range(NCH):
            issue_cast(0, i)
        l1_cast = list(range(NCH))

        def wbf_res(l, o, n):
            lo_i = min(NCH - 1, o // ch)
            hi_i = min(NCH - 1, (o + n - 1) // ch)
            return [R("wbf", l, i) for i in range(lo_i, hi_i + 1)]

        def sm(l, a, b):
            return small[:, 16 + l * 128 + a:16 + l * 128 + b]

        validL = small[:, 16 + 98:16 + 99]
        validR = small[:, 16 + 99:16 + 100]
        P.op("act", lambda e: e.activation(out=condb[:], in_=small[:, 0:16].rearrange("p (c t) -> p c t", t=2),
                                           func=AF.Silu), reads=[R("small")], writes=[R("condb")])

        seq = slab_sequence(do_sample, limit)
        wstate = {"next_load": 0, "pos": 0}

        def ensure_loaded(upto):
            while wstate["next_load"] <= upto and wstate["next_load"] < len(seq):
                s = wstate["next_load"]
                l, name = seq[s]
                o, kc, n = PACK_OFFS[name]
                slot = s % ring_n
                P.dma("sp", d_ring[slot], ring[:, slot, 0:kc * n], wbf_d[l][:, o:o + kc * n],
                      reads=wbf_res(l, o, kc * n), writes=[R("ring", slot)])
                wstate["next_load"] += 1

        def W(l, name):
            s = wstate["pos"]
            assert seq[s] == (l, name), (s, seq[s], l, name)
            wstate["pos"] += 1
            ensure_loaded(s + ring_n - 3)
            o, kc, n = PACK_OFFS[name]
            slot = s % ring_n
            return ring[:, slot, 0:kc * n].rearrange("p (k n) -> p k n", n=n), R("ring", slot)

        def mm_group(out_ap, out_res, items, extra_reads=(), signal_each=False):
            n = len(items)
            for i, (lhsT, rhs, rs) in enumerate(items):
                P.op("pe", lambda e, lhsT=lhsT, rhs=rhs, i=i: e.matmul(out_ap, lhsT=lhsT, rhs=rhs,
                                                                         start=(i == 0), stop=(i == n - 1)),
                     reads=list(rs) + list(extra_reads), writes=[out_res], signal=(signal_each or i == n - 1))

        def act(out, in_, func, reads, writes, **kw):
            P.op("act", lambda e: e.activation(out=out, in_=in_, func=func, **kw), reads=reads, writes=writes)

        def tt(eng, out, in0, in1, op, reads, writes):
            P.op(eng, lambda e: e.tensor_tensor(out=out, in0=in0, in1=in1, op=op), reads=reads, writes=writes)

        def stt(eng, out, in0, scalar, in1, op0, op1, reads, writes):
            P.op(eng, lambda e: e.scalar_tensor_tensor(out=out, in0=in0, scalar=scalar, in1=in1, op0=op0, op1=op1),
                 reads=reads, writes=writes)

        def layer_setup(l):
            ps, rps = psum()
            for i in range(12):
                w, rw = W(l, f"ada{i}")
                for mi in range(4):
                    m = i * 4 + mi
                    mm_group(ps[:, 2 * m:2 * m + 2], rps,
                             [(w[:, k, mi * 128:(mi + 1) * 128], condb[:, k, :], [rw, R("condb")]) for k in range(8)])
            bada = sm(l, 0, 48)
            P.op("dve", lambda e: e.tensor_tensor(out=modv[:, l], in0=ps[:, 0:96].rearrange("p (m t) -> p m t", t=2),
                                                  in1=bada.unsqueeze(2).to_broadcast([128, 48, 2]), op=ALU.add),
                 reads=[rps, R("small")], writes=[R("modv", l)])
            def gvec(i):
                return sm(l, 48 + 8 * i, 56 + 8 * i).unsqueeze(2).to_broadcast([128, 8, 2])
            mv = modv[:, l]
            dv = dvec[:, l]
            rd, wr = [R("modv", l), R("small")], [R("dvec", l)]
            stt("dve", dv[:, 0], mv[:, 8:16], 1.0, gvec(0), ALU.add, ALU.mult, rd, wr)
            P.op("dve", lambda e: e.tensor_copy(out=dv[:, 1], in_=mv[:, 0:8]), reads=rd, writes=wr)
            tt("dve", dv[:, 2], mv[:, 16:24], gvec(1), ALU.mult, rd, wr)
            stt("dve", dv[:, 3], mv[:, 32:40], 1.0, gvec(2), ALU.add, ALU.mult, rd, wr)
            P.op("dve", lambda e: e.tensor_copy(out=dv[:, 4], in_=mv[:, 24:32]), reads=rd, writes=wr)
            tt("dve", dv[:, 5], mv[:, 40:48], gvec(3), ALU.mult, rd, wr)
            wca = small_conv[l]
            for c in range(2):
                for k in range(3):
                    P.op("pool", lambda e, c=c, k=k: e.tensor_scalar(out=diagC[:, c, k, :], in0=ident32[:],
                                                                     scalar1=wca[:, c, 31 + k:32 + k], scalar2=None,
                                                                     op0=ALU.mult),
                         reads=[R("cn32"), R("convw")], writes=[R("diagC")])
            act(sinkf[:], sm(l, 90, 98)[0:64, :], AF.Exp, [R("small")], [R("sinkf")])

        convw_d = dt_in("convw", [128, L, 2, 34])
        convw = sb("convw_sb", [128, L, 2, 34], F32)
        P.dma("act", d_cs[6], convw[:], convw_d, writes=[R("convw")])
        small_conv = [convw[:, l] for l in range(L)]

        def norm_to_h(l, T, a_idx, b_idx, col, xres):
            ps, rps = psum()
            items = []
            SQE = ("act", "dve", "pool", "act", "dve", "act", "dve", "pool")
            for c in range(8):
                sq, rsq = S16[:, 8 + c, 0:T], R("S16", 8 + c)
                if DEBUG_STAGE != 212:
                    if SQE[c] == "act":
                        act(sq, XT()[:, c, 0:T], AF.Square, [xres[c]], [rsq])
                    else:
                        tt(SQE[c], sq, XT()[:, c, 0:T], XT()[:, c, 0:T], ALU.mult, [xres[c]], [rsq])
                if DEBUG_STAGE != 211:
                    P.op("pe", lambda e, sq=sq, c=c: e.matmul(ps[:, 0:T], lhsT=r1024, rhs=sq, start=(c == 0), stop=(c == 7)),
                         reads=[rsq, R("cbf")], writes=[rps], signal=True)
            if DEBUG_STAGE in (21, 211, 212):
                return
            rsq_, rr0 = S32[:, 12, 0:T], R("S32", 12)
            rstd, rr = S32[:, 13, 0:T], R("S32", 13)
            act(rsq_, ps[:, 0:T], AF.Sqrt, [rps], [rr0], bias=EPS, scale=1.0)
            if DEBUG_STAGE == 22:
                return
            P.op("dve", lambda e: e.reciprocal(out=rstd, in_=rsq_), reads=[rr0], writes=[rr])
            if DEBUG_STAGE == 23:
                return
            for c in range(8):
                tmp, rt = S32[:, 10 + (c % 2), 0:T], R("S32", 10 + (c % 2))
                eng = "dve"
                tt(eng, tmp, XT()[:, c, 0:T], rstd, ALU.mult, [xres[c], rr], [rt])
                act(hT[:, c, 0:T], tmp, AF.Identity, [rt, R("dvec", l)], [R("hT", c)],
                    scale=dvec[:, l, a_idx, c, col:col + 1], bias=dvec[:, l, b_idx, c, col:col + 1])

        def post_norm_residual(l, T, g_idx, col, mres, xres):
            ps, rps = psum()
            SQE = ("act", "dve", "pool", "act", "dve", "act", "dve", "pool")
            for c in range(8):
                sq, rsq = S16[:, 8 + c, 0:T], R("S16", 8 + c)
                if SQE[c] == "act":
                    act(sq, S32[:, c, 0:T], AF.Square, [mres[c]], [rsq])
                else:
                    tt(SQE[c], sq, S32[:, c, 0:T], S32[:, c, 0:T], ALU.mult, [mres[c]], [rsq])
                P.op("pe", lambda e, sq=sq, c=c: e.matmul(ps[:, 0:T], lhsT=r1024, rhs=sq, start=(c == 0), stop=(c == 7)),
                     reads=[rsq, R("cbf")], writes=[rps], signal=True)
            rsq_, rr0 = S32[:, 12, 0:T], R("S32", 12)
            rstd, rr = S32[:, 13, 0:T], R("S32", 13)
            act(rsq_, ps[:, 0:T], AF.Sqrt, [rps], [rr0], bias=EPS, scale=1.0)
            P.op("dve", lambda e: e.reciprocal(out=rstd, in_=rsq_), reads=[rr0], writes=[rr])
            for c in range(8):
                tt("dve", S32[:, c, 0:T], S32[:, c, 0:T], rstd, ALU.mult, [mres[c], rr], [mres[c]])
                stt("dve", XT()[:, c, 0:T], S32[:, c, 0:T], dvec[:, l, g_idx, c, col:col + 1],
                    XT()[:, c, 0:T], ALU.mult, ALU.add, [mres[c], R("dvec", l), xres[c]], [xres[c]])

        def XT():
            return xTs[XS["i"]]

        def issue_tok_dma(pas, blocks):
            src = xp_d if pas == "p" else xs_d
            for bi, b in enumerate(blocks):
                P.dma("act", d_xb[bi], xtok(bi), src[b * 128:(b + 1) * 128, :], writes=xtok_res(bi))

        def issue_feat_dma(pas, blocks, idx):
            T = len(blocks) * 128
            b0 = blocks[0]
            P.dma("act", d_xf[idx], xTs[idx][:, :, 0:T], xsc_d[pas][:, :, b0 * 128:b0 * 128 + T],
                  reads=[R("xsc", pas, 0), R("xsc", pas, 1)], writes=[R("xT", idx, c) for c in range(8)])

        def load_x(kind, pas, l, blocks, T, nxt):
            XS["i"] ^= 1
            cur = XS["i"]
            xres = [R("xT", cur, c) for c in range(8)]
            nb = len(blocks)
            key = (kind, pas, l, tuple(blocks))
            tokmode = (l == 0 and kind == "A")
            if XS["pref"] != key:
                if tokmode:
                    issue_tok_dma(pas, blocks)
                else:
                    issue_feat_dma(pas, blocks, cur)
            XS["pref"] = None
            if tokmode:
                for c in range(8):
                    ps, rps = psum()
                    for bi in range(nb):
                        P.op("pe", lambda e, bi=bi, c=c, ps=ps: e.matmul(ps[:, bi * 128:(bi + 1) * 128],
                                                                          lhsT=xtok(bi, c * 128, (c + 1) * 128), rhs=ident32[:],
                                                                          start=True, stop=True),
                             reads=xtok_res(bi) + [R("cn32")], writes=[rps], signal=(bi == nb - 1))
                    if c % 2 == 0:
                        act(XT()[:, c, 0:T], ps[:, 0:T], AF.Copy, [rps], [xres[c]])
                    else:
                        P.op("dve", lambda e, c=c, ps=ps, xb=XT(): e.tensor_copy(out=xb[:, c, 0:T], in_=ps[:, 0:T]),
                             reads=[rps], writes=[xres[c]])
                b0 = blocks[0]
                P.dma("act", d_scs[cur], xsc_d[pas][:, :, b0 * 128:b0 * 128 + T], XT()[:, :, 0:T],
                      reads=xres, writes=[R("xsc", pas, cur)])
            if nxt is not None:
                nkind, npas, nl, nblocks = nxt
                ntok = (nl == 0 and nkind == "A")
                if ntok and tokmode:
                    issue_tok_dma(npas, nblocks)
                    XS["pref"] = (nkind, npas, nl, tuple(nblocks))
                elif not ntok:
                    issue_feat_dma(npas, nblocks, cur ^ 1)
                    XS["pref"] = (nkind, npas, nl, tuple(nblocks))
            return xres

        def store_x(pas, l, blocks, T, xres):
            b0 = blocks[0]
            if l == 0:
                P.dma("act", d_scs[XS["i"]], xsc_d[pas][:, :, b0 * 128:b0 * 128 + T], XT()[:, :, 0:T],
                      reads=xres, writes=[R("xsc", pas, XS["i"])])
            else:
                dst = yp_d if pas == "p" else ys_d
                for bi, b in enumerate(blocks):
                    ob = b if pas == "p" else b - 2
                    for half in range(2):
                        ps, rps = psum()
                        for cc in range(4):
                            c = half * 4 + cc
                            P.op("pe", lambda e, bi=bi, c=c, cc=cc, ps=ps, xb=XT(): e.matmul(
                                ps[:, cc * 128:(cc + 1) * 128], lhsT=xb[:, c, bi * 128:(bi + 1) * 128], rhs=ident32[:],
                                start=True, stop=True),
                                 reads=[xres[c], R("cn32")], writes=[rps], signal=(cc == 3))
                        if half == 0:
                            act(xtok(bi, 0, 512), ps[:], AF.Copy, [rps], [R("S32", 2 * bi)])
                        else:
                            P.op("dve", lambda e, bi=bi, ps=ps: e.tensor_copy(out=xtok(bi, 512, 1024), in_=ps[:]),
                                 reads=[rps], writes=[R("S32", 2 * bi + 1)])
                    P.dma("act", d_outb[bi], dst[ob * 128:(ob + 1) * 128, :], xtok(bi), reads=xtok_res(bi))

        def ccol(pas, t):
            if pas == "p":
                s, r = t // 256, t % 256
                return s * (256 + 2 * PADC) + PADC + r
            return PADC + t

        def conv_rhs(buf, c, pas, blocks, T, shift):
            t0 = blocks[0] * 128
            if pas == "p" and T == 512:
                base = ccol(pas, t0) + shift
                return buf[:, c, base:base + 2 * (256 + 2 * PADC)].rearrange("p (s w) -> p s w", s=2)[:, :, 0:256]
            base = ccol(pas, t0) + shift
            return buf[:, c, base:base + T]

        def conv_dst(buf, c, pas, blocks, T):
            return conv_rhs(buf, c, pas, blocks, T, 0)

        def as_tile(ap, pas, T):
            if pas == "p" and T == 512:
                return ap.rearrange("p (s w) -> p s w", s=2)
            return ap

        def phase_A(pas, l, blocks, nxt=None):
            T = len(blocks) * 128
            col = 0 if pas == "p" else 1
            if DEBUG_STAGE == 14:
                return
            if DEBUG_STAGE == 12:
                P.op("pool", lambda e: e.memset(ua_c[:, :, 0:4 * (256 + 2 * PADC)], 0.0), writes=[R("ua_c")])
                return
            if pas == "p" and blocks[0] == 0 and DEBUG_STAGE != 13:
                P.op("pool", lambda e: e.memset(ua_c[:, :, 0:4 * (256 + 2 * PADC)], 0.0), writes=[R("ua_c")])
                P.op("pool", lambda e: e.memset(sc_c[:, :, 0:4 * (256 + 2 * PADC)], 0.0), writes=[R("sc_c")])
            xres = load_x("A", pas, l, blocks, T, nxt)
            norm_to_h(l, T, 0, 1, col, xres)
            if DEBUG_STAGE in (2, 21, 22, 23, 24, 211, 212):
                return
            hres = [R("hT", c) for c in range(8)]
            t0 = blocks[0] * 128
            if pas == "s":
                P.dma("act", d_rope, ropeT[:, :, 0:T], rope_d[:, :, t0:t0 + T], writes=[R("ropeT")])
            slabs = [W(l, "A0"), W(l, "A1"), W(l, "A2")]

            def proj(ci):
                w, rw = slabs[ci // 4]
                ps, rps = psum()
                mm_group(ps[:, 0:T], rps, [(w[:, k, (ci % 4) * 128:(ci % 4 + 1) * 128], hT[:, k, 0:T], [rw, hres[k]])
                                           for k in range(8)])
                return ps, rps

            for c in range(2):
                psg, rg = proj(c)
                sig, rs_ = S16[:, 10 + c, 0:T], R("S16", 10 + c)
                act(sig, psg[:, 0:T], AF.Sigmoid, [rg], [rs_])
                psv, rv = proj(2 + c)
                tt("dve", conv_dst(ua_c, c, pas, blocks, T), as_tile(psv[:, 0:T], pas, T), as_tile(sig, pas, T),
                   ALU.mult, [rv, rs_], [R("ua_c")])
            for c in range(2):
                psc, rc = proj(4 + c)
                tmp, rt = S32[:, 8 + c, 0:T], R("S32", 8 + c)
                act(tmp, psc[:, 0:T], AF.Copy, [rc], [rt])
                psh, rh = proj(6 + c)
                tt("dve", conv_dst(sc_c, c, pas, blocks, T), as_tile(psh[:, 0:T], pas, T), as_tile(tmp, pas, T),
                   ALU.mult, [rh, rt], [R("sc_c")])
            psk, rk = proj(8)
            if pas == "p":
                act(kT[:, t0:t0 + T], psk[:, 0:T], AF.Copy, [rk], [R("kT")])
            else:
                pss, rss = proj(9)
                t1, r1 = S32[:, 8, 0:T], R("S32", 8)
                t2, r2 = S32[:, 9, 0:T], R("S32", 9)
                tt("dve", t1, psk[:, 0:T], ropeT[:, 0, 0:T], ALU.mult, [rk, R("ropeT")], [r1])
                tt("dve", t2, pss[:, 0:T], ropeT[:, 1, 0:T], ALU.mult, [rss, R("ropeT")], [r2])
                tt("pool", kT[:, t0:t0 + T], t1, t2, ALU.add, [r1, r2], [R("kT")])
            if DEBUG_STAGE == 3:
                wstate["pos"] += 1
                return
            wt, rwt = W(l, "AT")
            for bi, b in enumerate(blocks):
                ps, rps = psum()
                ncol = 512 if pas == "p" else 384
                mm_group(ps[:, 0:ncol], rps, [(hT[:, k, bi * 128:(bi + 1) * 128], wt[:, k, 0:ncol], [rwt, hres[k]])
                                              for k in range(8)])
                vs = None
                if pas == "s" and b in (0, 1):
                    vs = validL
                if pas == "s" and b in (18, 19):
                    vs = validR
                if vs is None:
                    act(uptok[:, b, :], ps[:, 0:256], AF.Copy, [rps], [R("uptok")])
                    P.op("dve", lambda e, b=b, ps=ps: e.tensor_copy(out=vtok[:, b, :], in_=ps[:, 256:384]),
                         reads=[rps], writes=[R("vtok")])
                else:
                    act(uptok[:, b, :], ps[:, 0:256], AF.Identity, [rps, R("small")], [R("uptok")], scale=vs)
                    act(vtok[:, b, :], ps[:, 256:384], AF.Identity, [rps, R("small")], [R("vtok")], scale=vs)
                    for buf, nm in ((ua_c, "ua_c"), (sc_c, "sc_c")):
                        cs = ccol(pas, b * 128)
                        P.op("pool", lambda e, buf=buf, cs=cs, vs=vs: e.tensor_scalar(
                            out=buf[:, :, cs:cs + 128], in0=buf[:, :, cs:cs + 128], scalar1=vs, scalar2=None,
                            op0=ALU.mult), reads=[R(nm), R("small")], writes=[R(nm)])
                if pas == "p":
                    kb = 0
                    P.op("dve", lambda e, kb=kb, ps=ps: e.tensor_copy(out=kvo[:, kb, :], in_=ps[:, 256:512]),
                         reads=[rps], writes=[R("kvo", kb)])
                    P.dma("act", d_kvs[kb][0], nv_d[l, b * 128:(b + 1) * 128, :], kvo[:, kb, 0:128], reads=[R("kvo", kb)])
                    P.dma("act", d_kvs[kb][1], nk_d[l, b * 128:(b + 1) * 128, :], kvo[:, kb, 128:256], reads=[R("kvo", kb)])

        def phase_B(pas, l, blocks, nxt=None):
            T = len(blocks) * 128
            nb = len(blocks)
            col = 0 if pas == "p" else 1
            t0 = blocks[0] * 128
            xres = load_x("B", pas, l, blocks, T, nxt)
            norm_to_h(l, T, 0, 1, col, xres)
            hres = [R("hT", c) for c in range(8)]
            if pas == "s":
                P.dma("act", d_rope, ropeT[:, :, 0:T], rope_d[:, :, t0:t0 + T], writes=[R("ropeT")])
            slabs = [W(l, "B0"), W(l, "B1"), W(l, "B2")]

            def proj(ci):
                w, rw = slabs[ci // 4]
                ps, rps = psum()
                mm_group(ps[:, 0:T], rps, [(w[:, k, (ci % 4) * 128:(ci % 4 + 1) * 128], hT[:, k, 0:T], [rw, hres[k]])
                                           for k in range(8)])
                return ps, rps

            for c in range(2):
                ps, rps = proj(c)
                act(S32[:, 8 + c, 0:T], ps[:, 0:T], AF.Copy, [rps], [R("S32", 8 + c)])
            for c in range(4):
                psq, rq = proj(2 + c)
                if pas == "p":
                    act(qT[:, c, 0:T], psq[:, 0:T], AF.Copy, [rq], [R("qT", c)])
                else:
                    pss, rss = proj(6 + c)
                    t1, r1 = S32[:, 10, 0:T], R("S32", 10)
                    t2, r2 = S32[:, 11, 0:T], R("S32", 11)
                    tt("dve", t1, psq[:, 0:T], ropeT[:, 0, 0:T], ALU.mult, [rq, R("ropeT")], [r1])
                    tt("dve", t2, pss[:, 0:T], ropeT[:, 1, 0:T], ALU.mult, [rss, R("ropeT")], [r2])
                    tt("pool", qT[:, c, 0:T], t1, t2, ALU.add, [r1, r2], [R("qT", c)])
            uar = []
            for c in range(2):
                ps, rps = psum()
                for k in range(31):
                    di = dg_ctr[0] % 8
                    dg_ctr[0] += 1
                    P.op("dve", lambda e, c=c, k=k, di=di: e.tensor_scalar(
                        out=diagA[:, di, :], in0=ident32[:], scalar1=convw[:, l, c, k:k + 1], scalar2=None,
                        op0=ALU.mult), reads=[R("cn32"), R("convw")], writes=[R("diagA", di)])
                    P.op("pe", lambda e, c=c, k=k, di=di, ps=ps: e.matmul(
                        as_tile(ps[:, 0:T], pas, T), lhsT=diagA[:, di, :],
                        rhs=conv_rhs(ua_c, c, pas, blocks, T, k - 15), start=(k == 0), stop=(k == 30)),
                         reads=[R("diagA", di), R("ua_c")], writes=[rps], signal=True)
                u, ru = S32[:, c, 0:T], R("S32", c)
                act(u, ps[:, 0:T], AF.Identity, [rps, R("small")], [ru], bias=sm(l, 80 + c, 81 + c), scale=1.0)
                ub_, rub = S16[:, 18 + c, 0:T], R("S16", 18 + c)
                P.op("pool", lambda e, u=u, ub_=ub_: e.tensor_copy(out=ub_, in_=u), reads=[ru], writes=[rub])
                uar.append((u, ru, ub_, rub))
            psm, rpm = psum()
            mm_group(psm[:, 0:T], rpm, [(r256, uar[c][2], [uar[c][3], R("cbf")]) for c in range(2)])
            for c in range(2):
                u, ru, ub_, rub = uar[c]
                tt("dve", u, u, psm[:, 0:T], ALU.subtract, [ru, rpm], [ru])
                act(ub_, u, AF.Square, [ru], [rub])
            psv, rpv = psum()
            mm_group(psv[:, 0:T], rpv, [(r256, uar[c][2], [uar[c][3], R("cbf")]) for c in range(2)])
            rs_a0, rra0 = S32[:, 2, 0:T], R("S32", 2)
            rs_a, rra = S32[:, 13, 0:T], R("S32", 13)
            act(rs_a0, psv[:, 0:T], AF.Sqrt, [rpv], [rra0], bias=EPS, scale=1.0)
            P.op("dve", lambda e: e.reciprocal(out=rs_a, in_=rs_a0), reads=[rra0], writes=[rra])
            for c in range(2):
                u, ru, ub_, rub = uar[c]
                tt("dve" if c else "pool", u, u, rs_a, ALU.mult, [ru, rra], [ru])
                act(S16[:, 16 + c, 0:T], u, AF.Silu, [ru, R("small")], [R("S16", 16 + c)],
                    scale=sm(l, 82 + c, 83 + c), bias=sm(l, 84 + c, 85 + c))
            wp, rwp = W(l, "wpool")
            for g in range(4):
                ps, rps = psum()
                for bi, b in enumerate(blocks):
                    items = []
                    if pas == "p":
                        first = (b % 2 == 0)
                        kinds = [(b, 3 if first else 4)]
                        kinds.append((b + 1, 1) if first else (b - 1, 0))
                    else:
                        cur = 5 if b == 2 else (6 if b == 17 else 2)
                        kinds = [(b - 1, 0), (b, cur), (b + 1, 1)]
                    for (j, kind) in kinds:
                        items.append((uptok[:, j, g * 64:(g + 1) * 64], pmat[:, kind * 4 + g, :], [R("uptok"), R("pmat")]))
                    mm_group(ps[0:64, bi * 128:(bi + 1) * 128], rps, items)
                pl, rpl = S16[0:64, 8 + g, 0:T], R("S16", 8 + g)
                if g % 2:
                    P.op("dve", lambda e, pl=pl, ps=ps: e.tensor_copy(out=pl, in_=ps[0:64, 0:T]), reads=[rps], writes=[rpl])
                else:
                    act(pl, ps[0:64, 0:T], AF.Copy, [rps], [rpl])
            for c in range(2):
                ps, rps = psum()
                mm_group(as_tile(ps[:, 0:T], pas, T), rps,
                         [(diagC[:, c, k, :], conv_rhs(sc_c, c, pas, blocks, T, k - 1), [R("diagC"), R("sc_c")])
                          for k in range(3)])
                tt("dve", S16[:, 18 + c, 0:T], ps[:, 0:T], S32[:, 8 + c, 0:T], ALU.mult,
                   [rps, R("S32", 8 + c)], [R("S16", 18 + c)])
            for g in range(4):
                pl, rpl = S16[0:64, 8 + g, 0:T], R("S16", 8 + g)
                ps2, rps2 = psum()
                mm_group(ps2[0:64, 0:T], rps2, [(wp[0:64, 0, g * 64:(g + 1) * 64], pl, [rwp, rpl])])
                act(S16[0:64, 20 + g, 0:T], ps2[0:64, 0:T], AF.Identity, [rps2, R("small")], [R("S16", 20 + g)],
                    scale=sm(l, 86 + g, 87 + g)[0:64, :])
            jobs = []
            for bi, b in enumerate(blocks):
                if pas == "p":
                    s0 = (b // 2) * 2
                    keys = [("l", s0, None, 0), ("l", s0 + 1, None, 0)]
                else:
                    keys = []
                    for j, mk in ((b - 1, maskP), (b, None), (b + 1, maskN)):
                        oi = 1 if j in (0, 1) else (2 if j in (18, 19) else 0)
                        keys.append(("l", j, mk, oi))
                    keys += [("c", 0, None, 0), ("c", 1, None, 0)]
                for g in range(2):
                    for ki, ks in enumerate(keys):
                        jobs.append((bi, g, ki, len(keys), ks))

            def kv_of(job):
                bi, g, ki, nk, (kind, j, mk, oi) = job
                if kind == "l":
                    return (kT[g * 64:(g + 1) * 64, j * 128:(j + 1) * 128], R("kT"),
                            vtok[:, j, g * 64:(g + 1) * 64], R("vtok"))
                return (kcT[g * 64:(g + 1) * 64, l, j * 128:(j + 1) * 128], R("kc"),
                        vcs[:, l, j, g * 64:(g + 1) * 64], R("vc"))

            def issue_S(job):
                bi, g, ki, nk, ks = job
                klhs, kres, _, _ = kv_of(job)
                pS, rpS = psum(0, 4)
                qr = qT[g * 64:(g + 1) * 64, :, bi * 128:(bi + 1) * 128]
                mm_group(pS[:].rearrange("p (h q) -> p h q", h=4), rpS,
                         [(klhs, qr, [kres] + [R("qT", c) for c in range(4)])])
                return pS, rpS

            LA = 2
            pending = []
            inflight = {}
            for ji in range(min(LA, len(jobs))):
                inflight[ji] = issue_S(jobs[ji])
            acc = None
            for ji, job in enumerate(jobs):
                bi, g, ki, nk, (kind, j, mk, oi) = job
                if ki == 0:
                    acc = (psum(4, 6), psum(6, 8))
                (po, rpo), (pd, rpd) = acc
                pS, rpS = inflight.pop(ji)
                _, _, vlhs, vres = kv_of(job)
                es = 12 + (ji % 4)
                E, rE = S16[:, es, :], R("S16", es)
                act(E, pS[:], AF.Exp, [rpS], [rE], scale=0.125)
                if mk is not None:
                    E3 = E.rearrange("p (h q) -> p h q", h=4)
                    tt("pool", E3, E3, mk.unsqueeze(1).to_broadcast([128, 4, 128]), ALU.mult, [rE, R("cbf")], [rE])
                if ji + LA < len(jobs):
                    inflight[ji + LA] = issue_S(jobs[ji + LA])
                P.op("pe", lambda e, vlhs=vlhs, E=E, ki=ki, po=po, nk=nk: e.matmul(
                    po[0:64, :], lhsT=vlhs, rhs=E, start=(ki == 0), stop=(ki == nk - 1)),
                     reads=[vres, rE], writes=[rpo], signal=False)
                P.op("pe", lambda e, oi=oi, E=E, ki=ki, pd=pd, nk=nk: e.matmul(
                    pd[0:64, :], lhsT=ones64[oi], rhs=E, start=(ki == 0), stop=(ki == nk - 1)),
                     reads=[R("cbf"), rE], writes=[rpd, rpo], signal=True)
                while pending and pending[0][0] <= ji:
                    pending.pop(0)[1]()
                if ki == nk - 1:
                  def normalize(bi=bi, g=g, po=po, rpo=rpo, pd=pd, rpd=rpd):
                    den, rden = S32[0:64, 3 + g, :], R("S32", 3 + g)
                    rec, rrec = S32[0:64, 5 + g, :], R("S32", 5 + g)
                    tt("dve", den.rearrange("p (h q) -> p h q", h=4), pd[0:64, :].rearrange("p (h q) -> p h q", h=4),
                       sinkf[:, g * 4:(g + 1) * 4].unsqueeze(2).to_broadcast([64, 4, 128]), ALU.add,
                       [rpd, R("sinkf")], [rden])
                    act(den, den, AF.Ln, [rden], [rden])
                    act(rec, den, AF.Exp, [rden], [rrec], scale=-1.0)
                    for hh in range(4):
                        slot = 24 + g * 4 + hh
                        tt("dve", S16[0:64, slot, bi * 128:(bi + 1) * 128], po[0:64, hh * 128:(hh + 1) * 128],
                           rec[:, hh * 128:(hh + 1) * 128], ALU.mult, [rpo, rrec], [R("S16", slot)])
                  pending.append((ji + 2, normalize))
            while pending:
                pending.pop(0)[1]()
            for c in range(8):
                wg, rwg = W(l, f"g{c}")
                wb, rwb = W(l, f"br{c}")
                gsl = []
                for b_ in range(4):
                    ps, rps = psum()
                    mm_group(ps[:, 0:T], rps, [(wg[:, k, b_ * 128:(b_ + 1) * 128], hT[:, k, 0:T], [rwg, hres[k]])
                                               for k in range(8)])
                    gs = 8 + (c % 2) * 4 + b_
                    act(S16[:, gs, 0:T], ps[:, 0:T], AF.Sigmoid, [rps], [R("S16", gs)])
                    gsl.append(gs)
                brs = []
                ps, rps = psum()
                mm_group(ps[:, 0:T], rps, [(wb[:, 0, k * 128:(k + 1) * 128], S16[:, 16 + k, 0:T], [rwb, R("S16", 16 + k)])
                                           for k in range(2)])
                brs.append((ps, rps))
                ps, rps = psum()
                mm_group(ps[:, 0:T], rps, [(wb[0:64, 0, 256 + g * 128:256 + (g + 1) * 128], S16[0:64, 20 + g, 0:T],
                                            [rwb, R("S16", 20 + g)]) for g in range(4)])
                brs.append((ps, rps))
                ps, rps = psum()
                mm_group(ps[:, 0:T], rps, [(wb[:, 0, 768 + k * 128:768 + (k + 1) * 128], S16[:, 18 + k, 0:T],
                                            [rwb, R("S16", 18 + k)]) for k in range(2)])
                brs.append((ps, rps))
                ps, rps = psum()
                mm_group(ps[:, 0:T], rps, [(wb[0:64, 0, 1024 + h * 128:1024 + (h + 1) * 128], S16[0:64, 24 + h, 0:T],
                                            [rwb, R("S16", 24 + h)]) for h in range(8)])
                brs.append((ps, rps))
                acc, racc = S32[:, 5 + (c % 2), 0:T], R("S32", 5 + (c % 2))
                tmp, rtmp = S32[:, 7, 0:T], R("S32", 7)
                tt("dve", acc, brs[0][0][:, 0:T], S16[:, gsl[0], 0:T], ALU.mult, [brs[0][1], R("S16", gsl[0])], [racc])
                for b_ in range(1, 4):
                    tt("dve", tmp, brs[b_][0][:, 0:T], S16[:, gsl[b_], 0:T], ALU.mult,
                       [brs[b_][1], R("S16", gsl[b_])], [rtmp])
                    if b_ < 3:
                        tt("pool", acc, acc, tmp, ALU.add, [racc, rtmp], [racc])
                    else:
                        tt("pool", S16[:, c, 0:T], acc, tmp, ALU.add, [racc, rtmp], [R("S16", c)])
            wos = [W(l, "wo0"), W(l, "wo1")]
            mres = [R("S32", c) for c in range(8)]
            for c in range(8):
                w, rw = wos[c // 4]
                ps, rps = psum()
                mm_group(ps[:, 0:T], rps, [(w[:, k, (c % 4) * 128:(c % 4 + 1) * 128], S16[:, k, 0:T], [rw, R("S16", k)])
                                           for k in range(8)])
                act(S32[:, c, 0:T], ps[:, 0:T], AF.Copy, [rps], [mres[c]])
            post_norm_residual(l, T, 2, col, mres, xres)
            norm_to_h(l, T, 3, 4, col, xres)
            for i in range(8):
                w, rw = W(l, f"f1_{i}")
                for ci in range(4):
                    j = i * 4 + ci
                    ps, rps = psum()
                    mm_group(ps[:, 0:T], rps, [(w[:, k, ci * 128:(ci + 1) * 128], hT[:, k, 0:T], [rw, hres[k]])
                                               for k in range(8)])
                    tmp, rtmp = S32[:, 8 + (j % 4), 0:T], R("S32", 8 + (j % 4))
                    act(tmp, ps[:, 0:T], AF.Relu, [rps], [rtmp])
                    tt("dve" if j % 2 else "pool", S16[:, j, 0:T], tmp, tmp, ALU.mult, [rtmp], [R("S16", j)])
            for c in range(8):
                w, rw = W(l, f"f2_{c}")
                ps, rps = psum()
                mm_group(ps[:, 0:T], rps, [(w[:, k, :], S16[:, k, 0:T], [rw, R("S16", k)]) for k in range(32)])
                act(S32[:, c, 0:T], ps[:, 0:T], AF.Copy, [rps], [mres[c]])
            post_norm_residual(l, T, 5, col, mres, xres)
            store_x(pas, l, blocks, T, xres)

        sched_all = schedule(do_sample)[:limit]
        for si, (kind, pas, l, blocks) in enumerate(sched_all):
            if kind == "setup" and l == 1:
                while l1_cast:
                    issue_cast(1, l1_cast.pop(0))
            if l == 0 and si >= 4 and l1_cast:
                issue_cast(1, l1_cast.pop(0))
            if kind == "setup":
                layer_setup(l)
            else:
                nxt = None
                for it in sched_all[si + 1:]:
                    if it[0] != "setup":
                        nxt = it
                        break
                (phase_A if kind == "A" else phase_B)(pas, l, blocks, nxt)
        assert DEBUG_STAGE or wstate["pos"] == len(seq), (wstate["pos"], len(seq))
        P.wait_all("act", d_outb + d_kvs[0] + [d_sc, d_x, d_rope] + d_xf + d_scs + d_xb + d_cs)
        P.wait_all("sp", d_ring + d_cast)
        P.wait_all("pool", d_cast + d_cs)
        print("sbuf bytes remaining", nc.sbuf_bytes_remaining)
        print("instructions", P.n_inst, "waits", P.n_wait)
        P.emit()
    return nc


def tiles_of(blocks):
    return [blocks[i:i + 4] for i in range(0, len(blocks), 4)]


def schedule(do_sample):
    out = []
    for l in range(L):
        out.append(("setup", None, l, None))
        for t in tiles_of(list(range(8))):
            out.append(("A", "p", l, t))
        for t in tiles_of(list(range(8))):
            out.append(("B", "p", l, t))
        if do_sample:
            lo, hi = (0, 20) if l == 0 else (1, 19)
            for t in tiles_of(list(range(lo, hi))):
                out.append(("A", "s", l, t))
            lo, hi = (1, 19) if l == 0 else (2, 18)
            for t in tiles_of(list(range(lo, hi))):
                out.append(("B", "s", l, t))
    return out


def slab_sequence(do_sample, limit=None):
    seq = []
    for (kind, pas, l, blocks) in schedule(do_sample)[:limit]:
        if kind == "setup":
            seq += [(l, f"ada{i}") for i in range(12)]
        elif kind == "A":
            seq += [(l, "A0"), (l, "A1"), (l, "A2"), (l, "AT")]
        else:
            seq += [(l, "B0"), (l, "B1"), (l, "B2"), (l, "wpool")]
            for c in range(8):
                seq += [(l, f"g{c}"), (l, f"br{c}")]
            seq += [(l, "wo0"), (l, "wo1")]
            seq += [(l, f"f1_{i}") for i in range(8)]
            seq += [(l, f"f2_{c}") for c in range(8)]
    return seq


_NC_CACHE = {}


def host_inputs(inp, core):
    b, half = core // 2, core % 2
    start = half * 2048
    m = {}
    m["xp"] = np.ascontiguousarray(inp["x_prompt"][4 * core:4 * core + 4].reshape(1024, D))
    xs = np.zeros((WIN, D), np.float32)
    lo, hi = start - 256, start + 2304
    slo, shi = max(lo, 0), min(hi, 4096)
    xs[slo - lo:shi - lo] = inp["x_sample"][b, slo:shi]
    m["xs"] = xs
    validL = 1.0 if lo >= 0 else 0.0
    validR = 1.0 if hi <= 4096 else 0.0
    cn = np.zeros((128, 3 * 128 + 3 * 64 + 256), np.float32)
    cn[:, 0:128] = np.eye(128)
    kk = np.arange(128)[:, None]
    qq = np.arange(128)[None, :]
    cn[:, 128:256] = (kk >= qq)
    cn[:, 256:384] = (kk <= qq)
    cn[:, 384:448] = 1.0
    cn[:, 448:512] = validL
    cn[:, 512:576] = validR
    cn[:, 576:704] = 1.0 / 1024
    cn[:, 704:832] = 1.0 / 256
    m["cnst"] = cn
    PM = pool_mats()
    pm = np.zeros((7, 4, 128, 128), np.float32)
    pm[0:5] = PM
    pm[5] = PM[3] if half == 0 else PM[2]
    pm[6] = PM[4] if half == 1 else PM[2]
    m["pmat"] = np.ascontiguousarray(pm.reshape(28, 128, 128).transpose(1, 0, 2).reshape(128, 28 * 128))
    sm = np.zeros((128, 16 + L * 128), np.float32)
    cond = np.stack([inp["c_ctx"], inp["c"][b]], axis=1)
    sm[:, 0:16] = cond.reshape(8, 128, 2).transpose(1, 0, 2).reshape(128, 16)
    for l in range(L):
        o = 16 + l * 128
        sm[:, o:o + 48] = inp["b_ada"][l].reshape(48, 128).T
        for i, nm in enumerate(("g_pre_mix", "g_post_mix", "g_pre_ffn", "g_post_ffn")):
            sm[:, o + 48 + 8 * i:o + 56 + 8 * i] = inp[nm][l].reshape(8, 128).T
        sm[:, o + 80:o + 82] = inp["b_conv_a"][l].reshape(2, 128).T
        sm[:, o + 82:o + 84] = inp["ln_g_a"][l].reshape(2, 128).T
        sm[:, o + 84:o + 86] = inp["ln_b_a"][l].reshape(2, 128).T
        sm[0:64, o + 86:o + 90] = inp["pool_scale"][l].reshape(4, 64).T
        sm[:, o + 90:o + 98] = inp["sink"][l][None, :]
    sm[:, 16 + 98] = validL
    sm[:, 16 + 99] = validR
    m["small"] = sm
    cw = np.zeros((128, L, 2, 34), np.float32)
    for l in range(L):
        cw[:, l, :, 0:31] = inp["w_conv_a"][l].reshape(31, 2, 128).transpose(2, 1, 0)
        cw[:, l, :, 31:34] = inp["w_sc"][l].reshape(3, 2, 128).transpose(2, 1, 0)
    m["convw"] = cw
    pos = np.arange(lo, hi)
    posc = np.clip(pos, 0, 4095)
    row, colp = posc // 64, posc % 64
    p = np.arange(128)
    d = p % 64
    r = d % 32
    fi = (r % 16).astype(np.float64)
    freq = 10000.0 ** (-fi / 16.0)
    psel = np.where((d < 32)[:, None], row[None, :], colp[None, :]).astype(np.float64)
    ang = psel * freq[:, None]
    sgn = np.where(r < 16, -1.0, 1.0)[:, None]
    rp = np.zeros((128, 2, WIN), np.float32)
    rp[:, 0] = np.cos(ang)
    rp[:, 1] = np.sin(ang) * sgn
    m["rope"] = rp
    m["kcT"] = np.ascontiguousarray(inp["cache_k"][b].reshape(L, 256, 128).transpose(2, 0, 1))
    m["vc"] = np.ascontiguousarray(inp["cache_v"][b].reshape(L, 2, 128, 128).transpose(2, 0, 1, 3))
    return m


def kernel(**inputs):
    inp = {k: np.asarray(v) for k, v in inputs.items()}
    if "nc" not in _NC_CACHE:
        _NC_CACHE["nc"] = build_program(True)
    nc = _NC_CACHE["nc"]
    packs = [host_pack_layer(inp, l) for l in range(L)]
    in_maps = []
    for core in range(8):
        m = host_inputs(inp, core)
        for l in range(L):
            m[f"wpk{l}"] = packs[l]
        in_maps.append(m)
    res = run_bass_kernel_spmd(nc, in_maps, core_ids=list(range(8)))
    rs = res.results
    yp = np.concatenate([rs[c]["yp"].reshape(4, 256, D) for c in range(8)], axis=0)
    ys = np.stack([np.concatenate([rs[2 * b]["ys"], rs[2 * b + 1]["ys"]], axis=0) for b in range(4)], axis=0)
    nk = np.concatenate([rs[c]["nk"].reshape(L, 4, 256, 2, 64).transpose(1, 0, 2, 3, 4) for c in range(8)], axis=0)
    nv = np.concatenate([rs[c]["nv"].reshape(L, 4, 256, 2, 64).transpose(1, 0, 2, 3, 4) for c in range(8)], axis=0)
    return (yp.astype(np.float32), ys.astype(np.float32), nk.astype(np.float32), nv.astype(np.float32))
```

```python
from contextlib import ExitStack
import numpy as np
import concourse.bass as bass
import concourse.mybir as mybir
from concourse.bass_utils import run_bass_kernel_spmd

F32 = mybir.dt.float32
BF16 = mybir.dt.bfloat16
AF = mybir.ActivationFunctionType
ALU = mybir.AluOpType

ENGS = ("pe", "act", "dve", "pool", "sp")
DEBUG_STAGE = 0
D = 1024
L = 2
NB_S = 20
WIN = NB_S * 128
PADC = 16
EPS = 1e-6
SLAB = 4096


class Sem:
    __slots__ = ("h", "count", "name")

    def __init__(self, h, name):
        self.h = h
        self.count = 0
        self.name = name


class Res:
    __slots__ = ("name", "w", "r", "excl")

    def __init__(self, name=""):
        self.name = name
        self.w = None
        self.r = {}
        self.excl = False


class Prog:
    def __init__(self, nc, stack):
        self.nc = nc
        self.stack = stack
        self.streams = {e: [] for e in ENGS}
        self.esem = {e: self.new_sem("eng_" + e) for e in ENGS}
        self.waited = {e: {} for e in ENGS}
        self.n_inst = 0
        self.n_wait = 0
        self._res = {}

    def new_sem(self, name):
        return Sem(self.stack.enter_context(self.nc.semaphore(name)), name)

    def R(self, *key):
        r = self._res.get(key)
        if r is None:
            r = Res(str(key))
            self._res[key] = r
        return r

    def _collect(self, eng, reads, writes):
        deps = {}

        def add(t, kind):
            if t is None:
                return
            s, v, e = t
            if e == eng and eng == "pe":
                return
            if deps.get(s, 0) < v:
                deps[s] = v

        for r in reads:
            add(r.w, "raw")
            if r.excl:
                for s, (v, e) in r.r.items():
                    if e != eng:
                        add((s, v, e), "rar")
        for w in writes:
            add(w.w, "waw")
            for s, (v, e) in w.r.items():
                add((s, v, e), "war")
        waits = []
        wd = self.waited[eng]
        for s, v in deps.items():
            if wd.get(s, 0) < v:
                wd[s] = v
                waits.append((s.h, v))
        return waits

    def _mark(self, ticket, reads, writes):
        s, v, e = ticket
        for r in reads:
            cur = r.r.get(s)
            if cur is None or cur[0] < v:
                r.r[s] = (v, e)
        for w in writes:
            w.w = ticket
            w.r = {}

    def op(self, eng, fn, reads=(), writes=(), signal=True):
        waits = self._collect(eng, reads, writes)
        sem = self.esem[eng]
        ticket = (sem, sem.count + 1, eng)
        if signal:
            sem.count += 1
        self.streams[eng].append((waits, fn, (sem.h, 1) if signal else None))
        self._mark(ticket, reads, writes)
        self.n_inst += 1
        self.n_wait += len(waits)
        return ticket

    def dma(self, eng, dsem, out_ap, in_ap, reads=(), writes=(), **kw):
        waits = self._collect(eng, reads, writes)
        dsem.count += 16
        ticket = (dsem, dsem.count, "dma")

        def fn(e, out_ap=out_ap, in_ap=in_ap):
            return e.dma_start(out=out_ap, in_=in_ap, **kw)

        self.streams[eng].append((waits, fn, (dsem.h, 16)))
        self._mark(ticket, reads, writes)
        self.n_inst += 1
        self.n_wait += len(waits)
        return ticket

    def wait_all(self, eng, sems):
        waits = [(s.h, s.count) for s in sems if s.count > 0]
        self.streams[eng].append((waits, None, None))

    def emit(self):
        streams = self.streams

        def run(e_obj, lst):
            for waits, fn, inc in lst:
                for (h, v) in waits:
                    e_obj.wait_ge(h, v)
                if fn is None:
                    continue
                ins = fn(e_obj)
                if inc is not None:
                    ins.then_inc(inc[0], inc[1])

        with self.nc.Block() as block:
            @block.tensor
            def _(t):
                run(t, streams["pe"])

            @block.scalar
            def _(s):
                run(s, streams["act"])

            @block.vector
            def _(v):
                run(v, streams["dve"])

            @block.gpsimd
            def _(g):
                run(g, streams["pool"])

            @block.sync
            def _(sp):
                run(sp, streams["sp"])


def pack_spec():
    spec = []
    for i in range(12):
        spec.append((f"ada{i}", 8, 512))
    spec += [("A0", 8, 512), ("A1", 8, 512), ("A2", 8, 256), ("AT", 8, 512)]
    spec += [("B0", 8, 512), ("B1", 8, 512), ("B2", 8, 256)]
    spec.append(("wpool", 1, 256))
    for c in range(8):
        spec.append((f"g{c}", 8, 512))
        spec.append((f"br{c}", 1, 2048))
    spec += [("wo0", 8, 512), ("wo1", 8, 512)]
    for i in range(8):
        spec.append((f"f1_{i}", 8, 512))
    for c in range(8):
        spec.append((f"f2_{c}", 32, 128))
    offs = {}
    o = 0
    for name, kc, n in spec:
        offs[name] = (o, kc, n)
        o += kc * n
    return spec, offs, o


PACK_SPEC, PACK_OFFS, PACK_EL = pack_spec()


def rope_swap_idx():
    idx = np.zeros(64, np.int64)
    for d in range(64):
        blk, r = d // 32, d % 32
        partner = r + 16 if r < 16 else r - 16
        idx[d] = blk * 32 + partner
    return idx


def host_pack_layer(inp, l):
    w_in = inp["w_in"][l]
    out = np.zeros((128, PACK_EL), np.float32)

    def put(name, arr):
        o, kc, n = PACK_OFFS[name]
        assert arr.shape == (128, kc, n), (name, arr.shape)
        out[:, o:o + kc * n] = arr.reshape(128, kc * n)

    def kmaj(w):
        K, n = w.shape
        return np.ascontiguousarray(w.reshape(K // 128, 128, n).transpose(1, 0, 2))

    wada = inp["w_ada"][l]
    for i in range(12):
        put(f"ada{i}", kmaj(wada[:, i * 512:(i + 1) * 512]))
    sw = rope_swap_idx()
    kcols = np.arange(2048, 2176)
    kscols = np.concatenate([2048 + g * 64 + sw for g in range(2)])
    colsA = np.concatenate([np.arange(256, 512), np.arange(0, 256), np.arange(1024, 1280),
                            np.arange(1280, 1536), kcols, kscols])
    WA = w_in[:, colsA]
    put("A0", kmaj(WA[:, 0:512])); put("A1", kmaj(WA[:, 512:1024])); put("A2", kmaj(WA[:, 1024:1280]))
    colsT = np.concatenate([np.arange(512, 768), np.arange(2176, 2304), np.arange(2048, 2176)])
    put("AT", kmaj(w_in[:, colsT]))
    qcols = np.concatenate([np.concatenate([1536 + i * 64 + np.arange(64), 1536 + (4 + i) * 64 + np.arange(64)])
                            for i in range(4)])
    qscols = np.concatenate([np.concatenate([1536 + i * 64 + sw, 1536 + (4 + i) * 64 + sw]) for i in range(4)])
    colsB = np.concatenate([np.arange(768, 1024), qcols, qscols])
    WB = w_in[:, colsB]
    put("B0", kmaj(WB[:, 0:512])); put("B1", kmaj(WB[:, 512:1024])); put("B2", kmaj(WB[:, 1024:1280]))
    wp = np.zeros((128, 1, 256), np.float32)
    for g in range(4):
        wp[0:64, 0, g * 64:(g + 1) * 64] = inp["w_pool"][l, g]
    put("wpool", wp)
    for c in range(8):
        gcols = np.concatenate([2304 + b * 1024 + c * 128 + np.arange(128) for b in range(4)])
        put(f"g{c}", kmaj(w_in[:, gcols]))
        br = np.zeros((128, 1, 2048), np.float32)
        cs = slice(c * 128, (c + 1) * 128)
        for k in range(2):
            br[:, 0, k * 128:(k + 1) * 128] = inp["w_a_out"][l, k * 128:(k + 1) * 128, cs]
            br[:, 0, 768 + k * 128:768 + (k + 1) * 128] = inp["w_c_out"][l, k * 128:(k + 1) * 128, cs]
        for g in range(4):
            br[0:64, 0, 256 + g * 128:256 + (g + 1) * 128] = inp["w_b_out"][l, g * 64:(g + 1) * 64, cs]
        for h in range(8):
            br[0:64, 0, 1024 + h * 128:1024 + (h + 1) * 128] = inp["w_d_out"][l, h * 64:(h + 1) * 64, cs]
        put(f"br{c}", br)
    wo = inp["w_o"][l]
    put("wo0", kmaj(wo[:, 0:512])); put("wo1", kmaj(wo[:, 512:1024]))
    w1 = inp["w_ff1"][l]
    for i in range(8):
        put(f"f1_{i}", kmaj(w1[:, i * 512:(i + 1) * 512]))
    w2 = inp["w_ff2"][l]
    for c in range(8):
        put(f"f2_{c}", kmaj(w2[:, c * 128:(c + 1) * 128]))
    return out


def pool_mats():
    sizes = (2, 4, 8, 16)
    M = np.zeros((5, 4, 128, 128), np.float64)
    for g, w in enumerate(sizes):
        hw = w // 2
        for t in range(128):
            for kind in range(5):
                for tp_rel in range(t - hw, t + hw):
                    if kind in (0, 1, 2):
                        cnt = w
                    elif kind == 3:
                        cnt = (t + hw) - max(t - hw, 0)
                    else:
                        cnt = min(t + hw, 128) - (t - hw)
                    if kind == 0 and tp_rel < 0:
                        M[kind, g, tp_rel + 128, t] += 1.0 / cnt
                    elif kind == 1 and tp_rel >= 128:
                        M[kind, g, tp_rel - 128, t] += 1.0 / cnt
                    elif kind >= 2 and 0 <= tp_rel < 128:
                        M[kind, g, tp_rel, t] += 1.0 / cnt
                if kind >= 2:
                    M[kind, g, t, t] -= 1.0
    return M.astype(np.float32)


def build_program(do_sample=True, limit=None):
    nc = bass.Bass("TRN2", target_bir_lowering=False)
    dt_in = lambda name, shape: nc.dram_tensor(name, list(shape), F32, kind="ExternalInput").ap()
    xp_d = dt_in("xp", [1024, D])
    xs_d = dt_in("xs", [WIN, D])
    wpk_d = [dt_in(f"wpk{l}", [128, PACK_EL]) for l in range(L)]
    cnst_d = dt_in("cnst", [128, 3 * 128 + 3 * 64 + 128 * 2])
    pmat_d = dt_in("pmat", [128, 7 * 4 * 128])
    small_d = dt_in("small", [128, 16 + L * 128])
    rope_d = dt_in("rope", [128, 2, WIN])
    kc_d = dt_in("kcT", [128, L, 256])
    vc_d = dt_in("vc", [128, L, 2, 128])
    yp_d = nc.dram_tensor("yp", [1024, D], F32, kind="ExternalOutput").ap()
    ys_d = nc.dram_tensor("ys", [2048, D], F32, kind="ExternalOutput").ap()
    nk_d = nc.dram_tensor("nk", [L, 1024, 128], F32, kind="ExternalOutput").ap()
    nv_d = nc.dram_tensor("nv", [L, 1024, 128], F32, kind="ExternalOutput").ap()
    wbf_d = [nc.dram_tensor(f"wbf{l}", [128, PACK_EL], BF16).ap() for l in range(L)]
    xsc_d = {"p": nc.dram_tensor("xscp", [128, 8, 1024], F32).ap(),
             "s": nc.dram_tensor("xscs", [128, 8, WIN], F32).ap()}

    with ExitStack() as st:
        P = Prog(nc, st)
        R = P.R

        def sb(name, shape, dt):
            return st.enter_context(nc.sbuf_tensor(name, list(shape), dt))

        ring_n = 5
        ring = sb("ring", [128, ring_n, SLAB], BF16)
        xTs = [sb("xTa", [128, 8, 512], F32), sb("xTb", [128, 8, 512], F32)]
        XS = {"i": 0, "cur": 0, "pref": None, "early": None, "pstore": None, "pprefetch": None}
        hT = sb("hT", [128, 8, 512], BF16)
        S16 = sb("S16", [128, 32, 512], BF16)
        S32 = sb("S32", [128, 14, 512], F32)

        def xtok(bi, lo=0, hi=D):
            return S32[:, 2 * bi:2 * bi + 2, :].rearrange("p a b -> p (a b)")[:, lo:hi]

        def xtok_res(bi):
            return [R("S32", 2 * bi), R("S32", 2 * bi + 1)]
        ua_c = sb("ua_c", [128, 2, WIN + 2 * PADC + 64], BF16)
        sc_c = sb("sc_c", [128, 2, WIN + 2 * PADC + 64], BF16)
        kT = sb("kT", [128, WIN], BF16)
        vtok = sb("vtok", [128, NB_S, 128], BF16)
        uptok = sb("uptok", [128, NB_S, 256], BF16)
        qT = sb("qT", [128, 4, 512], BF16)
        kvo = sb("kvo", [128, 1, 256], F32)
        ropeT = sb("ropeT", [128, 2, 512], F32)
        kcT = sb("kcTs", [128, L, 256], BF16)
        vcs = sb("vcs", [128, L, 2, 128], BF16)
        cn32 = sb("cn32", [128, 128], F32)
        cbf = sb("cbf", [128, 3 * 128 + 3 * 64 + 256], BF16)
        pmat = sb("pmat_sb", [128, 28, 128], BF16)
        small = sb("small_sb", [128, 16 + L * 128], F32)
        condb = sb("condb", [128, 8, 2], BF16)
        modv = sb("modv", [128, L, 48, 2], F32)
        dvec = sb("dvec", [128, L, 6, 8, 2], F32)
        diagA = sb("diagA", [128, 8, 128], BF16)
        diagC = sb("diagC", [128, 2, 3, 128], BF16)
        sinkf = sb("sinkf", [64, 8], F32)
        psb = [st.enter_context(nc.psum_tensor(f"ps{i}", [128, 512], F32)) for i in range(8)]

        ident32 = cn32
        identb = cbf[:, 0:128]
        maskP = cbf[:, 128:256]
        maskN = cbf[:, 256:384]
        ones64 = [cbf[:, 384 + i * 64:384 + (i + 1) * 64] for i in range(3)]
        r1024 = cbf[:, 576:704]
        r256 = cbf[:, 704:832]

        NCH = 8
        d_cast = [P.new_sem(f"d_cast{l}_{i}") for l in range(L) for i in range(NCH)]
        d_ring = [P.new_sem(f"d_ring{i}") for i in range(ring_n)]
        d_x = P.new_sem("d_x")
        d_xf = [P.new_sem("d_xf0"), P.new_sem("d_xf1")]
        d_scs = [P.new_sem("d_sc0"), P.new_sem("d_sc1")]
        d_xb = [P.new_sem(f"d_xb{i}") for i in range(4)]
        d_cs = [P.new_sem(f"d_c{i}") for i in range(8)]
        d_rope = P.new_sem("d_rope")
        d_sc = P.new_sem("d_sc")
        d_outb = [P.new_sem(f"d_out{i}") for i in range(4)]
        d_kvs = [[P.new_sem(f"d_kv{i}{j}") for j in range(2)] for i in range(1)]

        ps_ctr = {}
        dg_ctr = [0]

        def psum(lo=0, hi=8):
            k = (lo, hi)
            n = ps_ctr.get(k, 0)
            ps_ctr[k] = n + 1
            i = lo + (n % (hi - lo))
            r = R("ps", i)
            r.excl = True
            return psb[i], r

        ncn = 3 * 128 + 3 * 64 + 256
        P.dma("pool", d_cs[0], cbf[:], cnst_d[:, 0:ncn], writes=[R("cbf")])
        P.dma("act", d_cs[1], cn32[:], cnst_d[:, 0:128], writes=[R("cn32")])
        P.dma("pool", d_cs[2], pmat[:], pmat_d.rearrange("p (a b) -> p a b", b=128), writes=[R("pmat")])
        P.dma("act", d_cs[3], small[:], small_d, writes=[R("small")])
        P.dma("pool", d_cs[4], kcT[:], kc_d, writes=[R("kc")])
        P.dma("pool", d_cs[5], vcs[:], vc_d, writes=[R("vc")])
        P.op("pool", lambda e: e.memset(ua_c[:], 0.0), writes=[R("ua_c")])
        P.op("pool", lambda e: e.memset(sc_c[:], 0.0), writes=[R("sc_c")])
        ch = PACK_EL // NCH

        def issue_cast(l, i):
            hi_ = PACK_EL if i == NCH - 1 else (i + 1) * ch
            P.dma("pool", d_cast[l * NCH + i], wbf_d[l][:, i * ch:hi_], wpk_d[l][:, i * ch:hi_],
                  writes=[R("wbf", l, i)], max_dma_last_dim=16384)

        for i in range(NCH):
            issue_cast(0, i)
        l1_cast = list(range(NCH))

        def wbf_res(l, o, n):
            lo_i = min(NCH - 1, o // ch)
            hi_i = min(NCH - 1, (o + n - 1) // ch)
            return [R("wbf", l, i) for i in range(lo_i, hi_i + 1)]

        def sm(l, a, b):
            return small[:, 16 + l * 128 + a:16 + l * 128 + b]

        validL = small[:, 16 + 98:16 + 99]
        validR = small[:, 16 + 99:16 + 100]
        P.op("act", lambda e: e.activation(out=condb[:], in_=small[:, 0:16].rearrange("p (c t) -> p c t", t=2),
                                           func=AF.Silu), reads=[R("small")], writes=[R("condb")])

        seq = slab_sequence(do_sample, limit)
        wstate = {"next_load": 0, "pos": 0}

        def ensure_loaded(upto):
            while wstate["next_load"] <= upto and wstate["next_load"] < len(seq):
                s = wstate["next_load"]
                l, name = seq[s]
                o, kc, n = PACK_OFFS[name]
                slot = s % ring_n
                P.dma("sp", d_ring[slot], ring[:, slot, 0:kc * n], wbf_d[l][:, o:o + kc * n],
                      reads=wbf_res(l, o, kc * n), writes=[R("ring", slot)])
                wstate["next_load"] += 1

        def W(l, name):
            s = wstate["pos"]
            assert seq[s] == (l, name), (s, seq[s], l, name)
            wstate["pos"] += 1
            ensure_loaded(s + ring_n - 3)
            o, kc, n = PACK_OFFS[name]
            slot = s % ring_n
            return ring[:, slot, 0:kc * n].rearrange("p (k n) -> p k n", n=n), R("ring", slot)

        def mm_group(out_ap, out_res, items, extra_reads=(), signal_each=False):
            n = len(items)
            for i, (lhsT, rhs, rs) in enumerate(items):
                P.op("pe", lambda e, lhsT=lhsT, rhs=rhs, i=i: e.matmul(out_ap, lhsT=lhsT, rhs=rhs,
                                                                         start=(i == 0), stop=(i == n - 1)),
                     reads=list(rs) + list(extra_reads), writes=[out_res], signal=(signal_each or i == n - 1))

        def act(out, in_, func, reads, writes, **kw):
            P.op("act", lambda e: e.activation(out=out, in_=in_, func=func, **kw), reads=reads, writes=writes)

        def tt(eng, out, in0, in1, op, reads, writes):
            P.op(eng, lambda e: e.tensor_tensor(out=out, in0=in0, in1=in1, op=op), reads=reads, writes=writes)

        def stt(eng, out, in0, scalar, in1, op0, op1, reads, writes):
            P.op(eng, lambda e: e.scalar_tensor_tensor(out=out, in0=in0, scalar=scalar, in1=in1, op0=op0, op1=op1),
                 reads=reads, writes=writes)

        def layer_setup(l):
            ps, rps = psum()
            for i in range(12):
                w, rw = W(l, f"ada{i}")
                for mi in range(4):
                    m = i * 4 + mi
                    mm_group(ps[:, 2 * m:2 * m + 2], rps,
                             [(w[:, k, mi * 128:(mi + 1) * 128], condb[:, k, :], [rw, R("condb")]) for k in range(8)])
            bada = sm(l, 0, 48)
            P.op("dve", lambda e: e.tensor_tensor(out=modv[:, l], in0=ps[:, 0:96].rearrange("p (m t) -> p m t", t=2),
                                                  in1=bada.unsqueeze(2).to_broadcast([128, 48, 2]), op=ALU.add),
                 reads=[rps, R("small")], writes=[R("modv", l)])
            def gvec(i):
                return sm(l, 48 + 8 * i, 56 + 8 * i).unsqueeze(2).to_broadcast([128, 8, 2])
            mv = modv[:, l]
            dv = dvec[:, l]
            rd, wr = [R("modv", l), R("small")], [R("dvec", l)]
            stt("dve", dv[:, 0], mv[:, 8:16], 1.0, gvec(0), ALU.add, ALU.mult, rd, wr)
            P.op("dve", lambda e: e.tensor_copy(out=dv[:, 1], in_=mv[:, 0:8]), reads=rd, writes=wr)
            tt("dve", dv[:, 2], mv[:, 16:24], gvec(1), ALU.mult, rd, wr)
            stt("dve", dv[:, 3], mv[:, 32:40], 1.0, gvec(2), ALU.add, ALU.mult, rd, wr)
            P.op("dve", lambda e: e.tensor_copy(out=dv[:, 4], in_=mv[:, 24:32]), reads=rd, writes=wr)
            tt("dve", dv[:, 5], mv[:, 40:48], gvec(3), ALU.mult, rd, wr)
            wca = small_conv[l]
            for c in range(2):
                for k in range(3):
                    P.op("pool", lambda e, c=c, k=k: e.tensor_scalar(out=diagC[:, c, k, :], in0=ident32[:],
                                                                     scalar1=wca[:, c, 31 + k:32 + k], scalar2=None,
                                                                     op0=ALU.mult),
                         reads=[R("cn32"), R("convw")], writes=[R("diagC")])
            act(sinkf[:], sm(l, 90, 98)[0:64, :], AF.Exp, [R("small")], [R("sinkf")])

        convw_d = dt_in("convw", [128, L, 2, 34])
        convw = sb("convw_sb", [128, L, 2, 34], F32)
        P.dma("act", d_cs[6], convw[:], convw_d, writes=[R("convw")])
        small_conv = [convw[:, l] for l in range(L)]

        def norm_sq(T, xres, xb, sqalt):
            SQE = ("act", "dve", "pool", "act", "dve", "act", "dve", "pool")
            out = []
            for c in range(8):
                if sqalt:
                    sq, rsq = (qT[:, c, 0:T], R("qT", c)) if c < 4 else (hT[:, c, 0:T], R("hT", c))
                else:
                    sq, rsq = S16[:, 8 + c, 0:T], R("S16", 8 + c)
                if SQE[c] == "act":
                    act(sq, xb[:, c, 0:T], AF.Square, [xres[c]], [rsq])
                else:
                    tt(SQE[c], sq, xb[:, c, 0:T], xb[:, c, 0:T], ALU.mult, [xres[c]], [rsq])
                out.append((sq, rsq))
            return out

        def norm_rest(l, T, a_idx, b_idx, col, xres, xb, sqs):
            ps, rps = psum()
            for c in range(8):
                sq, rsq = sqs[c]
                P.op("pe", lambda e, sq=sq, c=c: e.matmul(ps[:, 0:T], lhsT=r1024, rhs=sq, start=(c == 0), stop=(c == 7)),
                     reads=[rsq, R("cbf")], writes=[rps], signal=(c == 7))
            rsq_, rr0 = S32[:, 12, 0:T], R("S32", 12)
            rstd, rr = S32[:, 13, 0:T], R("S32", 13)
            act(rsq_, ps[:, 0:T], AF.Sqrt, [rps], [rr0], bias=EPS, scale=1.0)
            P.op("dve", lambda e: e.reciprocal(out=rstd, in_=rsq_), reads=[rr0], writes=[rr])
            for c in range(8):
                tmp, rt = S32[:, 10 + (c % 2), 0:T], R("S32", 10 + (c % 2))
                tt("dve", tmp, xb[:, c, 0:T], rstd, ALU.mult, [xres[c], rr], [rt])
                act(hT[:, c, 0:T], tmp, AF.Identity, [rt, R("dvec", l)], [R("hT", c)],
                    scale=dvec[:, l, a_idx, c, col:col + 1], bias=dvec[:, l, b_idx, c, col:col + 1])

        def norm_to_h(l, T, a_idx, b_idx, col, xres, xb=None):
            if xb is None:
                xb = XT()
            sqs = norm_sq(T, xres, xb, False)
            norm_rest(l, T, a_idx, b_idx, col, xres, xb, sqs)

        def post_norm_residual(l, T, g_idx, col, mres, xres):
            ps, rps = psum()
            SQE = ("act", "dve", "pool", "act", "dve", "act", "dve", "pool")
            for c in range(8):
                sq, rsq = S16[:, 8 + c, 0:T], R("S16", 8 + c)
                if SQE[c] == "act":
                    act(sq, S32[:, c, 0:T], AF.Square, [mres[c]], [rsq])
                else:
                    tt(SQE[c], sq, S32[:, c, 0:T], S32[:, c, 0:T], ALU.mult, [mres[c]], [rsq])
                P.op("pe", lambda e, sq=sq, c=c: e.matmul(ps[:, 0:T], lhsT=r1024, rhs=sq, start=(c == 0), stop=(c == 7)),
                     reads=[rsq, R("cbf")], writes=[rps], signal=True)
            rsq_, rr0 = S32[:, 12, 0:T], R("S32", 12)
            rstd, rr = S32[:, 13, 0:T], R("S32", 13)
            act(rsq_, ps[:, 0:T], AF.Sqrt, [rps], [rr0], bias=EPS, scale=1.0)
            P.op("dve", lambda e: e.reciprocal(out=rstd, in_=rsq_), reads=[rr0], writes=[rr])
            for c in range(8):
                tt("dve", S32[:, c, 0:T], S32[:, c, 0:T], rstd, ALU.mult, [mres[c], rr], [mres[c]])
                stt("dve", XT()[:, c, 0:T], S32[:, c, 0:T], dvec[:, l, g_idx, c, col:col + 1],
                    XT()[:, c, 0:T], ALU.mult, ALU.add, [mres[c], R("dvec", l), xres[c]], [xres[c]])

        def XT():
            return xTs[XS["cur"]]

        def issue_tok_dma(pas, blocks):
            src = xp_d if pas == "p" else xs_d
            for bi, b in enumerate(blocks):
                P.dma("act", d_xb[bi], xtok(bi), src[b * 128:(b + 1) * 128, :], writes=xtok_res(bi))

        def issue_feat_dma(pas, blocks, idx):
            T = len(blocks) * 128
            b0 = blocks[0]
            P.dma("act", d_xf[idx], xTs[idx][:, :, 0:T], xsc_d[pas][:, :, b0 * 128:b0 * 128 + T],
                  reads=[R("xsc", pas, 0), R("xsc", pas, 1)], writes=[R("xT", idx, c) for c in range(8)])

        def load_x(kind, pas, l, blocks, T, nxt, do_prefetch=True, set_cur=True):
            XS["i"] ^= 1
            cur = XS["i"]
            if set_cur:
                XS["cur"] = cur
            xres = [R("xT", cur, c) for c in range(8)]
            nb = len(blocks)
            key = (kind, pas, l, tuple(blocks))
            tokmode = (l == 0 and kind == "A")
            if XS["pref"] != key:
                if tokmode:
                    issue_tok_dma(pas, blocks)
                else:
                    issue_feat_dma(pas, blocks, cur)
            XS["pref"] = None
            if tokmode:
                for c in range(8):
                    ps, rps = psum()
                    for bi in range(nb):
                        P.op("pe", lambda e, bi=bi, c=c, ps=ps: e.matmul(ps[:, bi * 128:(bi + 1) * 128],
                                                                          lhsT=xtok(bi, c * 128, (c + 1) * 128), rhs=ident32[:],
                                                                          start=True, stop=True),
                             reads=xtok_res(bi) + [R("cn32")], writes=[rps], signal=(bi == nb - 1))
                    if c % 2 == 0:
                        act(xTs[cur][:, c, 0:T], ps[:, 0:T], AF.Copy, [rps], [xres[c]])
                    else:
                        P.op("dve", lambda e, c=c, ps=ps, xb=xTs[cur]: e.tensor_copy(out=xb[:, c, 0:T], in_=ps[:, 0:T]),
                             reads=[rps], writes=[xres[c]])
                b0 = blocks[0]
                P.dma("act", d_scs[cur], xsc_d[pas][:, :, b0 * 128:b0 * 128 + T], xTs[cur][:, :, 0:T],
                      reads=xres, writes=[R("xsc", pas, cur)])
            if do_prefetch:
                prefetch_next(kind, l, nxt)
            return xres

        def prefetch_next(kind, l, nxt):
            cur = XS["i"]
            tokmode = (l == 0 and kind == "A")
            if nxt is not None:
                nkind, npas, nl, nblocks = nxt
                ntok = (nl == 0 and nkind == "A")
                if ntok and tokmode:
                    issue_tok_dma(npas, nblocks)
                    XS["pref"] = (nkind, npas, nl, tuple(nblocks))
                elif not ntok:
                    issue_feat_dma(npas, nblocks, cur ^ 1)
                    XS["pref"] = (nkind, npas, nl, tuple(nblocks))

        def store_x(pas, l, blocks, T, xres, ci=None):
            b0 = blocks[0]
            if ci is None:
                ci = XS["cur"]
            if l == 0:
                P.dma("act", d_scs[ci], xsc_d[pas][:, :, b0 * 128:b0 * 128 + T], xTs[ci][:, :, 0:T],
                      reads=xres, writes=[R("xsc", pas, ci)])
            else:
                dst = yp_d if pas == "p" else ys_d
                for bi, b in enumerate(blocks):
                    ob = b if pas == "p" else b - 2
                    for half in range(2):
                        ps, rps = psum()
                        for cc in range(4):
                            c = half * 4 + cc
                            P.op("pe", lambda e, bi=bi, c=c, cc=cc, ps=ps, xb=xTs[ci]: e.matmul(
                                ps[:, cc * 128:(cc + 1) * 128], lhsT=xb[:, c, bi * 128:(bi + 1) * 128], rhs=ident32[:],
                                start=True, stop=True),
                                 reads=[xres[c], R("cn32")], writes=[rps], signal=(cc == 3))
                        if half == 0:
                            act(xtok(bi, 0, 512), ps[:], AF.Copy, [rps], [R("S32", 2 * bi)])
                        else:
                            P.op("dve", lambda e, bi=bi, ps=ps: e.tensor_copy(out=xtok(bi, 512, 1024), in_=ps[:]),
                                 reads=[rps], writes=[R("S32", 2 * bi + 1)])
                    P.dma("act", d_outb[bi], dst[ob * 128:(ob + 1) * 128, :], xtok(bi), reads=xtok_res(bi))

        def ccol(pas, t):
            if pas == "p":
                s, r = t // 256, t % 256
                return s * (256 + 2 * PADC) + PADC + r
            return PADC + t

        def conv_rhs(buf, c, pas, blocks, T, shift):
            t0 = blocks[0] * 128
            if pas == "p" and T == 512:
                base = ccol(pas, t0) + shift
                return buf[:, c, base:base + 2 * (256 + 2 * PADC)].rearrange("p (s w) -> p s w", s=2)[:, :, 0:256]
            base = ccol(pas, t0) + shift
            return buf[:, c, base:base + T]

        def conv_dst(buf, c, pas, blocks, T):
            return conv_rhs(buf, c, pas, blocks, T, 0)

        def as_tile(ap, pas, T):
            if pas == "p" and T == 512:
                return ap.rearrange("p (s w) -> p s w", s=2)
            return ap

        def phase_front(kind, pas, l, blocks, T, nxt, col):
            key = (kind, pas, l, tuple(blocks))
            if XS["early"] is not None and XS["early"][0] == key:
                xres = XS["early"][1]
                XS["early"] = None
                XS["cur"] = XS["i"]
                if XS["pstore"] is not None:
                    XS["pprefetch"] = lambda: prefetch_next(kind, l, nxt)
                else:
                    prefetch_next(kind, l, nxt)
                return xres
            flush_store()
            xres = load_x(kind, pas, l, blocks, T, nxt)
            norm_to_h(l, T, 0, 1, col, xres)
            return xres

        def flush_store():
            if XS["pstore"] is not None:
                XS["pstore"]()
                XS["pstore"] = None
            if XS["pprefetch"] is not None:
                XS["pprefetch"]()
                XS["pprefetch"] = None

        def early_front(nxt, l_cur):
            if nxt is None:
                return
            nkind, npas, nl, nblocks = nxt
            if nl != l_cur or (nl == 0 and nkind == "A"):
                return
            key = (nkind, npas, nl, tuple(nblocks))
            if XS["pref"] != key:
                return
            nT = len(nblocks) * 128
            ncol = 0 if npas == "p" else 1
            xres_n = load_x(nkind, npas, nl, nblocks, nT, None, do_prefetch=False, set_cur=False)
            xbn = xTs[XS["i"]]
            sqs = norm_sq(nT, xres_n, xbn, True)
            XS["early"] = (key, xres_n)
            XS["early2"] = lambda: norm_rest(nl, nT, 0, 1, ncol, xres_n, xbn, sqs)

        def phase_A(pas, l, blocks, nxt=None):
            T = len(blocks) * 128
            col = 0 if pas == "p" else 1
            if DEBUG_STAGE == 14:
                return
            if DEBUG_STAGE == 12:
                P.op("pool", lambda e: e.memset(ua_c[:, :, 0:4 * (256 + 2 * PADC)], 0.0), writes=[R("ua_c")])
                return
            if pas == "p" and blocks[0] == 0 and DEBUG_STAGE != 13:
                P.op("pool", lambda e: e.memset(ua_c[:, :, 0:4 * (256 + 2 * PADC)], 0.0), writes=[R("ua_c")])
                P.op("pool", lambda e: e.memset(sc_c[:, :, 0:4 * (256 + 2 * PADC)], 0.0), writes=[R("sc_c")])
            xres = phase_front("A", pas, l, blocks, T, nxt, col)
            if DEBUG_STAGE in (2, 21, 22, 23, 24, 211, 212):
                return
            hres = [R("hT", c) for c in range(8)]
            t0 = blocks[0] * 128
            if pas == "s":
                P.dma("act", d_rope, ropeT[:, :, 0:T], rope_d[:, :, t0:t0 + T], writes=[R("ropeT")])
            slabs = [W(l, "A0"), W(l, "A1"), W(l, "A2")]

            def proj(ci):
                w, rw = slabs[ci // 4]
                ps, rps = psum()
                mm_group(ps[:, 0:T], rps, [(w[:, k, (ci % 4) * 128:(ci % 4 + 1) * 128], hT[:, k, 0:T], [rw, hres[k]])
                                           for k in range(8)])
                return ps, rps

            for c in range(2):
                psg, rg = proj(c)
                sig, rs_ = S16[:, 10 + c, 0:T], R("S16", 10 + c)
                act(sig, psg[:, 0:T], AF.Sigmoid, [rg], [rs_])
                psv, rv = proj(2 + c)
                tt("dve", conv_dst(ua_c, c, pas, blocks, T), as_tile(psv[:, 0:T], pas, T), as_tile(sig, pas, T),
                   ALU.mult, [rv, rs_], [R("ua_c")])
            for c in range(2):
                psc, rc = proj(4 + c)
                tmp, rt = S32[:, 8 + c, 0:T], R("S32", 8 + c)
                act(tmp, psc[:, 0:T], AF.Copy, [rc], [rt])
                psh, rh = proj(6 + c)
                tt("dve", conv_dst(sc_c, c, pas, blocks, T), as_tile(psh[:, 0:T], pas, T), as_tile(tmp, pas, T),
                   ALU.mult, [rh, rt], [R("sc_c")])
            psk, rk = proj(8)
            if pas == "p":
                act(kT[:, t0:t0 + T], psk[:, 0:T], AF.Copy, [rk], [R("kT")])
            else:
                pss, rss = proj(9)
                t1, r1 = S32[:, 8, 0:T], R("S32", 8)
                t2, r2 = S32[:, 9, 0:T], R("S32", 9)
                tt("dve", t1, psk[:, 0:T], ropeT[:, 0, 0:T], ALU.mult, [rk, R("ropeT")], [r1])
                tt("dve", t2, pss[:, 0:T], ropeT[:, 1, 0:T], ALU.mult, [rss, R("ropeT")], [r2])
                tt("pool", kT[:, t0:t0 + T], t1, t2, ALU.add, [r1, r2], [R("kT")])
            flush_store()
            if DEBUG_STAGE == 3:
                wstate["pos"] += 1
                return
            wt, rwt = W(l, "AT")
            for bi, b in enumerate(blocks):
                ps, rps = psum()
                ncol = 512 if pas == "p" else 384
                mm_group(ps[:, 0:ncol], rps, [(hT[:, k, bi * 128:(bi + 1) * 128], wt[:, k, 0:ncol], [rwt, hres[k]])
                                              for k in range(8)])
                vs = None
                if pas == "s" and b in (0, 1):
                    vs = validL
                if pas == "s" and b in (18, 19):
                    vs = validR
                if vs is None:
                    act(uptok[:, b, :], ps[:, 0:256], AF.Copy, [rps], [R("uptok")])
                    P.op("dve", lambda e, b=b, ps=ps: e.tensor_copy(out=vtok[:, b, :], in_=ps[:, 256:384]),
                         reads=[rps], writes=[R("vtok")])
                else:
                    act(uptok[:, b, :], ps[:, 0:256], AF.Identity, [rps, R("small")], [R("uptok")], scale=vs)
                    act(vtok[:, b, :], ps[:, 256:384], AF.Identity, [rps, R("small")], [R("vtok")], scale=vs)
                    for buf, nm in ((ua_c, "ua_c"), (sc_c, "sc_c")):
                        cs = ccol(pas, b * 128)
                        P.op("pool", lambda e, buf=buf, cs=cs, vs=vs: e.tensor_scalar(
                            out=buf[:, :, cs:cs + 128], in0=buf[:, :, cs:cs + 128], scalar1=vs, scalar2=None,
                            op0=ALU.mult), reads=[R(nm), R("small")], writes=[R(nm)])
                if pas == "p":
                    kb = 0
                    P.op("dve", lambda e, kb=kb, ps=ps: e.tensor_copy(out=kvo[:, kb, :], in_=ps[:, 256:512]),
                         reads=[rps], writes=[R("kvo", kb)])
                    P.dma("act", d_kvs[kb][0], nv_d[l, b * 128:(b + 1) * 128, :], kvo[:, kb, 0:128], reads=[R("kvo", kb)])
                    P.dma("act", d_kvs[kb][1], nk_d[l, b * 128:(b + 1) * 128, :], kvo[:, kb, 128:256], reads=[R("kvo", kb)])

        def phase_B(pas, l, blocks, nxt=None):
            T = len(blocks) * 128
            nb = len(blocks)
            col = 0 if pas == "p" else 1
            t0 = blocks[0] * 128
            xres = phase_front("B", pas, l, blocks, T, nxt, col)
            hres = [R("hT", c) for c in range(8)]
            if pas == "s":
                P.dma("act", d_rope, ropeT[:, :, 0:T], rope_d[:, :, t0:t0 + T], writes=[R("ropeT")])
            slabs = [W(l, "B0"), W(l, "B1"), W(l, "B2")]

            def proj(ci):
                w, rw = slabs[ci // 4]
                ps, rps = psum()
                mm_group(ps[:, 0:T], rps, [(w[:, k, (ci % 4) * 128:(ci % 4 + 1) * 128], hT[:, k, 0:T], [rw, hres[k]])
                                           for k in range(8)])
                return ps, rps

            for c in range(2):
                ps, rps = proj(c)
                act(S32[:, 8 + c, 0:T], ps[:, 0:T], AF.Copy, [rps], [R("S32", 8 + c)])
            for c in range(4):
                psq, rq = proj(2 + c)
                if pas == "p":
                    act(qT[:, c, 0:T], psq[:, 0:T], AF.Copy, [rq], [R("qT", c)])
                else:
                    pss, rss = proj(6 + c)
                    t1, r1 = S32[:, 10, 0:T], R("S32", 10)
                    t2, r2 = S32[:, 11, 0:T], R("S32", 11)
                    tt("dve", t1, psq[:, 0:T], ropeT[:, 0, 0:T], ALU.mult, [rq, R("ropeT")], [r1])
                    tt("dve", t2, pss[:, 0:T], ropeT[:, 1, 0:T], ALU.mult, [rss, R("ropeT")], [r2])
                    tt("pool", qT[:, c, 0:T], t1, t2, ALU.add, [r1, r2], [R("qT", c)])
            flush_store()
            uar = []
            for c in range(2):
                ps, rps = psum()
                for k in range(31):
                    di = dg_ctr[0] % 8
                    dg_ctr[0] += 1
                    P.op("dve", lambda e, c=c, k=k, di=di: e.tensor_scalar(
                        out=diagA[:, di, :], in0=ident32[:], scalar1=convw[:, l, c, k:k + 1], scalar2=None,
                        op0=ALU.mult), reads=[R("cn32"), R("convw")], writes=[R("diagA", di)])
                    P.op("pe", lambda e, c=c, k=k, di=di, ps=ps: e.matmul(
                        as_tile(ps[:, 0:T], pas, T), lhsT=diagA[:, di, :],
                        rhs=conv_rhs(ua_c, c, pas, blocks, T, k - 15), start=(k == 0), stop=(k == 30)),
                         reads=[R("diagA", di), R("ua_c")], writes=[rps], signal=True)
                u, ru = S32[:, c, 0:T], R("S32", c)
                act(u, ps[:, 0:T], AF.Identity, [rps, R("small")], [ru], bias=sm(l, 80 + c, 81 + c), scale=1.0)
                ub_, rub = S16[:, 18 + c, 0:T], R("S16", 18 + c)
                P.op("pool", lambda e, u=u, ub_=ub_: e.tensor_copy(out=ub_, in_=u), reads=[ru], writes=[rub])
                uar.append((u, ru, ub_, rub))
            psm, rpm = psum()
            mm_group(psm[:, 0:T], rpm, [(r256, uar[c][2], [uar[c][3], R("cbf")]) for c in range(2)])
            for c in range(2):
                u, ru, ub_, rub = uar[c]
                tt("dve", u, u, psm[:, 0:T], ALU.subtract, [ru, rpm], [ru])
                act(ub_, u, AF.Square, [ru], [rub])
            psv, rpv = psum()
            mm_group(psv[:, 0:T], rpv, [(r256, uar[c][2], [uar[c][3], R("cbf")]) for c in range(2)])
            rs_a0, rra0 = S32[:, 2, 0:T], R("S32", 2)
            rs_a, rra = S32[:, 13, 0:T], R("S32", 13)
            act(rs_a0, psv[:, 0:T], AF.Sqrt, [rpv], [rra0], bias=EPS, scale=1.0)
            P.op("dve", lambda e: e.reciprocal(out=rs_a, in_=rs_a0), reads=[rra0], writes=[rra])
            for c in range(2):
                u, ru, ub_, rub = uar[c]
                tt("dve" if c else "pool", u, u, rs_a, ALU.mult, [ru, rra], [ru])
                act(S16[:, 16 + c, 0:T], u, AF.Silu, [ru, R("small")], [R("S16", 16 + c)],
                    scale=sm(l, 82 + c, 83 + c), bias=sm(l, 84 + c, 85 + c))
            wp, rwp = W(l, "wpool")
            for g in range(4):
                ps, rps = psum()
                for bi, b in enumerate(blocks):
                    items = []
                    if pas == "p":
                        first = (b % 2 == 0)
                        kinds = [(b, 3 if first else 4)]
                        kinds.append((b + 1, 1) if first else (b - 1, 0))
                    else:
                        cur = 5 if b == 2 else (6 if b == 17 else 2)
                        kinds = [(b - 1, 0), (b, cur), (b + 1, 1)]
                    for (j, kind) in kinds:
                        items.append((uptok[:, j, g * 64:(g + 1) * 64], pmat[:, kind * 4 + g, :], [R("uptok"), R("pmat")]))
                    mm_group(ps[0:64, bi * 128:(bi + 1) * 128], rps, items)
                pl, rpl = S16[0:64, 8 + (g % 2), 0:T], R("S16", 8 + (g % 2))
                if g % 2:
                    P.op("dve", lambda e, pl=pl, ps=ps: e.tensor_copy(out=pl, in_=ps[0:64, 0:T]), reads=[rps], writes=[rpl])
                else:
                    act(pl, ps[0:64, 0:T], AF.Copy, [rps], [rpl])
                ps2, rps2 = psum()
                mm_group(ps2[0:64, 0:T], rps2, [(wp[0:64, 0, g * 64:(g + 1) * 64], pl, [rwp, rpl])])
                act(S16[0:64, 20 + g, 0:T], ps2[0:64, 0:T], AF.Identity, [rps2, R("small")], [R("S16", 20 + g)],
                    scale=sm(l, 86 + g, 87 + g)[0:64, :])
            for c in range(2):
                ps, rps = psum()
                mm_group(as_tile(ps[:, 0:T], pas, T), rps,
                         [(diagC[:, c, k, :], conv_rhs(sc_c, c, pas, blocks, T, k - 1), [R("diagC"), R("sc_c")])
                          for k in range(3)])
                tt("dve", S16[:, 18 + c, 0:T], ps[:, 0:T], S32[:, 8 + c, 0:T], ALU.mult,
                   [rps, R("S32", 8 + c)], [R("S16", 18 + c)])
            jobs = []
            for bi, b in enumerate(blocks):
                if pas == "p":
                    s0 = (b // 2) * 2
                    keys = [("l", s0, None, 0), ("l", s0 + 1, None, 0)]
                else:
                    keys = []
                    for j, mk in ((b - 1, maskP), (b, None), (b + 1, maskN)):
                        oi = 1 if j in (0, 1) else (2 if j in (18, 19) else 0)
                        keys.append(("l", j, mk, oi))
                    keys += [("c", 0, None, 0), ("c", 1, None, 0)]
                for g in range(2):
                    for ki, ks in enumerate(keys):
                        jobs.append((bi, g, ki, len(keys), ks))

            def kv_of(job):
                bi, g, ki, nk, (kind, j, mk, oi) = job
                if kind == "l":
                    return (kT[g * 64:(g + 1) * 64, j * 128:(j + 1) * 128], R("kT"),
                            vtok[:, j, g * 64:(g + 1) * 64], R("vtok"))
                return (kcT[g * 64:(g + 1) * 64, l, j * 128:(j + 1) * 128], R("kc"),
                        vcs[:, l, j, g * 64:(g + 1) * 64], R("vc"))

            def issue_S(job):
                bi, g, ki, nk, ks = job
                klhs, kres, _, _ = kv_of(job)
                pS, rpS = psum(0, 4)
                qr = qT[g * 64:(g + 1) * 64, :, bi * 128:(bi + 1) * 128]
                mm_group(pS[:].rearrange("p (h q) -> p h q", h=4), rpS,
                         [(klhs, qr, [kres] + [R("qT", c) for c in range(4)])])
                return pS, rpS

            LA = 3
            pending = []
            inflight = {}
            for ji in range(min(LA, len(jobs))):
                inflight[ji] = issue_S(jobs[ji])
            acc = None
            for ji, job in enumerate(jobs):
                bi, g, ki, nk, (kind, j, mk, oi) = job
                if ki == 0:
                    acc = (psum(4, 6), psum(6, 8))
                (po, rpo), (pd, rpd) = acc
                pS, rpS = inflight.pop(ji)
                _, _, vlhs, vres = kv_of(job)
                es = 12 + (ji % 4)
                E, rE = S16[:, es, :], R("S16", es)
                act(E, pS[:], AF.Exp, [rpS], [rE], scale=0.125)
                if mk is not None:
                    E3 = E.rearrange("p (h q) -> p h q", h=4)
                    tt("dve", E3, E3, mk.unsqueeze(1).to_broadcast([128, 4, 128]), ALU.mult, [rE, R("cbf")], [rE])
                if ji + LA < len(jobs):
                    inflight[ji + LA] = issue_S(jobs[ji + LA])
                P.op("pe", lambda e, vlhs=vlhs, E=E, ki=ki, po=po, nk=nk: e.matmul(
                    po[0:64, :], lhsT=vlhs, rhs=E, start=(ki == 0), stop=(ki == nk - 1)),
                     reads=[vres, rE], writes=[rpo], signal=False)
                P.op("pe", lambda e, oi=oi, E=E, ki=ki, pd=pd, nk=nk: e.matmul(
                    pd[0:64, :], lhsT=ones64[oi], rhs=E, start=(ki == 0), stop=(ki == nk - 1)),
                     reads=[R("cbf"), rE], writes=[rpd, rpo], signal=True)
                while pending and pending[0][0] <= ji:
                    pending.pop(0)[1]()
                if ki == nk - 1:
                  def normalize(bi=bi, g=g, po=po, rpo=rpo, pd=pd, rpd=rpd):
                    den, rden = S32[0:64, 3 + g, :], R("S32", 3 + g)
                    rec, rrec = S32[0:64, 5 + g, :], R("S32", 5 + g)
                    tt("dve", den.rearrange("p (h q) -> p h q", h=4), pd[0:64, :].rearrange("p (h q) -> p h q", h=4),
                       sinkf[:, g * 4:(g + 1) * 4].unsqueeze(2).to_broadcast([64, 4, 128]), ALU.add,
                       [rpd, R("sinkf")], [rden])
                    act(den, den, AF.Ln, [rden], [rden])
                    act(rec, den, AF.Exp, [rden], [rrec], scale=-1.0)
                    for hh in range(4):
                        slot = 24 + g * 4 + hh
                        tt("dve", S16[0:64, slot, bi * 128:(bi + 1) * 128], po[0:64, hh * 128:(hh + 1) * 128],
                           rec[:, hh * 128:(hh + 1) * 128], ALU.mult, [rpo, rrec], [R("S16", slot)])
                  pending.append((ji + 2, normalize))
            while pending:
                pending.pop(0)[1]()
            for c in range(8):
                wg, rwg = W(l, f"g{c}")
                wb, rwb = W(l, f"br{c}")
                gsl = []
                for b_ in range(4):
                    ps, rps = psum()
                    mm_group(ps[:, 0:T], rps, [(wg[:, k, b_ * 128:(b_ + 1) * 128], hT[:, k, 0:T], [rwg, hres[k]])
                                               for k in range(8)])
                    gs = 8 + (c % 2) * 4 + b_
                    act(S16[:, gs, 0:T], ps[:, 0:T], AF.Sigmoid, [rps], [R("S16", gs)])
                    gsl.append(gs)
                brs = []
                ps, rps = psum()
                mm_group(ps[:, 0:T], rps, [(wb[:, 0, k * 128:(k + 1) * 128], S16[:, 16 + k, 0:T], [rwb, R("S16", 16 + k)])
                                           for k in range(2)])
                brs.append((ps, rps))
                ps, rps = psum()
                mm_group(ps[:, 0:T], rps, [(wb[0:64, 0, 256 + g * 128:256 + (g + 1) * 128], S16[0:64, 20 + g, 0:T],
                                            [rwb, R("S16", 20 + g)]) for g in range(4)])
                brs.append((ps, rps))
                ps, rps = psum()
                mm_group(ps[:, 0:T], rps, [(wb[:, 0, 768 + k * 128:768 + (k + 1) * 128], S16[:, 18 + k, 0:T],
                                            [rwb, R("S16", 18 + k)]) for k in range(2)])
                brs.append((ps, rps))
                ps, rps = psum()
                mm_group(ps[:, 0:T], rps, [(wb[0:64, 0, 1024 + h * 128:1024 + (h + 1) * 128], S16[0:64, 24 + h, 0:T],
                                            [rwb, R("S16", 24 + h)]) for h in range(8)])
                brs.append((ps, rps))
                acc, racc = S32[:, 5 + (c % 2), 0:T], R("S32", 5 + (c % 2))
                tmp, rtmp = S32[:, 7, 0:T], R("S32", 7)
                tt("dve", acc, brs[0][0][:, 0:T], S16[:, gsl[0], 0:T], ALU.mult, [brs[0][1], R("S16", gsl[0])], [racc])
                for b_ in range(1, 4):
                    tt("dve", tmp, brs[b_][0][:, 0:T], S16[:, gsl[b_], 0:T], ALU.mult,
                       [brs[b_][1], R("S16", gsl[b_])], [rtmp])
                    if b_ < 3:
                        tt("pool", acc, acc, tmp, ALU.add, [racc, rtmp], [racc])
                    else:
                        tt("pool", S16[:, c, 0:T], acc, tmp, ALU.add, [racc, rtmp], [R("S16", c)])
            wos = [W(l, "wo0"), W(l, "wo1")]
            mres = [R("S32", c) for c in range(8)]
            for c in range(8):
                w, rw = wos[c // 4]
                ps, rps = psum()
                mm_group(ps[:, 0:T], rps, [(w[:, k, (c % 4) * 128:(c % 4 + 1) * 128], S16[:, k, 0:T], [rw, R("S16", k)])
                                           for k in range(8)])
                act(S32[:, c, 0:T], ps[:, 0:T], AF.Copy, [rps], [mres[c]])
            post_norm_residual(l, T, 2, col, mres, xres)
            norm_to_h(l, T, 3, 4, col, xres)
            for i in range(8):
                w, rw = W(l, f"f1_{i}")
                for ci in range(4):
                    j = i * 4 + ci
                    ps, rps = psum()
                    mm_group(ps[:, 0:T], rps, [(w[:, k, ci * 128:(ci + 1) * 128], hT[:, k, 0:T], [rw, hres[k]])
                                               for k in range(8)])
                    tmp, rtmp = S32[:, 8 + (j % 4), 0:T], R("S32", 8 + (j % 4))
                    act(tmp, ps[:, 0:T], AF.Relu, [rps], [rtmp])
                    tt("dve" if j % 2 else "pool", S16[:, j, 0:T], tmp, tmp, ALU.mult, [rtmp], [R("S16", j)])
            XS["early2"] = None
            early_front(nxt, l)
            for c in range(8):
                if c == 2 and XS["early2"] is not None:
                    XS["early2"]()
                    XS["early2"] = None
                w, rw = W(l, f"f2_{c}")
                ps, rps = psum()
                mm_group(ps[:, 0:T], rps, [(w[:, k, :], S16[:, k, 0:T], [rw, R("S16", k)]) for k in range(32)])
                act(S32[:, c, 0:T], ps[:, 0:T], AF.Copy, [rps], [mres[c]])
            post_norm_residual(l, T, 5, col, mres, xres)
            if l == 1 and nxt is not None:
                ci_ = XS["cur"]
                XS["pstore"] = lambda: store_x(pas, l, blocks, T, xres, ci_)
            else:
                store_x(pas, l, blocks, T, xres)

        sched_all = schedule(do_sample)[:limit]
        for si, (kind, pas, l, blocks) in enumerate(sched_all):
            if kind == "setup" and l == 1:
                while l1_cast:
                    issue_cast(1, l1_cast.pop(0))
            if l == 0 and si >= 4 and l1_cast:
                issue_cast(1, l1_cast.pop(0))
            if kind == "setup":
                layer_setup(l)
            else:
                nxt = None
                for it in sched_all[si + 1:]:
                    if it[0] != "setup":
                        nxt = it
                        break
                (phase_A if kind == "A" else phase_B)(pas, l, blocks, nxt)
        assert DEBUG_STAGE or wstate["pos"] == len(seq), (wstate["pos"], len(seq))
        P.wait_all("act", d_outb + d_kvs[0] + [d_sc, d_x, d_rope] + d_xf + d_scs + d_xb + d_cs)
        P.wait_all("sp", d_ring + d_cast)
        P.wait_all("pool", d_cast + d_cs)
        print("sbuf bytes remaining", nc.sbuf_bytes_remaining)
        print("instructions", P.n_inst, "waits", P.n_wait)
        P.emit()
    return nc


def tiles_of(blocks):
    return [blocks[i:i + 4] for i in range(0, len(blocks), 4)]


def schedule(do_sample):
    out = []
    for l in range(L):
        out.append(("setup", None, l, None))
        for t in tiles_of(list(range(8))):
            out.append(("A", "p", l, t))
        for t in tiles_of(list(range(8))):
            out.append(("B", "p", l, t))
        if do_sample:
            lo, hi = (0, 20) if l == 0 else (1, 19)
            for t in tiles_of(list(range(lo, hi))):
                out.append(("A", "s", l, t))
            lo, hi = (1, 19) if l == 0 else (2, 18)
            for t in tiles_of(list(range(lo, hi))):
                out.append(("B", "s", l, t))
    return out


def slab_sequence(do_sample, limit=None):
    seq = []
    for (kind, pas, l, blocks) in schedule(do_sample)[:limit]:
        if kind == "setup":
            seq += [(l, f"ada{i}") for i in range(12)]
        elif kind == "A":
            seq += [(l, "A0"), (l, "A1"), (l, "A2"), (l, "AT")]
        else:
            seq += [(l, "B0"), (l, "B1"), (l, "B2"), (l, "wpool")]
            for c in range(8):
                seq += [(l, f"g{c}"), (l, f"br{c}")]
            seq += [(l, "wo0"), (l, "wo1")]
            seq += [(l, f"f1_{i}") for i in range(8)]
            seq += [(l, f"f2_{c}") for c in range(8)]
    return seq


_NC_CACHE = {}


def host_inputs(inp, core):
    b, half = core // 2, core % 2
    start = half * 2048
    m = {}
    m["xp"] = np.ascontiguousarray(inp["x_prompt"][4 * core:4 * core + 4].reshape(1024, D))
    xs = np.zeros((WIN, D), np.float32)
    lo, hi = start - 256, start + 2304
    slo, shi = max(lo, 0), min(hi, 4096)
    xs[slo - lo:shi - lo] = inp["x_sample"][b, slo:shi]
    m["xs"] = xs
    validL = 1.0 if lo >= 0 else 0.0
    validR = 1.0 if hi <= 4096 else 0.0
    cn = np.zeros((128, 3 * 128 + 3 * 64 + 256), np.float32)
    cn[:, 0:128] = np.eye(128)
    kk = np.arange(128)[:, None]
    qq = np.arange(128)[None, :]
    cn[:, 128:256] = (kk >= qq)
    cn[:, 256:384] = (kk <= qq)
    cn[:, 384:448] = 1.0
    cn[:, 448:512] = validL
    cn[:, 512:576] = validR
    cn[:, 576:704] = 1.0 / 1024
    cn[:, 704:832] = 1.0 / 256
    m["cnst"] = cn
    PM = pool_mats()
    pm = np.zeros((7, 4, 128, 128), np.float32)
    pm[0:5] = PM
    pm[5] = PM[3] if half == 0 else PM[2]
    pm[6] = PM[4] if half == 1 else PM[2]
    m["pmat"] = np.ascontiguousarray(pm.reshape(28, 128, 128).transpose(1, 0, 2).reshape(128, 28 * 128))
    sm = np.zeros((128, 16 + L * 128), np.float32)
    cond = np.stack([inp["c_ctx"], inp["c"][b]], axis=1)
    sm[:, 0:16] = cond.reshape(8, 128, 2).transpose(1, 0, 2).reshape(128, 16)
    for l in range(L):
        o = 16 + l * 128
        sm[:, o:o + 48] = inp["b_ada"][l].reshape(48, 128).T
        for i, nm in enumerate(("g_pre_mix", "g_post_mix", "g_pre_ffn", "g_post_ffn")):
            sm[:, o + 48 + 8 * i:o + 56 + 8 * i] = inp[nm][l].reshape(8, 128).T
        sm[:, o + 80:o + 82] = inp["b_conv_a"][l].reshape(2, 128).T
        sm[:, o + 82:o + 84] = inp["ln_g_a"][l].reshape(2, 128).T
        sm[:, o + 84:o + 86] = inp["ln_b_a"][l].reshape(2, 128).T
        sm[0:64, o + 86:o + 90] = inp["pool_scale"][l].reshape(4, 64).T
        sm[:, o + 90:o + 98] = inp["sink"][l][None, :]
    sm[:, 16 + 98] = validL
    sm[:, 16 + 99] = validR
    m["small"] = sm
    cw = np.zeros((128, L, 2, 34), np.float32)
    for l in range(L):
        cw[:, l, :, 0:31] = inp["w_conv_a"][l].reshape(31, 2, 128).transpose(2, 1, 0)
        cw[:, l, :, 31:34] = inp["w_sc"][l].reshape(3, 2, 128).transpose(2, 1, 0)
    m["convw"] = cw
    pos = np.arange(lo, hi)
    posc = np.clip(pos, 0, 4095)
    row, colp = posc // 64, posc % 64
    p = np.arange(128)
    d = p % 64
    r = d % 32
    fi = (r % 16).astype(np.float64)
    freq = 10000.0 ** (-fi / 16.0)
    psel = np.where((d < 32)[:, None], row[None, :], colp[None, :]).astype(np.float64)
    ang = psel * freq[:, None]
    sgn = np.where(r < 16, -1.0, 1.0)[:, None]
    rp = np.zeros((128, 2, WIN), np.float32)
    rp[:, 0] = np.cos(ang)
    rp[:, 1] = np.sin(ang) * sgn
    m["rope"] = rp
    m["kcT"] = np.ascontiguousarray(inp["cache_k"][b].reshape(L, 256, 128).transpose(2, 0, 1))
    m["vc"] = np.ascontiguousarray(inp["cache_v"][b].reshape(L, 2, 128, 128).transpose(2, 0, 1, 3))
    return m


def kernel(**inputs):
    inp = {k: np.asarray(v) for k, v in inputs.items()}
    if "nc" not in _NC_CACHE:
        _NC_CACHE["nc"] = build_program(True)
    nc = _NC_CACHE["nc"]
    packs = [host_pack_layer(inp, l) for l in range(L)]
    in_maps = []
    for core in range(8):
        m = host_inputs(inp, core)
        for l in range(L):
            m[f"wpk{l}"] = packs[l]
        in_maps.append(m)
    res = run_bass_kernel_spmd(nc, in_maps, core_ids=list(range(8)))
    rs = res.results
    yp = np.concatenate([rs[c]["yp"].reshape(4, 256, D) for c in range(8)], axis=0)
    ys = np.stack([np.concatenate([rs[2 * b]["ys"], rs[2 * b + 1]["ys"]], axis=0) for b in range(4)], axis=0)
    nk = np.concatenate([rs[c]["nk"].reshape(L, 4, 256, 2, 64).transpose(1, 0, 2, 3, 4) for c in range(8)], axis=0)
    nv = np.concatenate([rs[c]["nv"].reshape(L, 4, 256, 2, 64).transpose(1, 0, 2, 3, 4) for c in range(8)], axis=0)
    return (yp.astype(np.float32), ys.astype(np.float32), nk.astype(np.float32), nv.astype(np.float32))
```

```python
from contextlib import ExitStack
import numpy as np
import concourse.bass as bass
import concourse.mybir as mybir
from concourse.bass_utils import run_bass_kernel_spmd

F32 = mybir.dt.float32
BF16 = mybir.dt.bfloat16
AF = mybir.ActivationFunctionType
ALU = mybir.AluOpType

ENGS = ("pe", "act", "dve", "pool", "sp")
DEBUG_STAGE = 0
D = 1024
L = 2
NB_S = 20
WIN = NB_S * 128
PADC = 16
EPS = 1e-6
SLAB = 4096


class Sem:
    __slots__ = ("h", "count", "name")

    def __init__(self, h, name):
        self.h = h
        self.count = 0
        self.name = name


class Res:
    __slots__ = ("name", "w", "r", "excl")

    def __init__(self, name=""):
        self.name = name
        self.w = None
        self.r = {}
        self.excl = False


class Prog:
    def __init__(self, nc, stack):
        self.nc = nc
        self.stack = stack
        self.streams = {e: [] for e in ENGS}
        self.esem = {e: self.new_sem("eng_" + e) for e in ENGS}
        self.waited = {e: {} for e in ENGS}
        self.n_inst = 0
        self.n_wait = 0
        self._res = {}

    def new_sem(self, name):
        return Sem(self.stack.enter_context(self.nc.semaphore(name)), name)

    def R(self, *key):
        r = self._res.get(key)
        if r is None:
            r = Res(str(key))
            self._res[key] = r
        return r

    def _collect(self, eng, reads, writes):
        deps = {}

        def add(t, kind):
            if t is None:
                return
            s, v, e = t
            if e == eng and eng == "pe":
                return
            if deps.get(s, 0) < v:
                deps[s] = v

        for r in reads:
            add(r.w, "raw")
            if r.excl:
                for s, (v, e) in r.r.items():
                    if e != eng:
                        add((s, v, e), "rar")
        for w in writes:
            add(w.w, "waw")
            for s, (v, e) in w.r.items():
                add((s, v, e), "war")
        waits = []
        wd = self.waited[eng]
        for s, v in deps.items():
            if wd.get(s, 0) < v:
                wd[s] = v
                waits.append((s.h, v))
        return waits

    def _mark(self, ticket, reads, writes):
        s, v, e = ticket
        for r in reads:
            cur = r.r.get(s)
            if cur is None or cur[0] < v:
                r.r[s] = (v, e)
        for w in writes:
            w.w = ticket
            w.r = {}

    def op(self, eng, fn, reads=(), writes=(), signal=True):
        waits = self._collect(eng, reads, writes)
        sem = self.esem[eng]
        ticket = (sem, sem.count + 1, eng)
        if signal:
            sem.count += 1
        self.streams[eng].append((waits, fn, (sem.h, 1) if signal else None))
        self._mark(ticket, reads, writes)
        self.n_inst += 1
        self.n_wait += len(waits)
        return ticket

    def dma(self, eng, dsem, out_ap, in_ap, reads=(), writes=(), **kw):
        waits = self._collect(eng, reads, writes)
        dsem.count += 16
        ticket = (dsem, dsem.count, "dma")

        def fn(e, out_ap=out_ap, in_ap=in_ap):
            return e.dma_start(out=out_ap, in_=in_ap, **kw)

        self.streams[eng].append((waits, fn, (dsem.h, 16)))
        self._mark(ticket, reads, writes)
        self.n_inst += 1
        self.n_wait += len(waits)
        return ticket

    def wait_all(self, eng, sems):
        waits = [(s.h, s.count) for s in sems if s.count > 0]
        self.streams[eng].append((waits, None, None))

    def emit(self):
        streams = self.streams

        def run(e_obj, lst):
            for waits, fn, inc in lst:
                for (h, v) in waits:
                    e_obj.wait_ge(h, v)
                if fn is None:
                    continue
                ins = fn(e_obj)
                if inc is not None:
                    ins.then_inc(inc[0], inc[1])

        with self.nc.Block() as block:
            @block.tensor
            def _(t):
                run(t, streams["pe"])

            @block.scalar
            def _(s):
                run(s, streams["act"])

            @block.vector
            def _(v):
                run(v, streams["dve"])

            @block.gpsimd
            def _(g):
                run(g, streams["pool"])

            @block.sync
            def _(sp):
                run(sp, streams["sp"])


def pack_spec():
    spec = []
    for i in range(12):
        spec.append((f"ada{i}", 8, 512))
    spec += [("A0", 8, 512), ("A1", 8, 512), ("A2", 8, 256), ("AT", 8, 512)]
    spec += [("B0", 8, 512), ("B1", 8, 512), ("B2", 8, 256)]
    spec.append(("wpool", 1, 256))
    for c in range(8):
        spec.append((f"g{c}", 8, 512))
        spec.append((f"br{c}", 1, 2048))
    spec += [("wo0", 8, 512), ("wo1", 8, 512)]
    for i in range(8):
        spec.append((f"f1_{i}", 8, 512))
    for c in range(8):
        spec.append((f"f2_{c}", 32, 128))
    offs = {}
    o = 0
    for name, kc, n in spec:
        offs[name] = (o, kc, n)
        o += kc * n
    return spec, offs, o


PACK_SPEC, PACK_OFFS, PACK_EL = pack_spec()


def rope_swap_idx():
    idx = np.zeros(64, np.int64)
    for d in range(64):
        blk, r = d // 32, d % 32
        partner = r + 16 if r < 16 else r - 16
        idx[d] = blk * 32 + partner
    return idx


def host_pack_layer(inp, l):
    w_in = inp["w_in"][l]
    out = np.zeros((128, PACK_EL), np.float32)

    def put(name, arr):
        o, kc, n = PACK_OFFS[name]
        assert arr.shape == (128, kc, n), (name, arr.shape)
        out[:, o:o + kc * n] = arr.reshape(128, kc * n)

    def kmaj(w):
        K, n = w.shape
        return np.ascontiguousarray(w.reshape(K // 128, 128, n).transpose(1, 0, 2))

    wada = inp["w_ada"][l]
    for i in range(12):
        put(f"ada{i}", kmaj(wada[:, i * 512:(i + 1) * 512]))
    sw = rope_swap_idx()
    kcols = np.arange(2048, 2176)
    kscols = np.concatenate([2048 + g * 64 + sw for g in range(2)])
    colsA = np.concatenate([np.arange(256, 512), np.arange(0, 256), np.arange(1024, 1280),
                            np.arange(1280, 1536), kcols, kscols])
    WA = w_in[:, colsA]
    put("A0", kmaj(WA[:, 0:512])); put("A1", kmaj(WA[:, 512:1024])); put("A2", kmaj(WA[:, 1024:1280]))
    colsT = np.concatenate([np.arange(512, 768), np.arange(2176, 2304), np.arange(2048, 2176)])
    put("AT", kmaj(w_in[:, colsT]))
    qcols = np.concatenate([np.concatenate([1536 + i * 64 + np.arange(64), 1536 + (4 + i) * 64 + np.arange(64)])
                            for i in range(4)])
    qscols = np.concatenate([np.concatenate([1536 + i * 64 + sw, 1536 + (4 + i) * 64 + sw]) for i in range(4)])
    colsB = np.concatenate([np.arange(768, 1024), qcols, qscols])
    WB = w_in[:, colsB]
    put("B0", kmaj(WB[:, 0:512])); put("B1", kmaj(WB[:, 512:1024])); put("B2", kmaj(WB[:, 1024:1280]))
    wp = np.zeros((128, 1, 256), np.float32)
    for g in range(4):
        wp[0:64, 0, g * 64:(g + 1) * 64] = inp["w_pool"][l, g]
    put("wpool", wp)
    for c in range(8):
        gcols = np.concatenate([2304 + b * 1024 + c * 128 + np.arange(128) for b in range(4)])
        put(f"g{c}", kmaj(w_in[:, gcols]))
        br = np.zeros((128, 1, 2048), np.float32)
        cs = slice(c * 128, (c + 1) * 128)
        for k in range(2):
            br[:, 0, k * 128:(k + 1) * 128] = inp["w_a_out"][l, k * 128:(k + 1) * 128, cs]
            br[:, 0, 768 + k * 128:768 + (k + 1) * 128] = inp["w_c_out"][l, k * 128:(k + 1) * 128, cs]
        for g in range(4):
            br[0:64, 0, 256 + g * 128:256 + (g + 1) * 128] = inp["w_b_out"][l, g * 64:(g + 1) * 64, cs]
        for h in range(8):
            br[0:64, 0, 1024 + h * 128:1024 + (h + 1) * 128] = inp["w_d_out"][l, h * 64:(h + 1) * 64, cs]
        put(f"br{c}", br)
    wo = inp["w_o"][l]
    put("wo0", kmaj(wo[:, 0:512])); put("wo1", kmaj(wo[:, 512:1024]))
    w1 = inp["w_ff1"][l]
    for i in range(8):
        put(f"f1_{i}", kmaj(w1[:, i * 512:(i + 1) * 512]))
    w2 = inp["w_ff2"][l]
    for c in range(8):
        put(f"f2_{c}", kmaj(w2[:, c * 128:(c + 1) * 128]))
    return out


def pool_mats():
    sizes = (2, 4, 8, 16)
    M = np.zeros((5, 4, 128, 128), np.float64)
    for g, w in enumerate(sizes):
        hw = w // 2
        for t in range(128):
            for kind in range(5):
                for tp_rel in range(t - hw, t + hw):
                    if kind in (0, 1, 2):
                        cnt = w
                    elif kind == 3:
                        cnt = (t + hw) - max(t - hw, 0)
                    else:
                        cnt = min(t + hw, 128) - (t - hw)
                    if kind == 0 and tp_rel < 0:
                        M[kind, g, tp_rel + 128, t] += 1.0 / cnt
                    elif kind == 1 and tp_rel >= 128:
                        M[kind, g, tp_rel - 128, t] += 1.0 / cnt
                    elif kind >= 2 and 0 <= tp_rel < 128:
                        M[kind, g, tp_rel, t] += 1.0 / cnt
                if kind >= 2:
                    M[kind, g, t, t] -= 1.0
    return M.astype(np.float32)


def build_program(do_sample=True, limit=None):
    nc = bass.Bass("TRN2", target_bir_lowering=False)
    dt_in = lambda name, shape: nc.dram_tensor(name, list(shape), F32, kind="ExternalInput").ap()
    xp_d = dt_in("xp", [1024, D])
    xs_d = dt_in("xs", [WIN, D])
    wpk_d = [dt_in(f"wpk{l}", [128, PACK_EL]) for l in range(L)]
    cnst_d = dt_in("cnst", [128, 3 * 128 + 3 * 64 + 128 * 2])
    pmat_d = dt_in("pmat", [128, 7 * 4 * 128])
    small_d = dt_in("small", [128, 16 + L * 128])
    rope_d = dt_in("rope", [128, 2, WIN])
    kc_d = dt_in("kcT", [128, L, 256])
    vc_d = dt_in("vc", [128, L, 2, 128])
    yp_d = nc.dram_tensor("yp", [1024, D], F32, kind="ExternalOutput").ap()
    ys_d = nc.dram_tensor("ys", [2048, D], F32, kind="ExternalOutput").ap()
    nk_d = nc.dram_tensor("nk", [L, 1024, 128], F32, kind="ExternalOutput").ap()
    nv_d = nc.dram_tensor("nv", [L, 1024, 128], F32, kind="ExternalOutput").ap()
    wbf_d = [nc.dram_tensor(f"wbf{l}", [128, PACK_EL], BF16).ap() for l in range(L)]
    xsc_d = {"p": nc.dram_tensor("xscp", [128, 8, 1024], F32).ap(),
             "s": nc.dram_tensor("xscs", [128, 8, WIN], F32).ap()}

    with ExitStack() as st:
        P = Prog(nc, st)
        R = P.R

        def sb(name, shape, dt):
            return st.enter_context(nc.sbuf_tensor(name, list(shape), dt))

        ring_n = 5
        ring = sb("ring", [128, ring_n, SLAB], BF16)
        xTs = [sb("xTa", [128, 8, 512], F32), sb("xTb", [128, 8, 512], F32)]
        XS = {"i": 0, "cur": 0, "pref": None, "early": None, "pstore": None, "pprefetch": None}
        hT = sb("hT", [128, 8, 512], BF16)
        S16 = sb("S16", [128, 32, 512], BF16)
        S32 = sb("S32", [128, 14, 512], F32)

        def xtok(bi, lo=0, hi=D):
            return S32[:, 2 * bi:2 * bi + 2, :].rearrange("p a b -> p (a b)")[:, lo:hi]

        def xtok_res(bi):
            return [R("S32", 2 * bi), R("S32", 2 * bi + 1)]
        ua_c = sb("ua_c", [128, 2, WIN + 2 * PADC + 64], BF16)
        sc_c = sb("sc_c", [128, 2, WIN + 2 * PADC + 64], BF16)
        kT = sb("kT", [128, WIN], BF16)
        vtok = sb("vtok", [128, NB_S, 128], BF16)
        uptok = sb("uptok", [128, NB_S, 256], BF16)
        qT = sb("qT", [128, 4, 512], BF16)
        kvo = sb("kvo", [128, 1, 256], F32)
        ropeT = sb("ropeT", [128, 2, 512], F32)
        kcT = sb("kcTs", [128, L, 256], BF16)
        vcs = sb("vcs", [128, L, 2, 128], BF16)
        cn32 = sb("cn32", [128, 128], F32)
        cbf = sb("cbf", [128, 3 * 128 + 3 * 64 + 256], BF16)
        pmat = sb("pmat_sb", [128, 28, 128], BF16)
        small = sb("small_sb", [128, 16 + L * 128], F32)
        condb = sb("condb", [128, 8, 2], BF16)
        modv = sb("modv", [128, L, 48, 2], F32)
        dvec = sb("dvec", [128, L, 6, 8, 2], F32)
        diagA = sb("diagA", [128, 8, 128], BF16)
        diagC = sb("diagC", [128, 2, 3, 128], BF16)
        sinkf = sb("sinkf", [64, 8], F32)
        psb = [st.enter_context(nc.psum_tensor(f"ps{i}", [128, 512], F32)) for i in range(8)]

        ident32 = cn32
        identb = cbf[:, 0:128]
        maskP = cbf[:, 128:256]
        maskN = cbf[:, 256:384]
        ones64 = [cbf[:, 384 + i * 64:384 + (i + 1) * 64] for i in range(3)]
        r1024 = cbf[:, 576:704]
        r256 = cbf[:, 704:832]

        NCH = 8
        d_cast = [P.new_sem(f"d_cast{l}_{i}") for l in range(L) for i in range(NCH)]
        d_ring = [P.new_sem(f"d_ring{i}") for i in range(ring_n)]
        d_x = P.new_sem("d_x")
        d_xf = [P.new_sem("d_xf0"), P.new_sem("d_xf1")]
        d_scs = [P.new_sem("d_sc0"), P.new_sem("d_sc1")]
        d_xb = [P.new_sem(f"d_xb{i}") for i in range(4)]
        d_cs = [P.new_sem(f"d_c{i}") for i in range(8)]
        d_rope = P.new_sem("d_rope")
        d_sc = P.new_sem("d_sc")
        d_outb = [P.new_sem(f"d_out{i}") for i in range(4)]
        d_kvs = [[P.new_sem(f"d_kv{i}{j}") for j in range(2)] for i in range(1)]

        ps_ctr = {}
        dg_ctr = [0]

        def psum(lo=0, hi=8):
            k = (lo, hi)
            n = ps_ctr.get(k, 0)
            ps_ctr[k] = n + 1
            i = lo + (n % (hi - lo))
            r = R("ps", i)
            r.excl = True
            return psb[i], r

        ncn = 3 * 128 + 3 * 64 + 256
        P.dma("pool", d_cs[0], cbf[:], cnst_d[:, 0:ncn], writes=[R("cbf")])
        P.dma("act", d_cs[1], cn32[:], cnst_d[:, 0:128], writes=[R("cn32")])
        P.dma("pool", d_cs[2], pmat[:], pmat_d.rearrange("p (a b) -> p a b", b=128), writes=[R("pmat")])
        P.dma("act", d_cs[3], small[:], small_d, writes=[R("small")])
        P.dma("pool", d_cs[4], kcT[:], kc_d, writes=[R("kc")])
        P.dma("pool", d_cs[5], vcs[:], vc_d, writes=[R("vc")])
        P.op("pool", lambda e: e.memset(ua_c[:], 0.0), writes=[R("ua_c")])
        P.op("pool", lambda e: e.memset(sc_c[:], 0.0), writes=[R("sc_c")])
        ch = PACK_EL // NCH

        def issue_cast(l, i):
            hi_ = PACK_EL if i == NCH - 1 else (i + 1) * ch
            P.dma("pool", d_cast[l * NCH + i], wbf_d[l][:, i * ch:hi_], wpk_d[l][:, i * ch:hi_],
                  writes=[R("wbf", l, i)], max_dma_last_dim=16384)

        for i in range(NCH):
            issue_cast(0, i)
        l1_cast = list(range(NCH))

        def wbf_res(l, o, n):
            lo_i = min(NCH - 1, o // ch)
            hi_i = min(NCH - 1, (o + n - 1) // ch)
            return [R("wbf", l, i) for i in range(lo_i, hi_i + 1)]

        def sm(l, a, b):
            return small[:, 16 + l * 128 + a:16 + l * 128 + b]

        validL = small[:, 16 + 98:16 + 99]
        validR = small[:, 16 + 99:16 + 100]
        P.op("act", lambda e: e.activation(out=condb[:], in_=small[:, 0:16].rearrange("p (c t) -> p c t", t=2),
                                           func=AF.Silu), reads=[R("small")], writes=[R("condb")])

        seq = slab_sequence(do_sample, limit)
        wstate = {"next_load": 0, "pos": 0}

        def ensure_loaded(upto):
            while wstate["next_load"] <= upto and wstate["next_load"] < len(seq):
                s = wstate["next_load"]
                l, name = seq[s]
                o, kc, n = PACK_OFFS[name]
                slot = s % ring_n
                P.dma("sp", d_ring[slot], ring[:, slot, 0:kc * n], wbf_d[l][:, o:o + kc * n],
                      reads=wbf_res(l, o, kc * n), writes=[R("ring", slot)])
                wstate["next_load"] += 1

        def W(l, name):
            s = wstate["pos"]
            assert seq[s] == (l, name), (s, seq[s], l, name)
            wstate["pos"] += 1
            ensure_loaded(s + ring_n - 3)
            o, kc, n = PACK_OFFS[name]
            slot = s % ring_n
            return ring[:, slot, 0:kc * n].rearrange("p (k n) -> p k n", n=n), R("ring", slot)

        def mm_group(out_ap, out_res, items, extra_reads=(), signal_each=False):
            n = len(items)
            for i, (lhsT, rhs, rs) in enumerate(items):
                P.op("pe", lambda e, lhsT=lhsT, rhs=rhs, i=i: e.matmul(out_ap, lhsT=lhsT, rhs=rhs,
                                                                         start=(i == 0), stop=(i == n - 1)),
                     reads=list(rs) + list(extra_reads), writes=[out_res], signal=(signal_each or i == n - 1))

        def act(out, in_, func, reads, writes, **kw):
            P.op("act", lambda e: e.activation(out=out, in_=in_, func=func, **kw), reads=reads, writes=writes)

        def tt(eng, out, in0, in1, op, reads, writes):
            P.op(eng, lambda e: e.tensor_tensor(out=out, in0=in0, in1=in1, op=op), reads=reads, writes=writes)

        def stt(eng, out, in0, scalar, in1, op0, op1, reads, writes):
            P.op(eng, lambda e: e.scalar_tensor_tensor(out=out, in0=in0, scalar=scalar, in1=in1, op0=op0, op1=op1),
                 reads=reads, writes=writes)

        def layer_setup(l):
            ps, rps = psum()
            for i in range(12):
                w, rw = W(l, f"ada{i}")
                for mi in range(4):
                    m = i * 4 + mi
                    mm_group(ps[:, 2 * m:2 * m + 2], rps,
                             [(w[:, k, mi * 128:(mi + 1) * 128], condb[:, k, :], [rw, R("condb")]) for k in range(8)])
            bada = sm(l, 0, 48)
            P.op("dve", lambda e: e.tensor_tensor(out=modv[:, l], in0=ps[:, 0:96].rearrange("p (m t) -> p m t", t=2),
                                                  in1=bada.unsqueeze(2).to_broadcast([128, 48, 2]), op=ALU.add),
                 reads=[rps, R("small")], writes=[R("modv", l)])
            def gvec(i):
                return sm(l, 48 + 8 * i, 56 + 8 * i).unsqueeze(2).to_broadcast([128, 8, 2])
            mv = modv[:, l]
            dv = dvec[:, l]
            rd, wr = [R("modv", l), R("small")], [R("dvec", l)]
            stt("dve", dv[:, 0], mv[:, 8:16], 1.0, gvec(0), ALU.add, ALU.mult, rd, wr)
            P.op("dve", lambda e: e.tensor_copy(out=dv[:, 1], in_=mv[:, 0:8]), reads=rd, writes=wr)
            tt("dve", dv[:, 2], mv[:, 16:24], gvec(1), ALU.mult, rd, wr)
            stt("dve", dv[:, 3], mv[:, 32:40], 1.0, gvec(2), ALU.add, ALU.mult, rd, wr)
            P.op("dve", lambda e: e.tensor_copy(out=dv[:, 4], in_=mv[:, 24:32]), reads=rd, writes=wr)
            tt("dve", dv[:, 5], mv[:, 40:48], gvec(3), ALU.mult, rd, wr)
            wca = small_conv[l]
            for c in range(2):
                for k in range(3):
                    P.op("pool", lambda e, c=c, k=k: e.tensor_scalar(out=diagC[:, c, k, :], in0=ident32[:],
                                                                     scalar1=wca[:, c, 31 + k:32 + k], scalar2=None,
                                                                     op0=ALU.mult),
                         reads=[R("cn32"), R("convw")], writes=[R("diagC")])
            act(sinkf[:], sm(l, 90, 98)[0:64, :], AF.Exp, [R("small")], [R("sinkf")])

        convw_d = dt_in("convw", [128, L, 2, 34])
        convw = sb("convw_sb", [128, L, 2, 34], F32)
        P.dma("act", d_cs[6], convw[:], convw_d, writes=[R("convw")])
        small_conv = [convw[:, l] for l in range(L)]

        def norm_sq(T, xres, xb, sqalt):
            SQE = ("act", "dve", "pool", "act", "dve", "act", "dve", "pool")
            out = []
            for c in range(8):
                if sqalt:
                    sq, rsq = (qT[:, c, 0:T], R("qT", c)) if c < 4 else (hT[:, c, 0:T], R("hT", c))
                else:
                    sq, rsq = S16[:, 8 + c, 0:T], R("S16", 8 + c)
                if SQE[c] == "act":
                    act(sq, xb[:, c, 0:T], AF.Square, [xres[c]], [rsq])
                else:
                    tt(SQE[c], sq, xb[:, c, 0:T], xb[:, c, 0:T], ALU.mult, [xres[c]], [rsq])
                out.append((sq, rsq))
            return out

        def norm_rest(l, T, a_idx, b_idx, col, xres, xb, sqs):
            ps, rps = psum()
            for c in range(8):
                sq, rsq = sqs[c]
                P.op("pe", lambda e, sq=sq, c=c: e.matmul(ps[:, 0:T], lhsT=r1024, rhs=sq, start=(c == 0), stop=(c == 7)),
                     reads=[rsq, R("cbf")], writes=[rps], signal=(c == 7))
            rsq_, rr0 = S32[:, 12, 0:T], R("S32", 12)
            rstd, rr = S32[:, 13, 0:T], R("S32", 13)
            act(rsq_, ps[:, 0:T], AF.Sqrt, [rps], [rr0], bias=EPS, scale=1.0)
            P.op("dve", lambda e: e.reciprocal(out=rstd, in_=rsq_), reads=[rr0], writes=[rr])
            for c in range(8):
                tmp, rt = S32[:, 10 + (c % 2), 0:T], R("S32", 10 + (c % 2))
                tt("dve", tmp, xb[:, c, 0:T], rstd, ALU.mult, [xres[c], rr], [rt])
                act(hT[:, c, 0:T], tmp, AF.Identity, [rt, R("dvec", l)], [R("hT", c)],
                    scale=dvec[:, l, a_idx, c, col:col + 1], bias=dvec[:, l, b_idx, c, col:col + 1])

        def norm_to_h(l, T, a_idx, b_idx, col, xres, xb=None):
            if xb is None:
                xb = XT()
            sqs = norm_sq(T, xres, xb, False)
            norm_rest(l, T, a_idx, b_idx, col, xres, xb, sqs)

        def post_norm_residual(l, T, g_idx, col, mres, xres):
            ps, rps = psum()
            SQE = ("act", "dve", "pool", "act", "dve", "act", "dve", "pool")
            for c in range(8):
                sq, rsq = S16[:, 8 + c, 0:T], R("S16", 8 + c)
                if SQE[c] == "act":
                    act(sq, S32[:, c, 0:T], AF.Square, [mres[c]], [rsq])
                else:
                    tt(SQE[c], sq, S32[:, c, 0:T], S32[:, c, 0:T], ALU.mult, [mres[c]], [rsq])
                P.op("pe", lambda e, sq=sq, c=c: e.matmul(ps[:, 0:T], lhsT=r1024, rhs=sq, start=(c == 0), stop=(c == 7)),
                     reads=[rsq, R("cbf")], writes=[rps], signal=True)
            rsq_, rr0 = S32[:, 12, 0:T], R("S32", 12)
            rstd, rr = S32[:, 13, 0:T], R("S32", 13)
            act(rsq_, ps[:, 0:T], AF.Sqrt, [rps], [rr0], bias=EPS, scale=1.0)
            P.op("dve", lambda e: e.reciprocal(out=rstd, in_=rsq_), reads=[rr0], writes=[rr])
            for c in range(8):
                tt("dve", S32[:, c, 0:T], S32[:, c, 0:T], rstd, ALU.mult, [mres[c], rr], [mres[c]])
                stt("dve", XT()[:, c, 0:T], S32[:, c, 0:T], dvec[:, l, g_idx, c, col:col + 1],
                    XT()[:, c, 0:T], ALU.mult, ALU.add, [mres[c], R("dvec", l), xres[c]], [xres[c]])

        def XT():
            return xTs[XS["cur"]]

        def issue_tok_dma(pas, blocks):
            src = xp_d if pas == "p" else xs_d
            for bi, b in enumerate(blocks):
                P.dma("act", d_xb[bi], xtok(bi), src[b * 128:(b + 1) * 128, :], writes=xtok_res(bi))

        def issue_feat_dma(pas, blocks, idx):
            T = len(blocks) * 128
            b0 = blocks[0]
            P.dma("act", d_xf[idx], xTs[idx][:, :, 0:T], xsc_d[pas][:, :, b0 * 128:b0 * 128 + T],
                  reads=[R("xsc", pas, 0), R("xsc", pas, 1)], writes=[R("xT", idx, c) for c in range(8)])

        def load_x(kind, pas, l, blocks, T, nxt, do_prefetch=True, set_cur=True):
            XS["i"] ^= 1
            cur = XS["i"]
            if set_cur:
                XS["cur"] = cur
            xres = [R("xT", cur, c) for c in range(8)]
            nb = len(blocks)
            key = (kind, pas, l, tuple(blocks))
            tokmode = (l == 0 and kind == "A")
            if XS["pref"] != key:
                if tokmode:
                    issue_tok_dma(pas, blocks)
                else:
                    issue_feat_dma(pas, blocks, cur)
            XS["pref"] = None
            if tokmode:
                for c in range(8):
                    ps, rps = psum()
                    for bi in range(nb):
                        P.op("pe", lambda e, bi=bi, c=c, ps=ps: e.matmul(ps[:, bi * 128:(bi + 1) * 128],
                                                                          lhsT=xtok(bi, c * 128, (c + 1) * 128), rhs=ident32[:],
                                                                          start=True, stop=True),
                             reads=xtok_res(bi) + [R("cn32")], writes=[rps], signal=(bi == nb - 1))
                    if c % 2 == 0:
                        act(xTs[cur][:, c, 0:T], ps[:, 0:T], AF.Copy, [rps], [xres[c]])
                    else:
                        P.op("dve", lambda e, c=c, ps=ps, xb=xTs[cur]: e.tensor_copy(out=xb[:, c, 0:T], in_=ps[:, 0:T]),
                             reads=[rps], writes=[xres[c]])
                b0 = blocks[0]
                XS["rawstore"] = lambda: P.dma("act", d_scs[cur], xsc_d[pas][:, :, b0 * 128:b0 * 128 + T],
                                               xTs[cur][:, :, 0:T], reads=xres, writes=[R("xsc", pas, cur)])
            if do_prefetch:
                prefetch_next(kind, l, nxt)
            return xres

        def prefetch_next(kind, l, nxt):
            cur = XS["i"]
            tokmode = (l == 0 and kind == "A")
            if nxt is not None:
                nkind, npas, nl, nblocks = nxt
                ntok = (nl == 0 and nkind == "A")
                if ntok and tokmode:
                    issue_tok_dma(npas, nblocks)
                    XS["pref"] = (nkind, npas, nl, tuple(nblocks))
                elif not ntok:
                    issue_feat_dma(npas, nblocks, cur ^ 1)
                    XS["pref"] = (nkind, npas, nl, tuple(nblocks))

        def store_x(pas, l, blocks, T, xres, ci=None):
            b0 = blocks[0]
            if ci is None:
                ci = XS["cur"]
            if l == 0:
                P.dma("act", d_scs[ci], xsc_d[pas][:, :, b0 * 128:b0 * 128 + T], xTs[ci][:, :, 0:T],
                      reads=xres, writes=[R("xsc", pas, ci)])
            else:
                dst = yp_d if pas == "p" else ys_d
                for bi, b in enumerate(blocks):
                    ob = b if pas == "p" else b - 2
                    for half in range(2):
                        ps, rps = psum()
                        for cc in range(4):
                            c = half * 4 + cc
                            P.op("pe", lambda e, bi=bi, c=c, cc=cc, ps=ps, xb=xTs[ci]: e.matmul(
                                ps[:, cc * 128:(cc + 1) * 128], lhsT=xb[:, c, bi * 128:(bi + 1) * 128], rhs=ident32[:],
                                start=True, stop=True),
                                 reads=[xres[c], R("cn32")], writes=[rps], signal=(cc == 3))
                        if half == 0:
                            act(xtok(bi, 0, 512), ps[:], AF.Copy, [rps], [R("S32", 2 * bi)])
                        else:
                            P.op("dve", lambda e, bi=bi, ps=ps: e.tensor_copy(out=xtok(bi, 512, 1024), in_=ps[:]),
                                 reads=[rps], writes=[R("S32", 2 * bi + 1)])
                    P.dma("act", d_outb[bi], dst[ob * 128:(ob + 1) * 128, :], xtok(bi), reads=xtok_res(bi))

        def ccol(pas, t):
            if pas == "p":
                s, r = t // 256, t % 256
                return s * (256 + 2 * PADC) + PADC + r
            return PADC + t

        def conv_rhs(buf, c, pas, blocks, T, shift):
            t0 = blocks[0] * 128
            if pas == "p" and T == 512:
                base = ccol(pas, t0) + shift
                return buf[:, c, base:base + 2 * (256 + 2 * PADC)].rearrange("p (s w) -> p s w", s=2)[:, :, 0:256]
            base = ccol(pas, t0) + shift
            return buf[:, c, base:base + T]

        def conv_dst(buf, c, pas, blocks, T):
            return conv_rhs(buf, c, pas, blocks, T, 0)

        def as_tile(ap, pas, T):
            if pas == "p" and T == 512:
                return ap.rearrange("p (s w) -> p s w", s=2)
            return ap

        def phase_front(kind, pas, l, blocks, T, nxt, col):
            key = (kind, pas, l, tuple(blocks))
            if XS["early"] is not None and XS["early"][0] == key:
                xres = XS["early"][1]
                XS["early"] = None
                XS["cur"] = XS["i"]
                if XS["pstore"] is not None:
                    XS["pprefetch"] = lambda: prefetch_next(kind, l, nxt)
                else:
                    prefetch_next(kind, l, nxt)
                return xres
            flush_store()
            XS["rawstore"] = None
            xres = load_x(kind, pas, l, blocks, T, nxt, do_prefetch=False)
            norm_to_h(l, T, 0, 1, col, xres)
            if XS["rawstore"] is not None:
                XS["rawstore"]()
                XS["rawstore"] = None
            prefetch_next(kind, l, nxt)
            return xres

        def flush_store():
            if XS["pstore"] is not None:
                XS["pstore"]()
                XS["pstore"] = None
            if XS["pprefetch"] is not None:
                XS["pprefetch"]()
                XS["pprefetch"] = None

        def early_front(nxt, l_cur):
            if nxt is None:
                return
            nkind, npas, nl, nblocks = nxt
            if nl != l_cur or (nl == 0 and nkind == "A"):
                return
            key = (nkind, npas, nl, tuple(nblocks))
            if XS["pref"] != key:
                return
            nT = len(nblocks) * 128
            ncol = 0 if npas == "p" else 1
            xres_n = load_x(nkind, npas, nl, nblocks, nT, None, do_prefetch=False, set_cur=False)
            xbn = xTs[XS["i"]]
            sqs = norm_sq(nT, xres_n, xbn, True)
            XS["early"] = (key, xres_n)
            XS["early2"] = lambda: norm_rest(nl, nT, 0, 1, ncol, xres_n, xbn, sqs)

        def phase_A(pas, l, blocks, nxt=None):
            T = len(blocks) * 128
            col = 0 if pas == "p" else 1
            if DEBUG_STAGE == 14:
                return
            if DEBUG_STAGE == 12:
                P.op("pool", lambda e: e.memset(ua_c[:, :, 0:4 * (256 + 2 * PADC)], 0.0), writes=[R("ua_c")])
                return
            if pas == "p" and blocks[0] == 0 and DEBUG_STAGE != 13:
                P.op("pool", lambda e: e.memset(ua_c[:, :, 0:4 * (256 + 2 * PADC)], 0.0), writes=[R("ua_c")])
                P.op("pool", lambda e: e.memset(sc_c[:, :, 0:4 * (256 + 2 * PADC)], 0.0), writes=[R("sc_c")])
            xres = phase_front("A", pas, l, blocks, T, nxt, col)
            if DEBUG_STAGE in (2, 21, 22, 23, 24, 211, 212):
                return
            hres = [R("hT", c) for c in range(8)]
            t0 = blocks[0] * 128
            if pas == "s":
                P.dma("act", d_rope, ropeT[:, :, 0:T], rope_d[:, :, t0:t0 + T], writes=[R("ropeT")])
            slabs = [W(l, "A0"), W(l, "A1"), W(l, "A2")]

            def proj(ci):
                w, rw = slabs[ci // 4]
                ps, rps = psum()
                mm_group(ps[:, 0:T], rps, [(w[:, k, (ci % 4) * 128:(ci % 4 + 1) * 128], hT[:, k, 0:T], [rw, hres[k]])
                                           for k in range(8)])
                return ps, rps

            for c in range(2):
                psg, rg = proj(c)
                sig, rs_ = S16[:, 10 + c, 0:T], R("S16", 10 + c)
                act(sig, psg[:, 0:T], AF.Sigmoid, [rg], [rs_])
                psv, rv = proj(2 + c)
                tt("dve", conv_dst(ua_c, c, pas, blocks, T), as_tile(psv[:, 0:T], pas, T), as_tile(sig, pas, T),
                   ALU.mult, [rv, rs_], [R("ua_c")])
            for c in range(2):
                psc, rc = proj(4 + c)
                tmp, rt = S32[:, 8 + c, 0:T], R("S32", 8 + c)
                act(tmp, psc[:, 0:T], AF.Copy, [rc], [rt])
                psh, rh = proj(6 + c)
                tt("dve", conv_dst(sc_c, c, pas, blocks, T), as_tile(psh[:, 0:T], pas, T), as_tile(tmp, pas, T),
                   ALU.mult, [rh, rt], [R("sc_c")])
            psk, rk = proj(8)
            if pas == "p":
                act(kT[:, t0:t0 + T], psk[:, 0:T], AF.Copy, [rk], [R("kT")])
            else:
                pss, rss = proj(9)
                t1, r1 = S32[:, 8, 0:T], R("S32", 8)
                t2, r2 = S32[:, 9, 0:T], R("S32", 9)
                tt("dve", t1, psk[:, 0:T], ropeT[:, 0, 0:T], ALU.mult, [rk, R("ropeT")], [r1])
                tt("dve", t2, pss[:, 0:T], ropeT[:, 1, 0:T], ALU.mult, [rss, R("ropeT")], [r2])
                tt("pool", kT[:, t0:t0 + T], t1, t2, ALU.add, [r1, r2], [R("kT")])
            flush_store()
            if DEBUG_STAGE == 3:
                wstate["pos"] += 1
                return
            wt, rwt = W(l, "AT")
            for bi, b in enumerate(blocks):
                ps, rps = psum()
                ncol = 512 if pas == "p" else 384
                mm_group(ps[:, 0:ncol], rps, [(hT[:, k, bi * 128:(bi + 1) * 128], wt[:, k, 0:ncol], [rwt, hres[k]])
                                              for k in range(8)])
                vs = None
                if pas == "s" and b in (0, 1):
                    vs = validL
                if pas == "s" and b in (18, 19):
                    vs = validR
                if vs is None:
                    act(uptok[:, b, :], ps[:, 0:256], AF.Copy, [rps], [R("uptok")])
                    P.op("dve", lambda e, b=b, ps=ps: e.tensor_copy(out=vtok[:, b, :], in_=ps[:, 256:384]),
                         reads=[rps], writes=[R("vtok")])
                else:
                    act(uptok[:, b, :], ps[:, 0:256], AF.Identity, [rps, R("small")], [R("uptok")], scale=vs)
                    act(vtok[:, b, :], ps[:, 256:384], AF.Identity, [rps, R("small")], [R("vtok")], scale=vs)
                    for buf, nm in ((ua_c, "ua_c"), (sc_c, "sc_c")):
                        cs = ccol(pas, b * 128)
                        P.op("pool", lambda e, buf=buf, cs=cs, vs=vs: e.tensor_scalar(
                            out=buf[:, :, cs:cs + 128], in0=buf[:, :, cs:cs + 128], scalar1=vs, scalar2=None,
                            op0=ALU.mult), reads=[R(nm), R("small")], writes=[R(nm)])
                if pas == "p":
                    kb = 0
                    P.op("dve", lambda e, kb=kb, ps=ps: e.tensor_copy(out=kvo[:, kb, :], in_=ps[:, 256:512]),
                         reads=[rps], writes=[R("kvo", kb)])
                    P.dma("act", d_kvs[kb][0], nv_d[l, b * 128:(b + 1) * 128, :], kvo[:, kb, 0:128], reads=[R("kvo", kb)])
                    P.dma("act", d_kvs[kb][1], nk_d[l, b * 128:(b + 1) * 128, :], kvo[:, kb, 128:256], reads=[R("kvo", kb)])

        def phase_B(pas, l, blocks, nxt=None):
            T = len(blocks) * 128
            nb = len(blocks)
            col = 0 if pas == "p" else 1
            t0 = blocks[0] * 128
            xres = phase_front("B", pas, l, blocks, T, nxt, col)
            hres = [R("hT", c) for c in range(8)]
            if pas == "s":
                P.dma("act", d_rope, ropeT[:, :, 0:T], rope_d[:, :, t0:t0 + T], writes=[R("ropeT")])
            slabs = [W(l, "B0"), W(l, "B1"), W(l, "B2")]

            def proj(ci):
                w, rw = slabs[ci // 4]
                ps, rps = psum()
                mm_group(ps[:, 0:T], rps, [(w[:, k, (ci % 4) * 128:(ci % 4 + 1) * 128], hT[:, k, 0:T], [rw, hres[k]])
                                           for k in range(8)])
                return ps, rps

            for c in range(2):
                ps, rps = proj(c)
                act(S32[:, 8 + c, 0:T], ps[:, 0:T], AF.Copy, [rps], [R("S32", 8 + c)])
            for c in range(4):
                psq, rq = proj(2 + c)
                if pas == "p":
                    act(qT[:, c, 0:T], psq[:, 0:T], AF.Copy, [rq], [R("qT", c)])
                else:
                    pss, rss = proj(6 + c)
                    t1, r1 = S32[:, 10, 0:T], R("S32", 10)
                    t2, r2 = S32[:, 11, 0:T], R("S32", 11)
                    tt("dve", t1, psq[:, 0:T], ropeT[:, 0, 0:T], ALU.mult, [rq, R("ropeT")], [r1])
                    tt("dve", t2, pss[:, 0:T], ropeT[:, 1, 0:T], ALU.mult, [rss, R("ropeT")], [r2])
                    tt("pool", qT[:, c, 0:T], t1, t2, ALU.add, [r1, r2], [R("qT", c)])
            flush_store()
            uar = []
            for c in range(2):
                ps, rps = psum()
                for k in range(31):
                    di = dg_ctr[0] % 8
                    dg_ctr[0] += 1
                    P.op("dve", lambda e, c=c, k=k, di=di: e.tensor_scalar(
                        out=diagA[:, di, :], in0=ident32[:], scalar1=convw[:, l, c, k:k + 1], scalar2=None,
                        op0=ALU.mult), reads=[R("cn32"), R("convw")], writes=[R("diagA", di)])
                    P.op("pe", lambda e, c=c, k=k, di=di, ps=ps: e.matmul(
                        as_tile(ps[:, 0:T], pas, T), lhsT=diagA[:, di, :],
                        rhs=conv_rhs(ua_c, c, pas, blocks, T, k - 15), start=(k == 0), stop=(k == 30)),
                         reads=[R("diagA", di), R("ua_c")], writes=[rps], signal=True)
                u, ru = S32[:, c, 0:T], R("S32", c)
                act(u, ps[:, 0:T], AF.Identity, [rps, R("small")], [ru], bias=sm(l, 80 + c, 81 + c), scale=1.0)
                ub_, rub = S16[:, 18 + c, 0:T], R("S16", 18 + c)
                P.op("pool", lambda e, u=u, ub_=ub_: e.tensor_copy(out=ub_, in_=u), reads=[ru], writes=[rub])
                uar.append((u, ru, ub_, rub))
            psm, rpm = psum()
            mm_group(psm[:, 0:T], rpm, [(r256, uar[c][2], [uar[c][3], R("cbf")]) for c in range(2)])
            for c in range(2):
                u, ru, ub_, rub = uar[c]
                tt("dve", u, u, psm[:, 0:T], ALU.subtract, [ru, rpm], [ru])
                act(ub_, u, AF.Square, [ru], [rub])
            psv, rpv = psum()
            mm_group(psv[:, 0:T], rpv, [(r256, uar[c][2], [uar[c][3], R("cbf")]) for c in range(2)])
            rs_a0, rra0 = S32[:, 2, 0:T], R("S32", 2)
            rs_a, rra = S32[:, 13, 0:T], R("S32", 13)
            act(rs_a0, psv[:, 0:T], AF.Sqrt, [rpv], [rra0], bias=EPS, scale=1.0)
            P.op("dve", lambda e: e.reciprocal(out=rs_a, in_=rs_a0), reads=[rra0], writes=[rra])
            for c in range(2):
                u, ru, ub_, rub = uar[c]
                tt("dve" if c else "pool", u, u, rs_a, ALU.mult, [ru, rra], [ru])
                act(S16[:, 16 + c, 0:T], u, AF.Silu, [ru, R("small")], [R("S16", 16 + c)],
                    scale=sm(l, 82 + c, 83 + c), bias=sm(l, 84 + c, 85 + c))
            wp, rwp = W(l, "wpool")
            for g in range(4):
                ps, rps = psum()
                for bi, b in enumerate(blocks):
                    items = []
                    if pas == "p":
                        first = (b % 2 == 0)
                        kinds = [(b, 3 if first else 4)]
                        kinds.append((b + 1, 1) if first else (b - 1, 0))
                    else:
                        cur = 5 if b == 2 else (6 if b == 17 else 2)
                        kinds = [(b - 1, 0), (b, cur), (b + 1, 1)]
                    for (j, kind) in kinds:
                        items.append((uptok[:, j, g * 64:(g + 1) * 64], pmat[:, kind * 4 + g, :], [R("uptok"), R("pmat")]))
                    mm_group(ps[0:64, bi * 128:(bi + 1) * 128], rps, items)
                pl, rpl = S16[0:64, 8 + (g % 2), 0:T], R("S16", 8 + (g % 2))
                if g % 2:
                    P.op("dve", lambda e, pl=pl, ps=ps: e.tensor_copy(out=pl, in_=ps[0:64, 0:T]), reads=[rps], writes=[rpl])
                else:
                    act(pl, ps[0:64, 0:T], AF.Copy, [rps], [rpl])
                ps2, rps2 = psum()
                mm_group(ps2[0:64, 0:T], rps2, [(wp[0:64, 0, g * 64:(g + 1) * 64], pl, [rwp, rpl])])
                act(S16[0:64, 20 + g, 0:T], ps2[0:64, 0:T], AF.Identity, [rps2, R("small")], [R("S16", 20 + g)],
                    scale=sm(l, 86 + g, 87 + g)[0:64, :])
            for c in range(2):
                ps, rps = psum()
                mm_group(as_tile(ps[:, 0:T], pas, T), rps,
                         [(diagC[:, c, k, :], conv_rhs(sc_c, c, pas, blocks, T, k - 1), [R("diagC"), R("sc_c")])
                          for k in range(3)])
                tt("dve", S16[:, 18 + c, 0:T], ps[:, 0:T], S32[:, 8 + c, 0:T], ALU.mult,
                   [rps, R("S32", 8 + c)], [R("S16", 18 + c)])
            jobs = []
            for bi, b in enumerate(blocks):
                if pas == "p":
                    s0 = (b // 2) * 2
                    keys = [("l", s0, None, 0), ("l", s0 + 1, None, 0)]
                else:
                    keys = []
                    for j, mk in ((b - 1, maskP), (b, None), (b + 1, maskN)):
                        oi = 1 if j in (0, 1) else (2 if j in (18, 19) else 0)
                        keys.append(("l", j, mk, oi))
                    keys += [("c", 0, None, 0), ("c", 1, None, 0)]
                for g in range(2):
                    for ki, ks in enumerate(keys):
                        jobs.append((bi, g, ki, len(keys), ks))

            def kv_of(job):
                bi, g, ki, nk, (kind, j, mk, oi) = job
                if kind == "l":
                    return (kT[g * 64:(g + 1) * 64, j * 128:(j + 1) * 128], R("kT"),
                            vtok[:, j, g * 64:(g + 1) * 64], R("vtok"))
                return (kcT[g * 64:(g + 1) * 64, l, j * 128:(j + 1) * 128], R("kc"),
                        vcs[:, l, j, g * 64:(g + 1) * 64], R("vc"))

            def issue_S(job):
                bi, g, ki, nk, ks = job
                klhs, kres, _, _ = kv_of(job)
                pS, rpS = psum(0, 4)
                qr = qT[g * 64:(g + 1) * 64, :, bi * 128:(bi + 1) * 128]
                mm_group(pS[:].rearrange("p (h q) -> p h q", h=4), rpS,
                         [(klhs, qr, [kres] + [R("qT", c) for c in range(4)])])
                return pS, rpS

            LA = 3
            pending = []
            inflight = {}
            for ji in range(min(LA, len(jobs))):
                inflight[ji] = issue_S(jobs[ji])
            acc = None
            for ji, job in enumerate(jobs):
                bi, g, ki, nk, (kind, j, mk, oi) = job
                if ki == 0:
                    acc = (psum(4, 6), psum(6, 8))
                (po, rpo), (pd, rpd) = acc
                pS, rpS = inflight.pop(ji)
                _, _, vlhs, vres = kv_of(job)
                es = 12 + (ji % 4)
                E, rE = S16[:, es, :], R("S16", es)
                act(E, pS[:], AF.Exp, [rpS], [rE], scale=0.125)
                if mk is not None:
                    E3 = E.rearrange("p (h q) -> p h q", h=4)
                    tt("dve", E3, E3, mk.unsqueeze(1).to_broadcast([128, 4, 128]), ALU.mult, [rE, R("cbf")], [rE])
                if ji + LA < len(jobs):
                    inflight[ji + LA] = issue_S(jobs[ji + LA])
                P.op("pe", lambda e, vlhs=vlhs, E=E, ki=ki, po=po, nk=nk: e.matmul(
                    po[0:64, :], lhsT=vlhs, rhs=E, start=(ki == 0), stop=(ki == nk - 1)),
                     reads=[vres, rE], writes=[rpo], signal=False)
                P.op("pe", lambda e, oi=oi, E=E, ki=ki, pd=pd, nk=nk: e.matmul(
                    pd[0:64, :], lhsT=ones64[oi], rhs=E, start=(ki == 0), stop=(ki == nk - 1)),
                     reads=[R("cbf"), rE], writes=[rpd, rpo], signal=True)
                while pending and pending[0][0] <= ji:
                    pending.pop(0)[1]()
                if ki == nk - 1:
                  def normalize(bi=bi, g=g, po=po, rpo=rpo, pd=pd, rpd=rpd):
                    den, rden = S32[0:64, 3 + g, :], R("S32", 3 + g)
                    rec, rrec = S32[0:64, 5 + g, :], R("S32", 5 + g)
                    tt("dve", den.rearrange("p (h q) -> p h q", h=4), pd[0:64, :].rearrange("p (h q) -> p h q", h=4),
                       sinkf[:, g * 4:(g + 1) * 4].unsqueeze(2).to_broadcast([64, 4, 128]), ALU.add,
                       [rpd, R("sinkf")], [rden])
                    act(den, den, AF.Ln, [rden], [rden])
                    act(rec, den, AF.Exp, [rden], [rrec], scale=-1.0)
                    for hh in range(4):
                        slot = 24 + g * 4 + hh
                        tt("dve", S16[0:64, slot, bi * 128:(bi + 1) * 128], po[0:64, hh * 128:(hh + 1) * 128],
                           rec[:, hh * 128:(hh + 1) * 128], ALU.mult, [rpo, rrec], [R("S16", slot)])
                  pending.append((ji + 2, normalize))
            while pending:
                pending.pop(0)[1]()
            for c in range(8):
                wg, rwg = W(l, f"g{c}")
                wb, rwb = W(l, f"br{c}")
                gsl = []
                for b_ in range(4):
                    ps, rps = psum()
                    mm_group(ps[:, 0:T], rps, [(wg[:, k, b_ * 128:(b_ + 1) * 128], hT[:, k, 0:T], [rwg, hres[k]])
                                               for k in range(8)])
                    gs = 8 + (c % 2) * 4 + b_
                    act(S16[:, gs, 0:T], ps[:, 0:T], AF.Sigmoid, [rps], [R("S16", gs)])
                    gsl.append(gs)
                brs = []
                ps, rps = psum()
                mm_group(ps[:, 0:T], rps, [(wb[:, 0, k * 128:(k + 1) * 128], S16[:, 16 + k, 0:T], [rwb, R("S16", 16 + k)])
                                           for k in range(2)])
                brs.append((ps, rps))
                ps, rps = psum()
                mm_group(ps[:, 0:T], rps, [(wb[0:64, 0, 256 + g * 128:256 + (g + 1) * 128], S16[0:64, 20 + g, 0:T],
                                            [rwb, R("S16", 20 + g)]) for g in range(4)])
                brs.append((ps, rps))
                ps, rps = psum()
                mm_group(ps[:, 0:T], rps, [(wb[:, 0, 768 + k * 128:768 + (k + 1) * 128], S16[:, 18 + k, 0:T],
                                            [rwb, R("S16", 18 + k)]) for k in range(2)])
                brs.append((ps, rps))
                ps, rps = psum()
                mm_group(ps[:, 0:T], rps, [(wb[0:64, 0, 1024 + h * 128:1024 + (h + 1) * 128], S16[0:64, 24 + h, 0:T],
                                            [rwb, R("S16", 24 + h)]) for h in range(8)])
                brs.append((ps, rps))
                acc, racc = S32[:, 5 + (c % 2), 0:T], R("S32", 5 + (c % 2))
                tmp, rtmp = S32[:, 7, 0:T], R("S32", 7)
                tt("dve", acc, brs[0][0][:, 0:T], S16[:, gsl[0], 0:T], ALU.mult, [brs[0][1], R("S16", gsl[0])], [racc])
                for b_ in range(1, 4):
                    tt("dve", tmp, brs[b_][0][:, 0:T], S16[:, gsl[b_], 0:T], ALU.mult,
                       [brs[b_][1], R("S16", gsl[b_])], [rtmp])
                    if b_ < 3:
                        tt("pool", acc, acc, tmp, ALU.add, [racc, rtmp], [racc])
                    else:
                        tt("pool", S16[:, c, 0:T], acc, tmp, ALU.add, [racc, rtmp], [R("S16", c)])
            wos = [W(l, "wo0"), W(l, "wo1")]
            mres = [R("S32", c) for c in range(8)]
            for c in range(8):
                w, rw = wos[c // 4]
                ps, rps = psum()
                mm_group(ps[:, 0:T], rps, [(w[:, k, (c % 4) * 128:(c % 4 + 1) * 128], S16[:, k, 0:T], [rw, R("S16", k)])
                                           for k in range(8)])
                act(S32[:, c, 0:T], ps[:, 0:T], AF.Copy, [rps], [mres[c]])
            post_norm_residual(l, T, 2, col, mres, xres)
            norm_to_h(l, T, 3, 4, col, xres)
            for i in range(8):
                w, rw = W(l, f"f1_{i}")
                for ci in range(4):
                    j = i * 4 + ci
                    ps, rps = psum()
                    mm_group(ps[:, 0:T], rps, [(w[:, k, ci * 128:(ci + 1) * 128], hT[:, k, 0:T], [rw, hres[k]])
                                               for k in range(8)])
                    tmp, rtmp = S32[:, 8 + (j % 4), 0:T], R("S32", 8 + (j % 4))
                    act(tmp, ps[:, 0:T], AF.Relu, [rps], [rtmp])
                    tt("dve" if j % 2 else "pool", S16[:, j, 0:T], tmp, tmp, ALU.mult, [rtmp], [R("S16", j)])
            XS["early2"] = None
            early_front(nxt, l)
            for c in range(8):
                if c == 2 and XS["early2"] is not None:
                    XS["early2"]()
                    XS["early2"] = None
                w, rw = W(l, f"f2_{c}")
                ps, rps = psum()
                mm_group(ps[:, 0:T], rps, [(w[:, k, :], S16[:, k, 0:T], [rw, R("S16", k)]) for k in range(32)])
                act(S32[:, c, 0:T], ps[:, 0:T], AF.Copy, [rps], [mres[c]])
            post_norm_residual(l, T, 5, col, mres, xres)
            if l == 1 and nxt is not None:
                ci_ = XS["cur"]
                XS["pstore"] = lambda: store_x(pas, l, blocks, T, xres, ci_)
            else:
                store_x(pas, l, blocks, T, xres)

        sched_all = schedule(do_sample)[:limit]
        for si, (kind, pas, l, blocks) in enumerate(sched_all):
            if kind == "setup" and l == 1:
                while l1_cast:
                    issue_cast(1, l1_cast.pop(0))
            if l == 0 and si >= 4 and l1_cast:
                issue_cast(1, l1_cast.pop(0))
            if kind == "setup":
                layer_setup(l)
            else:
                nxt = None
                for it in sched_all[si + 1:]:
                    if it[0] != "setup":
                        nxt = it
                        break
                (phase_A if kind == "A" else phase_B)(pas, l, blocks, nxt)
        assert DEBUG_STAGE or wstate["pos"] == len(seq), (wstate["pos"], len(seq))
        P.wait_all("act", d_outb + d_kvs[0] + [d_sc, d_x, d_rope] + d_xf + d_scs + d_xb + d_cs)
        P.wait_all("sp", d_ring + d_cast)
        P.wait_all("pool", d_cast + d_cs)
        print("sbuf bytes remaining", nc.sbuf_bytes_remaining)
        print("instructions", P.n_inst, "waits", P.n_wait)
        P.emit()
    return nc


def tiles_of(blocks):
    return [blocks[i:i + 4] for i in range(0, len(blocks), 4)]


def schedule(do_sample):
    out = []
    for l in range(L):
        out.append(("setup", None, l, None))
        for t in tiles_of(list(range(8))):
            out.append(("A", "p", l, t))
        for t in tiles_of(list(range(8))):
            out.append(("B", "p", l, t))
        if do_sample:
            lo, hi = (0, 20) if l == 0 else (1, 19)
            for t in tiles_of(list(range(lo, hi))):
                out.append(("A", "s", l, t))
            lo, hi = (1, 19) if l == 0 else (2, 18)
            for t in tiles_of(list(range(lo, hi))):
                out.append(("B", "s", l, t))
    return out


def slab_sequence(do_sample, limit=None):
    seq = []
    for (kind, pas, l, blocks) in schedule(do_sample)[:limit]:
        if kind == "setup":
            seq += [(l, f"ada{i}") for i in range(12)]
        elif kind == "A":
            seq += [(l, "A0"), (l, "A1"), (l, "A2"), (l, "AT")]
        else:
            seq += [(l, "B0"), (l, "B1"), (l, "B2"), (l, "wpool")]
            for c in range(8):
                seq += [(l, f"g{c}"), (l, f"br{c}")]
            seq += [(l, "wo0"), (l, "wo1")]
            seq += [(l, f"f1_{i}") for i in range(8)]
            seq += [(l, f"f2_{c}") for c in range(8)]
    return seq


_NC_CACHE = {}


def host_inputs(inp, core):
    b, half = core // 2, core % 2
    start = half * 2048
    m = {}
    m["xp"] = np.ascontiguousarray(inp["x_prompt"][4 * core:4 * core + 4].reshape(1024, D))
    xs = np.zeros((WIN, D), np.float32)
    lo, hi = start - 256, start + 2304
    slo, shi = max(lo, 0), min(hi, 4096)
    xs[slo - lo:shi - lo] = inp["x_sample"][b, slo:shi]
    m["xs"] = xs
    validL = 1.0 if lo >= 0 else 0.0
    validR = 1.0 if hi <= 4096 else 0.0
    cn = np.zeros((128, 3 * 128 + 3 * 64 + 256), np.float32)
    cn[:, 0:128] = np.eye(128)
    kk = np.arange(128)[:, None]
    qq = np.arange(128)[None, :]
    cn[:, 128:256] = (kk >= qq)
    cn[:, 256:384] = (kk <= qq)
    cn[:, 384:448] = 1.0
    cn[:, 448:512] = validL
    cn[:, 512:576] = validR
    cn[:, 576:704] = 1.0 / 1024
    cn[:, 704:832] = 1.0 / 256
    m["cnst"] = cn
    PM = pool_mats()
    pm = np.zeros((7, 4, 128, 128), np.float32)
    pm[0:5] = PM
    pm[5] = PM[3] if half == 0 else PM[2]
    pm[6] = PM[4] if half == 1 else PM[2]
    m["pmat"] = np.ascontiguousarray(pm.reshape(28, 128, 128).transpose(1, 0, 2).reshape(128, 28 * 128))
    sm = np.zeros((128, 16 + L * 128), np.float32)
    cond = np.stack([inp["c_ctx"], inp["c"][b]], axis=1)
    sm[:, 0:16] = cond.reshape(8, 128, 2).transpose(1, 0, 2).reshape(128, 16)
    for l in range(L):
        o = 16 + l * 128
        sm[:, o:o + 48] = inp["b_ada"][l].reshape(48, 128).T
        for i, nm in enumerate(("g_pre_mix", "g_post_mix", "g_pre_ffn", "g_post_ffn")):
            sm[:, o + 48 + 8 * i:o + 56 + 8 * i] = inp[nm][l].reshape(8, 128).T
        sm[:, o + 80:o + 82] = inp["b_conv_a"][l].reshape(2, 128).T
        sm[:, o + 82:o + 84] = inp["ln_g_a"][l].reshape(2, 128).T
        sm[:, o + 84:o + 86] = inp["ln_b_a"][l].reshape(2, 128).T
        sm[0:64, o + 86:o + 90] = inp["pool_scale"][l].reshape(4, 64).T
        sm[:, o + 90:o + 98] = inp["sink"][l][None, :]
    sm[:, 16 + 98] = validL
    sm[:, 16 + 99] = validR
    m["small"] = sm
    cw = np.zeros((128, L, 2, 34), np.float32)
    for l in range(L):
        cw[:, l, :, 0:31] = inp["w_conv_a"][l].reshape(31, 2, 128).transpose(2, 1, 0)
        cw[:, l, :, 31:34] = inp["w_sc"][l].reshape(3, 2, 128).transpose(2, 1, 0)
    m["convw"] = cw
    pos = np.arange(lo, hi)
    posc = np.clip(pos, 0, 4095)
    row, colp = posc // 64, posc % 64
    p = np.arange(128)
    d = p % 64
    r = d % 32
    fi = (r % 16).astype(np.float64)
    freq = 10000.0 ** (-fi / 16.0)
    psel = np.where((d < 32)[:, None], row[None, :], colp[None, :]).astype(np.float64)
    ang = psel * freq[:, None]
    sgn = np.where(r < 16, -1.0, 1.0)[:, None]
    rp = np.zeros((128, 2, WIN), np.float32)
    rp[:, 0] = np.cos(ang)
    rp[:, 1] = np.sin(ang) * sgn
    m["rope"] = rp
    m["kcT"] = np.ascontiguousarray(inp["cache_k"][b].reshape(L, 256, 128).transpose(2, 0, 1))
    m["vc"] = np.ascontiguousarray(inp["cache_v"][b].reshape(L, 2, 128, 128).transpose(2, 0, 1, 3))
    return m


def kernel(**inputs):
    inp = {k: np.asarray(v) for k, v in inputs.items()}
    if "nc" not in _NC_CACHE:
        _NC_CACHE["nc"] = build_program(True)
    nc = _NC_CACHE["nc"]
    packs = [host_pack_layer(inp, l) for l in range(L)]
    in_maps = []
    for core in range(8):
        m = host_inputs(inp, core)
        for l in range(L):
            m[f"wpk{l}"] = packs[l]
        in_maps.append(m)
    res = run_bass_kernel_spmd(nc, in_maps, core_ids=list(range(8)))
    rs = res.results
    yp = np.concatenate([rs[c]["yp"].reshape(4, 256, D) for c in range(8)], axis=0)
    ys = np.stack([np.concatenate([rs[2 * b]["ys"], rs[2 * b + 1]["ys"]], axis=0) for b in range(4)], axis=0)
    nk = np.concatenate([rs[c]["nk"].reshape(L, 4, 256, 2, 64).transpose(1, 0, 2, 3, 4) for c in range(8)], axis=0)
    nv = np.concatenate([rs[c]["nv"].reshape(L, 4, 256, 2, 64).transpose(1, 0, 2, 3, 4) for c in range(8)], axis=0)
    return (yp.astype(np.float32), ys.astype(np.float32), nk.astype(np.float32), nv.astype(np.float32))
```
